# Optimizing a Trainium2 kernel written in Bass

```python
import math
import jax, jax.numpy as jnp
from jax import lax
import numpy as np

D_MODEL = 1024
BATCH = 4
SEQ = 4096
DEPTH = 1
DEC_BATCH = 128
DEC_SEQ = 4
PAST_LEN = 2048
PAGE_SIZE = 128

MIX_WIDTH = D_MODEL
ML_HEADS = 4
ML_WIDTH = MIX_WIDTH // 2
ML_HEAD_DIM = ML_WIDTH // ML_HEADS
ML_CHUNK = 64
ML_COLS = 4 * ML_WIDTH + 2 * ML_HEADS
DA_HEADS = 4
DA_WIDTH = MIX_WIDTH - ML_WIDTH
DA_V_DIM = DA_WIDTH // DA_HEADS
DA_QK_DIM = DA_V_DIM // 2
DA_COLS = 3 * DA_WIDTH
IN_COLS = ML_COLS + DA_COLS
ROPE_DIM = DA_QK_DIM // 4
ROPE_THETA = 500000.0
Q_BLOCK = 128
MEM_TOKENS = 256
MEM_HEADS = 4
MEM_HEAD_DIM = D_MODEL // MEM_HEADS
D_FF = (8 * D_MODEL // 3 + 63) // 64 * 64
CONV_W = 3
RMS_EPS = 1e-6

kernel_name = 'hymba_mlstm_diffattn_decoder'


def rms_norm(x, g):
    xf = x.astype(jnp.float32)
    y = xf * lax.rsqrt(jnp.mean(xf * xf, axis=-1, keepdims=True) + RMS_EPS)
    return (y * g.astype(jnp.float32)).astype(x.dtype)


def lambda_init_fn(layer_idx):
    return 0.8 - 0.6 * math.exp(-0.3 * layer_idx)


def rope_partial(x, pos):
    half = ROPE_DIM // 2
    inv = jnp.exp(-math.log(ROPE_THETA) * (2.0 * jnp.arange(half, dtype=jnp.float32) / ROPE_DIM))
    ang = pos.astype(jnp.float32)[:, None] * inv[None, :]
    ang = ang.reshape((1, pos.shape[0]) + (1,) * (x.ndim - 3) + (half,))
    cos, sin = jnp.cos(ang), jnp.sin(ang)
    x1 = x[..., :half].astype(jnp.float32)
    x2 = x[..., half:ROPE_DIM].astype(jnp.float32)
    rot = jnp.concatenate([x1 * cos - x2 * sin, x2 * cos + x1 * sin], axis=-1).astype(x.dtype)
    return jnp.concatenate([rot, x[..., ROPE_DIM:]], axis=-1)


def diff_attend(q, k, v, mask, lam):
    s = jnp.einsum('bqhcd,bkhcd->bhcqk', q.astype(jnp.float32), k.astype(jnp.float32)) * (DA_QK_DIM ** -0.5)
    s = jnp.where(mask[None, None, None], s, -jnp.inf)
    p = jax.nn.softmax(s, axis=-1)
    a = p[:, :, 0] - lam * p[:, :, 1]
    return jnp.einsum('bhqk,bkhd->bqhd', a, v.astype(jnp.float32))


def da_attend_prompt(q, k, v, lam):
    B, T = q.shape[0], q.shape[1]
    nb = T // Q_BLOCK
    qb = jnp.moveaxis(q.reshape(B, nb, Q_BLOCK, DA_HEADS, 2, DA_QK_DIM), 1, 0)
    kpos = jnp.arange(T)

    def blk(args):
        qi, bi = args
        qpos = bi * Q_BLOCK + jnp.arange(Q_BLOCK)
        return diff_attend(qi, k, v, kpos[None, :] <= qpos[:, None], lam)

    out = lax.map(blk, (qb, jnp.arange(nb)))
    return jnp.moveaxis(out, 0, 1).reshape(B, T, DA_HEADS, DA_V_DIM)


def da_attend_sample(q, k, v, lam, past_k, past_v):
    T, P = q.shape[1], past_k.shape[1]
    kk = jnp.concatenate([past_k.astype(k.dtype), k], axis=1)
    vv = jnp.concatenate([past_v.astype(v.dtype), v], axis=1)
    qpos = P + jnp.arange(T)
    kpos = jnp.arange(P + T)
    return diff_attend(q, kk, vv, kpos[None, :] <= qpos[:, None], lam)


def mlstm_chunkwise(q, k, v, i_pre, logf, C0, n0, m0):
    B, T, H, D = q.shape
    L = ML_CHUNK if T % ML_CHUNK == 0 else T
    nc = T // L

    def to_chunks(a):
        a = a.reshape((B, nc, L, H) + a.shape[3:])
        return jnp.moveaxis(jnp.moveaxis(a, 1, 0), 3, 2)

    qs = to_chunks(q.astype(jnp.float32))
    ks = to_chunks(k.astype(jnp.float32) * (D ** -0.5))
    vs = to_chunks(v.astype(jnp.float32))
    is_ = to_chunks(i_pre)
    fs = to_chunks(logf)
    causal = jnp.tril(jnp.ones((L, L), dtype=bool))

    def step(carry, inp):
        C, n, m = carry
        qc, kc, vc, ic, fc = inp
        b = jnp.cumsum(fc, axis=-1)
        logD = jnp.where(causal, b[..., :, None] - b[..., None, :] + ic[..., None, :], -jnp.inf)
        inter = b + m[..., None]
        m_row = jnp.maximum(jnp.max(logD, axis=-1), inter)
        w_inter = jnp.exp(inter - m_row)
        s = jnp.einsum('bhtd,bhsd->bhts', qc, kc) * jnp.exp(logD - m_row[..., None])
        num = w_inter[..., None] * jnp.einsum('bhtd,bhde->bhte', qc, C) + jnp.einsum('bhts,bhse->bhte', s, vc)
        den = w_inter * jnp.einsum('bhtd,bhd->bht', qc, n) + jnp.sum(s, axis=-1)
        h = num / jnp.maximum(jnp.abs(den), jnp.exp(-m_row))[..., None]
        bL = b[..., -1]
        log_w = bL[..., None] - b + ic
        m_new = jnp.maximum(bL + m, jnp.max(log_w, axis=-1))
        decay = jnp.exp(bL + m - m_new)
        wk = jnp.exp(log_w - m_new[..., None])[..., None] * kc
        C_new = decay[..., None, None] * C + jnp.einsum('bhsd,bhse->bhde', wk, vc)
        n_new = decay[..., None] * n + jnp.sum(wk, axis=2)
        return (C_new, n_new, m_new), h

    carry0 = (C0.astype(jnp.float32), n0.astype(jnp.float32), m0.astype(jnp.float32))
    (C, n, m), hs = lax.scan(step, carry0, (qs, ks, vs, is_, fs))
    h = jnp.swapaxes(jnp.moveaxis(hs, 0, 1), 2, 3).reshape(B, T, H, D)
    return h, C, n, m


def mem_kv(mem, g_mem_src, w_mk, w_mv):
    B = mem.shape[0]
    mn = rms_norm(mem, g_mem_src)
    mk = (mn @ w_mk).reshape(B, MEM_TOKENS, MEM_HEADS, MEM_HEAD_DIM)
    mv = (mn @ w_mv).reshape(B, MEM_TOKENS, MEM_HEADS, MEM_HEAD_DIM)
    return mk, mv


def mem_attend(h, mk, mv, w_mq, w_mo):
    B, T, _ = h.shape
    q = (h @ w_mq).reshape(B, T, MEM_HEADS, MEM_HEAD_DIM)
    s = jnp.einsum('bqhd,bkhd->bhqk', q.astype(jnp.float32), mk.astype(jnp.float32)) * (MEM_HEAD_DIM ** -0.5)
    p = jax.nn.softmax(s, axis=-1)
    o = jnp.einsum('bhqk,bkhd->bqhd', p, mv.astype(jnp.float32)).reshape(B, T, D_MODEL)
    return o.astype(h.dtype) @ w_mo


def conv_ffn(h, conv_buf, w_up, w_dw, b_dw, w_down):
    T = h.shape[1]
    u = h @ w_up
    ext = jnp.concatenate([conv_buf.astype(u.dtype), u], axis=1)
    c = b_dw
    for j in range(CONV_W):
        c = c + ext[:, j:j + T] * w_dw[j]
    a, g = c[..., :D_FF], c[..., D_FF:]
    y = (jax.nn.silu(g) * a) @ w_down
    return y, ext[:, ext.shape[1] - (CONV_W - 1):]


def layer(x, pos, attend, ml_C, ml_n, ml_m, mk, mv, conv_buf, lam_init,
          g_mix_pre, g_mix_post, w_in, b_if, g_ml_head, da_lambda, g_da_head, w_out,
          g_mem_pre, g_mem_post, w_mq, w_mo, g_ffn_pre, g_ffn_post, w_up, w_dw, b_dw, w_down):
    B, T, _ = x.shape
    h = rms_norm(x, g_mix_pre)
    z = h @ w_in
    zm, zd = z[..., :ML_COLS], z[..., ML_COLS:]
    q, k, v, o = [zm[..., j * ML_WIDTH:(j + 1) * ML_WIDTH].reshape(B, T, ML_HEADS, ML_HEAD_DIM) for j in range(4)]
    gates = (zm[..., 4 * ML_WIDTH:] + b_if).astype(jnp.float32)
    i_pre, logf = gates[..., :ML_HEADS], jax.nn.log_sigmoid(gates[..., ML_HEADS:])
    h_ml, C_new, n_new, m_new = mlstm_chunkwise(q, k, v, i_pre, logf, ml_C, ml_n, ml_m)
    h_ml = rms_norm(h_ml, g_ml_head) * jax.nn.sigmoid(o.astype(jnp.float32))
    dq = rope_partial(zd[..., :DA_WIDTH].reshape(B, T, DA_HEADS, 2, DA_QK_DIM), pos)
    dk = rope_partial(zd[..., DA_WIDTH:2 * DA_WIDTH].reshape(B, T, DA_HEADS, 2, DA_QK_DIM), pos)
    dv = zd[..., 2 * DA_WIDTH:].reshape(B, T, DA_HEADS, DA_V_DIM)
    lamv = da_lambda.astype(jnp.float32)
    lam = jnp.exp(jnp.sum(lamv[0] * lamv[1])) - jnp.exp(jnp.sum(lamv[2] * lamv[3])) + lam_init
    a = attend(dq, dk, dv, lam)
    h_da = rms_norm(a, g_da_head) * (1.0 - lam_init)
    mix = jnp.concatenate([h_ml.reshape(B, T, ML_WIDTH), h_da.reshape(B, T, DA_WIDTH)], axis=-1).astype(x.dtype)
    x = x + rms_norm(mix @ w_out, g_mix_post)
    x = x + rms_norm(mem_attend(rms_norm(x, g_mem_pre), mk, mv, w_mq, w_mo), g_mem_post)
    f, conv_new = conv_ffn(rms_norm(x, g_ffn_pre), conv_buf, w_up, w_dw, b_dw, w_down)
    x = x + rms_norm(f, g_ffn_post)
    return x, dk.reshape(B, T, DA_HEADS, 2 * DA_QK_DIM), dv, C_new, n_new, m_new, conv_new


def setup_inputs(seed: int = 0) -> dict:
    key = jax.random.key(seed)
    kit = iter(list(jax.random.split(key, 48)))
    f32 = jnp.float32

    def nrm(shape, scale=1.0):
        return scale * jax.random.normal(next(kit), shape, f32)

    def gain(shape):
        return 1.0 + nrm(shape, 0.05)

    L = DEPTH
    n_pages = PAST_LEN // PAGE_SIZE
    n_pool = (DEC_BATCH * n_pages * 5) // 4
    x_prompt = nrm((BATCH, SEQ, D_MODEL))
    x_sample = nrm((DEC_BATCH, DEC_SEQ, D_MODEL))
    cache_dk = nrm((L, n_pool, PAGE_SIZE, DA_HEADS, 2 * DA_QK_DIM))
    cache_dv = nrm((L, n_pool, PAGE_SIZE, DA_HEADS, DA_V_DIM))
    cache_mem_k = nrm((L, DEC_BATCH, MEM_TOKENS, MEM_HEADS, MEM_HEAD_DIM))
    cache_mem_v = nrm((L, DEC_BATCH, MEM_TOKENS, MEM_HEADS, MEM_HEAD_DIM))
    state_ml_C = nrm((L, DEC_BATCH, ML_HEADS, ML_HEAD_DIM, ML_HEAD_DIM), 0.5)
    state_ml_n = nrm((L, DEC_BATCH, ML_HEADS, ML_HEAD_DIM), 0.5)
    state_ml_m = nrm((L, DEC_BATCH, ML_HEADS))
    state_conv = nrm((L, DEC_BATCH, CONV_W - 1, 2 * D_FF))
    page_table = jax.random.permutation(next(kit), n_pool)[:DEC_BATCH * n_pages].reshape(DEC_BATCH, n_pages).astype(jnp.int32)
    mem_prompt = nrm((BATCH, MEM_TOKENS, D_MODEL))
    b_if = jnp.concatenate([nrm((L, ML_HEADS), 0.1), 3.0 + 3.0 * jax.random.uniform(next(kit), (L, ML_HEADS), f32)], axis=-1)
    return {
        'x_prompt': x_prompt, 'x_sample': x_sample,
        'cache_dk': cache_dk, 'cache_dv': cache_dv,
        'cache_mem_k': cache_mem_k, 'cache_mem_v': cache_mem_v,
        'state_ml_C': state_ml_C, 'state_ml_n': state_ml_n, 'state_ml_m': state_ml_m,
        'state_conv': state_conv, 'page_table': page_table, 'mem_prompt': mem_prompt,
        'g_mix_pre': gain((L, D_MODEL)), 'g_mix_post': gain((L, D_MODEL)),
        'w_in': nrm((L, D_MODEL, IN_COLS), D_MODEL ** -0.5),
        'b_if': b_if,
        'g_ml_head': gain((L, ML_HEADS, ML_HEAD_DIM)),
        'da_lambda': nrm((L, 4, DA_QK_DIM), 0.1),
        'g_da_head': gain((L, DA_HEADS, DA_V_DIM)),
        'w_out': nrm((L, MIX_WIDTH, D_MODEL), MIX_WIDTH ** -0.5),
        'g_mem_pre': gain((L, D_MODEL)), 'g_mem_post': gain((L, D_MODEL)), 'g_mem_src': gain((L, D_MODEL)),
        'w_mq': nrm((L, D_MODEL, D_MODEL), D_MODEL ** -0.5),
        'w_mk': nrm((L, D_MODEL, D_MODEL), D_MODEL ** -0.5),
        'w_mv': nrm((L, D_MODEL, D_MODEL), D_MODEL ** -0.5),
        'w_mo': nrm((L, D_MODEL, D_MODEL), D_MODEL ** -0.5),
        'g_ffn_pre': gain((L, D_MODEL)), 'g_ffn_post': gain((L, D_MODEL)),
        'w_up': nrm((L, D_MODEL, 2 * D_FF), D_MODEL ** -0.5),
        'w_dw': nrm((L, CONV_W, 2 * D_FF), CONV_W ** -0.5),
        'b_dw': nrm((L, 2 * D_FF), 0.01),
        'w_down': nrm((L, D_FF, D_MODEL), D_FF ** -0.5),
    }


def reference(x_prompt, x_sample, cache_dk, cache_dv, cache_mem_k, cache_mem_v,
              state_ml_C, state_ml_n, state_ml_m, state_conv, page_table, mem_prompt,
              g_mix_pre, g_mix_post, w_in, b_if, g_ml_head, da_lambda, g_da_head, w_out,
              g_mem_pre, g_mem_post, g_mem_src, w_mq, w_mk, w_mv, w_mo,
              g_ffn_pre, g_ffn_post, w_up, w_dw, b_dw, w_down):
    Bp, Tp = x_prompt.shape[0], x_prompt.shape[1]
    Bs, Ts = x_sample.shape[0], x_sample.shape[1]
    n_pages = page_table.shape[1]
    past_len = n_pages * cache_dk.shape[2]
    pos_p = jnp.arange(Tp)
    pos_s = past_len + jnp.arange(Ts)
    yp, ys = x_prompt, x_sample
    p_dk, p_dv, p_mk, p_mv, p_C, p_n, p_m, p_conv = [], [], [], [], [], [], [], []
    s_dk, s_dv, s_C, s_n, s_m, s_conv = [], [], [], [], [], []
    for l in range(DEPTH):
        lam_init = lambda_init_fn(l)
        lw = (g_mix_pre[l], g_mix_post[l], w_in[l], b_if[l], g_ml_head[l], da_lambda[l], g_da_head[l], w_out[l],
              g_mem_pre[l], g_mem_post[l], w_mq[l], w_mo[l], g_ffn_pre[l], g_ffn_post[l],
              w_up[l], w_dw[l], b_dw[l], w_down[l])
        mk_p, mv_p = mem_kv(mem_prompt, g_mem_src[l], w_mk[l], w_mv[l])
        yp, dk_p, dv_p, C_p, n_p, m_p, cv_p = layer(
            yp, pos_p, da_attend_prompt,
            jnp.zeros((Bp, ML_HEADS, ML_HEAD_DIM, ML_HEAD_DIM), jnp.float32),
            jnp.zeros((Bp, ML_HEADS, ML_HEAD_DIM), jnp.float32),
            jnp.zeros((Bp, ML_HEADS), jnp.float32),
            mk_p, mv_p, jnp.zeros((Bp, CONV_W - 1, 2 * D_FF), x_prompt.dtype), lam_init, *lw)
        past_k = cache_dk[l][page_table].reshape(Bs, past_len, DA_HEADS, 2, DA_QK_DIM)
        past_v = cache_dv[l][page_table].reshape(Bs, past_len, DA_HEADS, DA_V_DIM)

        def attend_s(q, k, v, lam, past_k=past_k, past_v=past_v):
            return da_attend_sample(q, k, v, lam, past_k, past_v)

        ys, dk_s, dv_s, C_s, n_s, m_s, cv_s = layer(
            ys, pos_s, attend_s, state_ml_C[l], state_ml_n[l], state_ml_m[l],
            cache_mem_k[l], cache_mem_v[l], state_conv[l], lam_init, *lw)
        p_dk.append(dk_p); p_dv.append(dv_p); p_mk.append(mk_p); p_mv.append(mv_p)
        p_C.append(C_p); p_n.append(n_p); p_m.append(m_p); p_conv.append(cv_p)
        s_dk.append(dk_s); s_dv.append(dv_s); s_C.append(C_s); s_n.append(n_s); s_m.append(m_s); s_conv.append(cv_s)
    return (yp, ys,
            jnp.stack(p_dk), jnp.stack(p_dv), jnp.stack(p_mk), jnp.stack(p_mv),
            jnp.stack(p_C), jnp.stack(p_n), jnp.stack(p_m), jnp.stack(p_conv),
            jnp.stack(s_dk), jnp.stack(s_dv), jnp.stack(s_C), jnp.stack(s_n), jnp.stack(s_m), jnp.stack(s_conv))
```

```python
import math
import numpy as np
from contextlib import ExitStack
import concourse.bass as bass
import concourse.mybir as mybir
from concourse.bass_utils import run_bass_kernel_spmd

F32 = mybir.dt.float32
BF16 = mybir.dt.bfloat16
I32 = mybir.dt.int32
AF = mybir.ActivationFunctionType
ALU = mybir.AluOpType
AX = mybir.AxisListType

D = 1024
NB = 32
W0 = 15
IN_COLS = 3592
DFF = 2752
NCF = 22
BIG = 30000.0
LNC = math.log(128.0 ** -0.5)
LAM_INIT = 0.2
EPS = 1e-6
SKIP = set()


class Prog:
    CENG = ['pe', 'act', 'dve', 'pool']
    ENG = ['pe', 'act', 'dve', 'pool', 'sp']
    DQ = ['sp', 'act', 'pool']
    NDS = 8

    def __init__(self, nc, es):
        self.nc = nc
        self.ins = {e: [] for e in self.ENG}
        self.NEP = {'pe': 32, 'act': 18, 'dve': 18, 'pool': 8}
        self.EPOCH = 1000
        self.csem = {e: [es.enter_context(nc.semaphore("c_%s_%d" % (e, k))) for k in range(self.NEP[e])]
                     for e in self.CENG}
        self.ccount = {e: 0 for e in self.CENG}
        self.dsems = {q: [es.enter_context(nc.semaphore("d_%s_%d" % (q, i))) for i in range(self.NDS)]
                      for q in self.DQ}
        self.dcount = {q: [0] * self.NDS for q in self.DQ}
        self.dpos = {q: 0 for q in self.DQ}
        self.semobj = {}
        for e in self.CENG:
            for k in range(self.NEP[e]):
                self.semobj[('c', e, k)] = self.csem[e][k]
        for q in self.dsems:
            for i, s in enumerate(self.dsems[q]):
                self.semobj[('d', q, i)] = s
        self.last_w = {}
        self.readers = {}
        self.seen = {e: {} for e in self.ENG}

    def _deps(self, eng, reads, writes, ev, is_pe):
        deps = {}

        def add(e):
            if e is None:
                return
            key, val = e
            if is_pe and key[0] == 'c' and key[1] == 'pe':
                return
            if deps.get(key, 0) < val:
                deps[key] = val
        for r in reads:
            add(self.last_w.get(r))
        for w in writes:
            add(self.last_w.get(w))
            for e in self.readers.get(w, ()):
                add(e)
        for r in reads:
            self.readers.setdefault(r, []).append(ev)
        for w in writes:
            self.last_w[w] = ev
            self.readers[w] = []
        waits = []
        seen = self.seen[eng]
        for key, val in deps.items():
            if seen.get(key, 0) >= val:
                continue
            seen[key] = val
            waits.append((key, val))
        return waits

    def op(self, eng, fn, reads=(), writes=()):
        ep = self.ccount[eng] // self.EPOCH
        assert ep < self.NEP[eng], eng
        ev = (('c', eng, ep), self.ccount[eng] % self.EPOCH + 1)
        self.ccount[eng] += 1
        waits = self._deps(eng, reads, writes, ev, eng == 'pe')
        self.ins[eng].append((fn, waits, ev, 1))
        return ev

    def dmaf(self, q, fn, reads=(), writes=()):
        i = self.dpos[q]
        self.dpos[q] = (i + 1) % self.NDS
        key = ('d', q, i)
        prev = self.dcount[q][i]
        self.dcount[q][i] += 16
        ev = (key, self.dcount[q][i])
        waits = self._deps(q, reads, writes, ev, False)
        seen = self.seen[q]
        if prev > 0 and seen.get(key, 0) < prev:
            seen[key] = prev
            waits.append((key, prev))
        self.ins[q].append((fn, waits, ev, 16))
        return ev

    def dma(self, q, out, in_, reads=(), writes=(), **kw):
        return self.dmaf(q, (lambda e: e.dma_start(out=out, in_=in_, **kw)), reads, writes)

    def barrier(self):
        cur = {}
        for e in self.CENG:
            n = self.ccount[e]
            if n:
                ep = (n - 1) // self.EPOCH
                cur[('c', e, ep)] = (n - 1) % self.EPOCH + 1
                if ep > 0:
                    cur[('c', e, ep - 1)] = self.EPOCH
        for q in self.DQ:
            for i in range(self.NDS):
                if self.dcount[q][i]:
                    cur[('d', q, i)] = self.dcount[q][i]
        for e in self.ENG:
            waits = []
            for key, val in cur.items():
                if key[0] == 'c' and key[1] == 'pe' and e == 'pe':
                    continue
                if self.seen[e].get(key, 0) < val:
                    self.seen[e][key] = val
                    waits.append((key, val))
            if waits:
                self.ins[e].append((None, waits, None, 0))
        self.last_w = {}
        self.readers = {}

    def emit(self):
        nc = self.nc
        self.barrier()

        def run(engname, engobj):
            for fn, waits, ev, inc in self.ins[engname]:
                for key, val in waits:
                    engobj.wait_ge(self.semobj[key], val)
                if fn is None:
                    continue
                inst = fn(engobj)
                inst.then_inc(self.semobj[ev[0]], inc)

        with nc.Block() as block:
            @block.tensor
            def _(e):
                run('pe', e)

            @block.scalar
            def _(e):
                run('act', e)

            @block.vector
            def _(e):
                run('dve', e)

            @block.gpsimd
            def _(e):
                run('pool', e)

            @block.sync
            def _(e):
                run('sp', e)


def build(n_pool, stage=99):
    nc = bass.Bass("TRN2", target_bir_lowering=False)

    def din(name, shape, dt=F32):
        return nc.dram_tensor(name, list(shape), dt, kind="ExternalInput").ap()

    def dout(name, shape, dt=F32):
        return nc.dram_tensor(name, list(shape), dt, kind="ExternalOutput").ap()

    xall = din("xall", [4096, D])
    xsm = din("xsm", [64, D])
    memp = din("memp", [256, D])
    w_in = din("w_in", [D, IN_COLS])
    w_out = din("w_out", [D, D])
    w_mq = din("w_mq", [D, D])
    w_mk = din("w_mk", [D, D])
    w_mv = din("w_mv", [D, D])
    w_mo = din("w_mo", [D, D])
    w_up = din("w_up", [D, 2 * DFF])
    w_down = din("w_down", [DFF, D])
    gpre = din("gpre", [128, 4, 8])
    gpost = din("gpost", [3, D])
    ghead = din("ghead", [2, 512])
    bif = din("bif", [4, 2])
    dlam = din("dlam", [1, 256])
    gffn_d = din("gffn", [1, D])
    cwa = din("cwa", [128, NCF, 4])
    cwg = din("cwg", [128, NCF, 4])
    ckp = din("ckp", [n_pool * 128, 512])
    cvp = din("cvp", [n_pool * 128, 512])
    ptab = din("ptab", [1, 256], I32)
    cmk = din("cmk", [16, 256, D])
    cmv = din("cmv", [16, 256, D])
    stC = din("stC", [16, 4, 128, 128])
    stn = din("stn", [16, 4, 128])
    stm = din("stm", [16, 4])
    stcv = din("stcv", [16, 2, 2 * DFF])
    ident_d = din("ident", [128, 128])
    tril_d = din("tril", [128, 128])
    ropeC_d = din("ropeC", [4096, 64])
    ropeS_d = din("ropeS", [4096, 64])
    ropeCs_d = din("ropeCs", [64, 64])
    ropeSs_d = din("ropeSs", [64, 64])
    kbias_d = din("kbias", [128, NB])
    gvalid_d = din("gvalid", [128, 128])
    lt_d = din("lt", [128, 128])
    hflag_d = din("hflag", [128, 1])
    iota_d = din("iota", [128, 1])
    bm_d = din("bm", [64, 64])
    bmask_d = din("bmask", [64, 16])
    sel_d = din("sel", [4, 4, 128])
    bmn_d = din("bmn", [64, 64])
    bmT_d = din("bmT", [16, 64])
    bifr_d = din("bifr", [1, 8])

    y_p = dout("y_p", [2048, D])
    p_dk = dout("p_dk", [2048, 512])
    p_dv = dout("p_dv", [2048, 512])
    p_mk = dout("p_mk", [256, D])
    p_mv = dout("p_mv", [256, D])
    p_C = dout("p_C", [4, 128, 128])
    p_n = dout("p_n", [4, 128])
    p_m = dout("p_m", [4, 1])
    p_cv = dout("p_cv", [2, 2 * DFF])
    y_s = dout("y_s", [64, D])
    s_dk = dout("s_dk", [64, 512])
    s_dv = dout("s_dv", [64, 512])
    s_C = dout("s_C", [16, 4, 128, 128])
    s_n = dout("s_n", [16, 4, 128])
    s_m = dout("s_m", [16, 4])
    s_cv = dout("s_cv", [16, 2, 2 * DFF])

    with ExitStack() as es:
        P = Prog(nc, es)

        def sb(name, shape, dt=F32):
            return es.enter_context(nc.sbuf_tensor("sb_" + name, list(shape), dt))

        PS = [es.enter_context(nc.psum_tensor("ps%d" % i, [128, 512], F32)) for i in range(8)]
        ARENA_W = 39300
        arena = sb("arena", [128, ARENA_W])
        apos = [0]
        amax = [0]

        def ar(name, shape, dt=F32):
            n = 1
            for d_ in shape[1:]:
                n *= d_
            words = n if dt in (F32, I32) else (n + 1) // 2
            o = apos[0]
            apos[0] += words
            amax[0] = max(amax[0], apos[0])
            assert apos[0] <= ARENA_W, (name, apos[0])
            v = arena[0:shape[0], o:o + words]
            if dt != F32:
                v = v.bitcast(dt)[:, 0:n]
            if len(shape) == 3:
                v = v.rearrange("p (a b) -> p a b", a=shape[1])
            elif len(shape) == 4:
                v = v.rearrange("p (a b c) -> p a b c", a=shape[1], b=shape[2])
            return v

        def act(out, in_, func, reads, writes, **kw):
            return P.op('act', lambda e: e.activation(out=out, in_=in_, func=func, **kw), reads, writes)

        def mm(out, lhsT, rhs, start, stop, reads, writes):
            return P.op('pe', lambda e: e.matmul(out, lhsT=lhsT, rhs=rhs, start=start, stop=stop), reads, writes)

        def tr(out, in_, idn, reads, writes):
            return P.op('pe', lambda e: e.transpose(out=out, in_=in_, identity=idn), reads, writes)

        def cp(eng, out, in_, reads, writes):
            if eng == 'act':
                return act(out, in_, AF.Copy, reads, writes)
            return P.op(eng, lambda e: e.tensor_copy(out=out, in_=in_), reads, writes)

        def ts(eng, out, in0, s1, s2, op0, op1, reads, writes):
            if op1 is None:
                return P.op(eng, lambda e: e.tensor_scalar(out=out, in0=in0, scalar1=s1, scalar2=None, op0=op0),
                            reads, writes)
            return P.op(eng, lambda e: e.tensor_scalar(out=out, in0=in0, scalar1=s1, scalar2=s2, op0=op0, op1=op1),
                        reads, writes)

        def tt(eng, out, in0, in1, op, reads, writes):
            return P.op(eng, lambda e: e.tensor_tensor(out=out, in0=in0, in1=in1, op=op), reads, writes)

        def stt(eng, out, in0, scalar, in1, op0, op1, reads, writes):
            return P.op(eng, lambda e: e.scalar_tensor_tensor(out=out, in0=in0, scalar=scalar, in1=in1,
                                                              op0=op0, op1=op1), reads, writes)

        def ms(eng, ap, val, writes):
            return P.op(eng, lambda e: e.memset(ap, val), (), writes)

        identf = sb("identf", [128, 128])
        identb = sb("identb", [128, 128], BF16)
        trilf = sb("trilf", [128, 128])
        trilb = sb("trilb", [128, 128], BF16)
        tril2b = sb("tril2b", [128, 256], BF16)
        epsc = sb("epsc", [128, 1])
        onec = sb("onec", [128, 1])
        lncc = sb("lncc", [128, 1])
        kbias = sb("kbias", [128, NB])
        hflag = sb("hflag", [128, 1])
        gpre_t = sb("gpre_t", [128, 4, 8])
        ghead_t = sb("ghead_t", [128, 2, 512])
        bif_t = sb("bif_t", [4, 2])
        nbf = sb("nbf", [4, 1])
        lam_t = sb("lam_t", [128, 1])
        dl_t = sb("dl_t", [128, 256])
        dl_p = sb("dl_p", [128, 128])
        dl_s = sb("dl_s", [128, 2])
        sel_t = sb("sel_t", [4, 4, 128])
        P.dma('sp', identf[:], ident_d, writes=['identf'])
        P.dma('sp', trilf[:], tril_d, writes=['trilf'])
        P.dma('sp', kbias[:], kbias_d, writes=['kbias'])
        P.dma('sp', hflag[:], hflag_d, writes=['hflag'])
        P.dma('sp', gpre_t[:], gpre, writes=['gpre'])
        P.dma('sp', bif_t[:], bif, writes=['bif'])
        P.dma('sp', sel_t[:], sel_d.rearrange("k h m -> k h m"), writes=['sel'])
        for i in range(2):
            P.dma('sp', ghead_t[:, i, :], ghead[i:i + 1, :].partition_broadcast(128), writes=['ghead'])
        P.dma('sp', dl_t[:], dlam.partition_broadcast(128), writes=['dl_t'])
        cp('dve', identb[:], identf[:], ['identf'], ['identb'])
        cp('dve', trilb[:], trilf[:], ['trilf'], ['trilb'])
        cp('dve', tril2b[:, 0:128], trilf[:], ['trilf'], ['tril2b'])
        cp('dve', tril2b[:, 128:256], trilf[:], ['trilf'], ['tril2b'])
        ms('pool', epsc[:], EPS, ['epsc'])
        ms('pool', onec[:], 1.0, ['onec'])
        ms('pool', lncc[:], LNC, ['lncc'])
        ts('dve', nbf[:], bif_t[:, 1:2], -1.0, None, ALU.mult, None, ['bif'], ['nbf'])
        dl3 = dl_t[:].rearrange("p (a b) -> p a b", a=4)
        tt('dve', dl_p[:, 0:64], dl3[:, 0, :], dl3[:, 1, :], ALU.mult, ['dl_t'], ['dl_p'])
        tt('dve', dl_p[:, 64:128], dl3[:, 2, :], dl3[:, 3, :], ALU.mult, ['dl_t'], ['dl_p'])
        P.op('dve', lambda e: e.tensor_reduce(out=dl_s[:], in_=dl_p[:].rearrange("p (a b) -> p a b", a=2),
                                              axis=AX.X, op=ALU.add), ['dl_p'], ['dl_s'])
        act(dl_s[:], dl_s[:], AF.Exp, ['dl_s'], ['dl_s'])
        tt('dve', lam_t[:], dl_s[:, 0:1], dl_s[:, 1:2], ALU.subtract, ['dl_s'], ['lam'])
        ts('dve', lam_t[:], lam_t[:], LAM_INIT, None, ALU.add, None, ['lam'], ['lam'])
        ts('dve', ghead_t[:, 1, :], ghead_t[:, 1, :], 1.0 - LAM_INIT, None, ALU.mult, None, ['ghead'], ['ghead'])

        mixT = sb("mixT", [128, 8, 17 * 128], BF16)
        mixTs = sb("mixTs", [128, 8, 64], BF16)
        Cst = sb("Cst", [128, 4, 132])
        Cbf = sb("Cbf", [128, 4, 136], BF16)
        colsT = sb("colsT", [128, 3, 4, NB])
        decb = sb("decb", [128, 4, NB])
        ssn = sb("ssn", [128, 2])
        dn = sb("dn", [128, 4])
        hss = sb("hss", [128, 2])
        ms('pool', Cst[:], 0.0, ['Cst'])
        ms('pool', Cbf[:], 0.0, ['Cbf'])
        PST = PS[0]
        C_ML = 0
        C_DA = 2056
        NML = 2056
        NDA = 1536

        def load_w(dst, src_cols0, ncols, gidx, nm):
            hw = (ncols + 1) // 2
            st = [ar("%s_st%d" % (nm, i), [128, hw]) for i in range(2)]
            for kc in range(8):
                for s in range(2):
                    c0 = s * hw
                    c1 = min(ncols, c0 + hw)
                    P.dma('sp', st[s][:, 0:c1 - c0], w_in[kc * 128:(kc + 1) * 128, src_cols0 + c0:src_cols0 + c1],
                          writes=[(nm + 'st', s)])
                    if s == 0:
                        act(dst[:, kc, c0:c1], st[s][:, 0:c1 - c0], AF.Copy, [(nm + 'st', s), 'gpre'], [(nm, kc, s)],
                            scale=gpre_t[:, gidx, kc:kc + 1])
                    else:
                        ts('pool', dst[:, kc, c0:c1], st[s][:, 0:c1 - c0], gpre_t[:, gidx, kc:kc + 1], None,
                           ALU.mult, None, [(nm + 'st', s), 'gpre'], [(nm, kc, s)])
            return [(nm, kc, j) for kc in range(8) for j in range(2)]

        def norm_bufs():
            xt = [ar("xt%d" % i, [128, D]) for i in range(2)]
            xsq = ar("xsq", [128, D], BF16)
            xsb = ar("xsb", [128, D], BF16)
            hT = [ar("hT%d" % i, [128, 8, 128], BF16) for i in range(2)]
            return xt, xsq, xsb, hT
        nctr = [0]

        def norm_T(nb, src, n, dst, dstname):
            xt, xsq, xsb, _ = nb
            s = nctr[0] % 2
            nctr[0] += 1
            P.dma('sp', xt[s][0:n, :], src, writes=[('xt', s)])
            act(xsq[0:n, :], xt[s][0:n, :], AF.Square, [('xt', s)], ['xsq', 'ssn'], accum_out=ssn[0:n, 0:1])
            act(ssn[0:n, 1:2], ssn[0:n, 0:1], AF.Ln, ['ssn', 'epsc'], ['ssn'], scale=1.0 / D, bias=epsc[0:n, :])
            act(ssn[0:n, 1:2], ssn[0:n, 1:2], AF.Exp, ['ssn'], ['ssn'], scale=-0.5)
            ts('dve', xsb[0:n, :], xt[s][0:n, :], ssn[0:n, 1:2], None, ALU.mult, None,
               [('xt', s), 'ssn'], ['xsb'])
            pT = PST[:].bitcast(BF16)
            for kc in range(8):
                tr(pT[:, kc * 128:kc * 128 + n], xsb[0:n, kc * 128:(kc + 1) * 128], identb[0:n, 0:n],
                   ['xsb', 'identb'], ['PST'])
            cp('act', dst, pT[:, :].rearrange("p (a b) -> p a b", a=8)[:, :, 0:n], ['PST'], [dstname])

        def rstd_of(ssap, n, width):
            act(ssap[:, 1:2], ssap[:, 0:1], AF.Ln, ['hss', 'epsc'], ['hss'], scale=1.0 / width, bias=epsc[0:n, :])
            act(ssap[:, 1:2], ssap[:, 1:2], AF.Exp, ['hss'], ['hss'], scale=-0.5)

        apos[0] = 0
        wml = ar("wml", [128, 8, NML], BF16)
        WML = load_w(wml, C_ML, NML, 0, 'wml')
        P.barrier()
        apos[0] = NML * 4
        nb = norm_bufs()
        hT = nb[3]
        gsc = nc.dram_tensor("gsc", [2, 4, 4096], F32, kind="Internal").ap()
        gst = [ar("gst%d" % i, [4, 512]) for i in range(2)]
        GI = ar("GI", [128, 128])
        GF = ar("GF", [128, 128])
        gval = ar("gval", [128, 128])
        gneg = ar("gneg", [128, 128])
        gzero = ar("gzero", [128, 128])
        bgt = ar("bgt", [128, 128])
        agt = ar("agt", [128, 128])
        cml = ar("cml", [128, 128])
        e1t = ar("e1t", [128, 128])
        e2t = ar("e2t", [128, 128])
        flt = ar("flt", [128, 128])
        LT = ar("LT", [128, 128])
        decm = ar("decm", [128, 128])
        gcol = ar("gcol", [128, 8])
        grow = ar("grow", [1, 3, 128])
        mfin = ar("mfin", [128, 1])
        P.dma('sp', gval, gvalid_d, writes=['gval'])
        P.dma('sp', LT, lt_d, writes=['LT'])
        ms('pool', gzero, 0.0, ['gzero'])
        ts('dve', gneg, gval, -1.0, BIG, ALU.add, ALU.mult, ['gval'], ['gneg'])
        PSG = PS[1]
        PSG2 = PS[2]
        for kb in range(NB if 'mlpass' not in SKIP else 0):
            s = kb % 2
            norm_T(nb, xall[kb * 128:(kb + 1) * 128, :], 128, hT[s], ('hT', s))
            q = kb % 4
            for kc in range(8):
                mm(PSG[0:4, q * 128:(q + 1) * 128], wml[:, kc, 2048:2052], hT[s][:, kc, :], kc == 0, kc == 7,
                   [('hT', s)] + WML, ['PSG'])
            for kc in range(8):
                mm(PSG2[0:4, q * 128:(q + 1) * 128], wml[:, kc, 2052:2056], hT[s][:, kc, :], kc == 0, kc == 7,
                   [('hT', s)] + WML, ['PSG2'])
            if q == 3:
                t0 = (kb - 3) * 128
                act(gst[0], PSG[0:4, :], AF.Identity, ['PSG', 'bif'], [('gst', 0)], bias=bif_t[:, 0:1])
                act(gst[1], PSG2[0:4, :], AF.Exp, ['PSG2', 'nbf'], [('gst', 1)], scale=-1.0, bias=nbf[:, 0:1])
                P.dma('sp', gsc[0, :, t0:t0 + 512], gst[0], reads=[('gst', 0)], writes=[('gsc', 0, kb)])
                P.dma('sp', gsc[1, :, t0:t0 + 512], gst[1], reads=[('gst', 1)], writes=[('gsc', 1, kb)])
        P.dma('sp', GI, gsc[0].rearrange("h (k t) -> (h k) t", t=128), reads=[('gsc', 0, k_) for k_ in range(3, NB, 4)], writes=['GI'])
        P.dma('sp', GF, gsc[1].rearrange("h (k t) -> (h k) t", t=128), reads=[('gsc', 1, k_) for k_ in range(3, NB, 4)], writes=['GF'])
        act(GF, GF, AF.Ln, ['GF', 'onec'], ['GF'], bias=onec[:, :])
        stt('dve', GF, GF, -1.0, gval, ALU.mult, ALU.mult, ['GF', 'gval'], ['GF'])
        P.op('dve', lambda e: e.tensor_tensor_scan(out=bgt, data0=GF, data1=gzero, initial=0.0,
                                                   op0=ALU.add, op1=ALU.add), ['GF', 'gzero'], ['bgt'])
        cp('dve', gcol[:, 0:1], bgt[:, 127:128], ['bgt'], ['gcol0'])
        mm(PSG[:, 0:1], LT, gcol[:, 0:1], True, True, ['LT', 'gcol0'], ['PSG'])
        cp('dve', gcol[:, 1:2], PSG[:, 0:1], ['PSG'], ['gcol1'])
        ts('dve', bgt, bgt, gcol[:, 1:2], None, ALU.add, None, ['bgt', 'gcol1'], ['bgt'])
        tt('dve', GI, GI, gval, ALU.mult, ['GI', 'gval'], ['GI'])
        tt('dve', GI, GI, gneg, ALU.add, ['GI', 'gneg'], ['GI'])
        tt('dve', agt, GI, bgt, ALU.subtract, ['GI', 'bgt'], ['agt'])
        P.op('dve', lambda e: e.tensor_tensor_scan(out=cml, data0=agt, data1=agt, initial=-1e30,
                                                   op0=ALU.max, op1=ALU.max), ['agt'], ['cml'])
        cp('dve', gcol[:, 2:3], cml[:, 127:128], ['cml'], ['gcol2'])
        mm(PSG[0:1, 128:256], gcol[:, 2:3], identf[:], True, True, ['gcol2', 'identf'], ['PSG'])
        cp('dve', grow[:, 0, :], PSG[0:1, 128:256], ['PSG'], ['grow0'])
        ms('pool', grow[:, 2, :], 0.0, ['grow2'])
        for h in range(4):
            P.op('dve', lambda e, h=h: e.tensor_tensor_scan(
                out=grow[:, 1, h * NB:(h + 1) * NB], data0=grow[:, 0, h * NB:(h + 1) * NB],
                data1=grow[:, 0, h * NB:(h + 1) * NB], initial=0.0, op0=ALU.max, op1=ALU.max),
                ['grow0'], ['grow1'])
            cp('dve', grow[:, 2, h * NB + 1:(h + 1) * NB], grow[:, 1, h * NB:(h + 1) * NB - 1], ['grow1', 'grow2'],
               ['grow2'])
        mm(PSG[:, 256:257], grow[:, 2, :], onec[0:1, 0:1], True, True, ['grow2', 'onec'], ['PSG'])
        mm(PSG[:, 257:258], grow[:, 1, :], onec[0:1, 0:1], True, True, ['grow1', 'onec'], ['PSG'])
        cp('dve', gcol[:, 3:5], PSG[:, 256:258], ['PSG'], ['gcol34'])
        ts('dve', gcol[:, 5:6], gcol[:, 3:4], -1.0, LNC, ALU.mult, ALU.add, ['gcol34'], ['gcol5'])
        ts('dve', gcol[:, 6:7], gcol[:, 4:5], -1.0, LNC, ALU.mult, ALU.add, ['gcol34'], ['gcol6'])
        ts('dve', gcol[:, 7:8], gcol[:, 3:4], -1.0, None, ALU.mult, None, ['gcol34'], ['gcol7'])
        act(e1t, agt, AF.Exp, ['agt', 'gcol5'], ['e1t'], bias=gcol[:, 5:6])
        act(e2t, agt, AF.Exp, ['agt', 'gcol6'], ['e2t'], bias=gcol[:, 6:7])
        act(flt, bgt, AF.Exp, ['bgt', 'gcol7'], ['flt'], scale=-1.0, bias=gcol[:, 7:8])
        tt('dve', mfin, bgt[:, 127:128], gcol[:, 4:5], ALU.add, ['bgt', 'gcol34'], ['mfin'])
        for h in range(4):
            P.dma('sp', p_m[h:h + 1, :], mfin[h * NB + NB - 1:h * NB + NB, :], reads=['mfin'])
        tt('dve', gcol[:, 0:1], gcol[:, 3:4], gcol[:, 4:5], ALU.subtract, ['gcol34', 'gcol0'], ['gcol0'])
        act(gcol[:, 0:1], gcol[:, 0:1], AF.Exp, ['gcol0'], ['gcol0'])
        ts('dve', decm, identf[:], 0.0, gcol[:, 0:1], ALU.mult, ALU.add, ['identf', 'gcol0'], ['decm'])
        mm(PSG2[:, 0:128], decm, identf[:], True, True, ['decm', 'identf'], ['PSG2'])
        cp('dve', decb[:].rearrange("p a b -> p (a b)"), PSG2[:, 0:128], ['PSG2'], ['decb'])
        PSC = PS[3]
        for j, r in enumerate((e1t, e2t, flt)):
            mm(PSC[:, j * 128:(j + 1) * 128], r, identf[:], True, True, ['e1t', 'e2t', 'flt', 'identf'], ['PSC'])
        cp('dve', colsT[:].rearrange("p a b c -> p (a b c)"), PSC[:, 0:384], ['PSC'], ['cols'])
        P.barrier()

        if stage == 1:
            P.emit()
            return nc
        apos[0] = NML * 4
        nb = norm_bufs()
        hT = nb[3]
        kTM = ar("kTM", [128, 512], BF16)
        wk2 = ar("wk2", [128, 4, 128], BF16)
        v1 = ar("v1", [128, 4, 136], BF16)
        osig = ar("osig", [128, 512])
        qTm = ar("qTm", [128, 4, 128], BF16)
        kTm = ar("kTm", [128, 4, 128], BF16)
        Sp = ar("Sp", [128, 128], BF16)
        hh = ar("hh", [128, 128])
        hsq = ar("hsq", [128, 128])
        mixTM = ar("mixTM", [128, 512], BF16)
        ms('pool', v1.rearrange("p a b -> p (a b)"), 1.0, ['v1'])
        PZ = [PS[1], PS[2]]
        PF = PS[3]
        PM = PS[7]
        PMS = PS[4]
        PMN = PS[5]
        zc = [0]

        def proj_tm(wv, wnames, hTs, hname, c0, n=128):
            z = zc[0] % 2
            zc[0] += 1
            for kc in range(8):
                mm(PZ[z][0:n, :], hTs[:, kc, 0:n], wv[:, kc, c0:c0 + 512], kc == 0, kc == 7,
                   [hname] + wnames, [('PZ', z)])
            return z

        for kb in range(NB if 'mlpass' not in SKIP else 0):
            if stage == 15 and kb == 15:
                break
            win = kb >= W0
            wb = kb - W0
            s = kb % 2
            norm_T(nb, xall[kb * 128:(kb + 1) * 128, :], 128, hT[s], ('hT', s))
            hn = ('hT', s)
            z = proj_tm(wml, WML, hT[s], hn, 512)
            cp('act', kTM, PZ[z][:], [('PZ', z)], ['kTM'])
            for h in range(4):
                ts('dve', wk2[:, h, :], kTM[:, h * 128:(h + 1) * 128], colsT[:, 1, h, kb:kb + 1], None, ALU.mult, None,
                   ['kTM', 'cols'], ['wk2'])
            z = proj_tm(wml, WML, hT[s], hn, 1024)
            cp('act', v1[:, :, 0:128], PZ[z][:].rearrange("p (a b) -> p a b", a=4), [('PZ', z)], ['v1'])
            if win and 'osig' not in SKIP:
                z = proj_tm(wml, WML, hT[s], hn, 1536)
                act(osig, PZ[z][:], AF.Exp, [('PZ', z)], ['osig'], scale=-1.0)
                ts('dve', osig, osig, 1.0, None, ALU.add, None, ['osig'], ['osig'])
                P.op('dve', lambda e: e.reciprocal(out=osig, in_=osig), ['osig'], ['osig'])
                tt('dve', osig, osig, ghead_t[:, 0, :], ALU.mult, ['osig', 'ghead'], ['osig'])
                for (c0, dst, nm) in (((0, qTm, 'qTm'), (512, kTm, 'kTm')) if 'fm' not in SKIP else ()):
                    for h in range(4):
                        for kc in range(8):
                            mm(PF[:, h * 128:(h + 1) * 128], wml[:, kc, c0 + h * 128:c0 + (h + 1) * 128],
                               hT[s][:, kc, :], kc == 0, kc == 7, [hn] + WML, ['PF'])
                    cp('act', dst.rearrange("p a b -> p (a b)"), PF[:], ['PF'], [nm])
            for h in range(4):
                if win and 'out' not in SKIP:
                    mm(PMS[:, 0:128], kTm[:, h, :], qTm[:, h, :], True, True, ['kTm', 'qTm'], ['PM0'])
                    stt('dve', Sp, PMS[:, 0:128], colsT[:, 0, h, kb:kb + 1], trilf[:], ALU.mult, ALU.mult,
                        ['PM0', 'cols', 'trilf'], ['Sp'])
                    mm(PMN[:, 0:129], qTm[:, h, :], Cbf[:, h, 0:129], True, False, ['qTm', ('Cbf', h)], ['PM1'])
                    mm(PMN[:, 0:129], Sp, v1[:, h, 0:129], False, True, ['Sp', 'v1'], ['PM1'])
                    act(dn[:, 0:1], PMN[:, 128:129], AF.Abs, ['PM1'], ['dn'])
                    ts('dve', dn[:, 0:1], dn[:, 0:1], colsT[:, 2, h, kb:kb + 1], None, ALU.max, None,
                       ['dn', 'cols'], ['dn'])
                    P.op('dve', lambda e: e.reciprocal(out=dn[:, 1:2], in_=dn[:, 0:1]), ['dn'], ['dn'])
                    act(hh, PMN[:, 0:128], AF.Copy, ['PM1', 'dn'], ['hh'], scale=dn[:, 1:2])
                    act(hsq, hh, AF.Square, ['hh'], ['hsq', 'hss'], accum_out=hss[:, 0:1])
                    rstd_of(hss[:], 128, 128)
                    stt('dve', mixTM[:, h * 128:(h + 1) * 128], hh, hss[:, 1:2], osig[:, h * 128:(h + 1) * 128],
                        ALU.mult, ALU.mult, ['hh', 'hss', 'osig'], ['mixTM'])
                mm(PM[:, 0:129], wk2[:, h, :], v1[:, h, 0:129], True, True, ['wk2', 'v1'], ['PM2'])
                stt('dve', Cst[:, h, 0:129], Cst[:, h, 0:129], decb[:, h, kb:kb + 1], PM[:, 0:129], ALU.mult, ALU.add,
                    ['Cst', 'decb', 'PM2'], ['Cst'])
                cp('act', Cbf[:, h, 0:129], Cst[:, h, 0:129], ['Cst'], [('Cbf', h)])
            if win and 'trans' not in SKIP:
                pT = PST[:].bitcast(BF16)
                for kc in range(4):
                    tr(pT[:, kc * 128:(kc + 1) * 128], mixTM[:, kc * 128:(kc + 1) * 128], identb[:],
                       ['mixTM', 'identb'], ['PST'])
                cp('act', mixT[:, 0:4, wb * 128:(wb + 1) * 128], pT[:, 0:512].rearrange("p (a b) -> p a b", a=4),
                   ['PST'], [('mixT', wb, 0)])
        if 'sml' not in SKIP:
            hTs = ar("hTs", [128, 8, 64], BF16)
            C0all = ar("C0all", [128, 16, 4, 128])
            kTMs = ar("kTMs", [64, 512], BF16)
            v1s = ar("v1s", [64, 4, 136], BF16)
            osigs = ar("osigs", [64, 512])
            qTMs = ar("qTMs", [64, 512])
            qTs = ar("qTs", [128, 4, 64], BF16)
            kTs = ar("kTs", [128, 4, 64], BF16)
            qTf = ar("qTf", [128, 4, 64])
            gs = ar("gs", [64, 8])
            bifb = ar("bifb", [64, 8])
            lfs = ar("lfs", [64, 4])
            bs = ar("bs", [64, 4])
            a_s = ar("a_s", [64, 4])
            m0s = ar("m0s", [16, 4])
            n0s = ar("n0s", [16, 512])
            n0f = ar("n0f", [64, 128])
            m0x = ar("m0x", [64, 4])
            n0x = ar("n0x", [64, 512])
            G1x = ar("G1x", [64, 4])
            cmx = ar("cmx", [64, 4])
            dgs = ar("dgs", [64, 64])
            Ams = ar("Ams", [64, 64])
            e1x = ar("e1x", [64, 4])
            e2x = ar("e2x", [64, 4])
            flx = ar("flx", [64, 4])
            dcx = ar("dcx", [64, 4])
            mnx = ar("mnx", [64, 4])
            tb = ar("tb", [64, 4])
            qn = ar("qn", [64, 4])
            prod = ar("prod", [64, 512])
            bm_t = ar("bm_t", [64, 64])
            bmn_t = ar("bmn_t", [64, 64])
            bmask_t = ar("bmask_t", [64, 16])
            bmaskb = ar("bmaskb", [64, 16], BF16)
            bmT_t = ar("bmT_t", [16, 64])
            onesf = ar("onesf", [64, 128])
            decbs = ar("decbs", [128, 4, 64])
            Sps = ar("Sps", [64, 64], BF16)
            itf = ar("itf", [128, 64])
            itT = ar("itT", [64, 128])
            hhs = ar("hhs", [64, 128])
            wk2s = ar("wk2s", [64, 4, 128], BF16)
            vx = ar("vx", [64, 16, 128], BF16)
            Cout = [ar("Cout%d" % i, [128, 4, 128]) for i in range(2)]
            n0T = ar("n0T", [128, 64])
            nnT = ar("nnT", [128, 64])
            nout = ar("nout", [64, 128])
            mixTMs = ar("mixTMs", [64, 512], BF16)
            P.dma('sp', bm_t, bm_d, writes=['bm'])
            P.dma('sp', bmn_t, bmn_d, writes=['bmn'])
            P.dma('sp', bmask_t, bmask_d, writes=['bmask'])
            P.dma('sp', bmT_t, bmT_d, writes=['bmT'])
            P.dma('sp', m0s, stm, writes=['m0s'])
            P.dma('sp', n0s, stn.rearrange("b h d -> b (h d)"), writes=['n0s'])
            P.dma('sp', n0f, stn.rearrange("b h d -> (b h) d"), writes=['n0f'])
            P.dma('sp', bifb, bifr_d.partition_broadcast(64), writes=['bifb'])
            for b_ in range(16):
                P.dma('sp', C0all[:, b_, :, :], stC[b_].rearrange("h d e -> d h e"), writes=[('C0', b_)])
            ms('pool', onesf, 1.0, ['onesf'])
            ms('pool', v1s.rearrange("p a b -> p (a b)"), 1.0, ['v1s'])
            cp('dve', bmaskb, bmask_t, ['bmask'], ['bmaskb'])
            norm_T(nb, xsm, 64, hTs, 'hTs')
            z = proj_tm(wml, WML, hTs, 'hTs', 512, 64)
            cp('act', kTMs, PZ[z][0:64, :], [('PZ', z)], ['kTMs'])
            z = proj_tm(wml, WML, hTs, 'hTs', 1024, 64)
            cp('act', v1s[:, :, 0:128], PZ[z][0:64, :].rearrange("p (a b) -> p a b", a=4), [('PZ', z)], ['v1s'])
            z = proj_tm(wml, WML, hTs, 'hTs', 1536, 64)
            act(osigs, PZ[z][0:64, :], AF.Exp, [('PZ', z)], ['osigs'], scale=-1.0)
            ts('dve', osigs, osigs, 1.0, None, ALU.add, None, ['osigs'], ['osigs'])
            P.op('dve', lambda e: e.reciprocal(out=osigs, in_=osigs), ['osigs'], ['osigs'])
            tt('dve', osigs, osigs, ghead_t[0:64, 0, :], ALU.mult, ['osigs', 'ghead'], ['osigs'])
            z = proj_tm(wml, WML, hTs, 'hTs', 0, 64)
            cp('act', qTMs, PZ[z][0:64, :], [('PZ', z)], ['qTMs'])
            for (c0, dst, nm) in ((0, qTs, 'qTs'), (512, kTs, 'kTs')):
                for h in range(4):
                    for kc in range(8):
                        mm(PF[:, h * 64:(h + 1) * 64], wml[:, kc, c0 + h * 128:c0 + (h + 1) * 128], hTs[:, kc, :],
                           kc == 0, kc == 7, ['hTs'] + WML, ['PF'])
                cp('act', dst.rearrange("p a b -> p (a b)"), PF[:, 0:256], ['PF'], [nm])
                if nm == 'qTs':
                    cp('act', qTf.rearrange("p a b -> p (a b)"), PF[:, 0:256], ['PF'], ['qTf'])
            for kc in range(8):
                mm(PM[0:64, 0:8], hTs[:, kc, :], wml[:, kc, 2048:2056], kc == 0, kc == 7, ['hTs'] + WML, ['PM2'])
            tt('dve', gs, PM[0:64, 0:8], bifb, ALU.add, ['PM2', 'bifb'], ['gs'])
            act(lfs, gs[:, 4:8], AF.Exp, ['gs'], ['lfs'], scale=-1.0)
            act(lfs, lfs, AF.Ln, ['lfs', 'onec'], ['lfs'], bias=onec[0:64, :])
            ts('dve', lfs, lfs, -1.0, None, ALU.mult, None, ['lfs'], ['lfs'])
            mm(PM[0:64, 8:12], bm_t, lfs, True, True, ['bm', 'lfs'], ['PM2'])
            cp('dve', bs, PM[0:64, 8:12], ['PM2'], ['bs'])
            tt('dve', a_s, gs[:, 0:4], bs, ALU.subtract, ['gs', 'bs'], ['a_s'])
            mm(PM[0:64, 12:16], bmT_t, m0s, True, True, ['bmT', 'm0s'], ['PM2'])
            cp('dve', m0x, PM[0:64, 12:16], ['PM2'], ['m0x'])
            mm(PZ[0][0:64, :], bmT_t, n0s, True, True, ['bmT', 'n0s'], [('PZ', 0)])
            cp('act', n0x, PZ[0][0:64, :], [('PZ', 0)], ['n0x'])
            for h in range(4):
                ts('dve', dgs, identf[0:64, 0:64], a_s[:, h:h + 1], None, ALU.mult, None, ['identf', 'a_s'], ['dgs'])
                mm(PMS[0:64, 0:64], onesf[:, 0:64], dgs, True, True, ['onesf', 'dgs'], ['PM0'])
                tt('dve', Ams, PMS[0:64, 0:64], bmn_t, ALU.add, ['PM0', 'bmn'], ['Ams'])
                P.op('dve', lambda e, h=h: e.tensor_reduce(out=cmx[:, h:h + 1], in_=Ams, axis=AX.X, op=ALU.max),
                     ['Ams'], ['cmx'])
            tt('dve', G1x, cmx, m0x, ALU.max, ['cmx', 'm0x'], ['G1x'])
            tt('dve', tb, a_s, m0x, ALU.subtract, ['a_s', 'm0x'], ['tb'])
            act(e1x, tb, AF.Exp, ['tb', 'lncc'], ['e1x'], bias=lncc[0:64, :])
            tt('dve', tb, a_s, G1x, ALU.subtract, ['a_s', 'G1x', 'e1x'], ['tb'])
            act(e2x, tb, AF.Exp, ['tb', 'lncc'], ['e2x'], bias=lncc[0:64, :])
            tt('dve', tb, bs, m0x, ALU.add, ['bs', 'm0x', 'e2x'], ['tb'])
            act(flx, tb, AF.Exp, ['tb'], ['flx'], scale=-1.0)
            tt('dve', tb, m0x, G1x, ALU.subtract, ['m0x', 'G1x', 'flx'], ['tb'])
            act(dcx, tb, AF.Exp, ['tb'], ['dcx'])
            tt('dve', mnx, bs, G1x, ALU.add, ['bs', 'G1x'], ['mnx'])
            P.dma('sp', s_m, mnx[3:64:4, :], reads=['mnx'])
            tt('dve', prod, qTMs, n0x, ALU.mult, ['qTMs', 'n0x'], ['prod'])
            P.op('dve', lambda e: e.tensor_reduce(out=qn, in_=prod.rearrange("p (a b) -> p a b", a=4), axis=AX.X,
                                                  op=ALU.add), ['prod'], ['qn'])
            for h in range(4):
                ts('dve', dgs, identf[0:64, 0:64], dcx[:, h:h + 1], None, ALU.mult, None, ['identf', 'dcx'], ['dgs'])
                mm(PMS[:, 64:128], onesf, dgs, True, True, ['onesf', 'dgs'], ['PM0'])
                cp('dve', decbs[:, h, :], PMS[:, 64:128], ['PM0'], ['decbs'])
            mm(PMS[:, 128:192], n0f, identf[0:64, 0:64], True, True, ['n0f', 'identf'], ['PM0'])
            cp('dve', n0T, PMS[:, 128:192], ['PM0'], ['n0T'])
            for h in range(4):
                mm(PMS[0:64, 0:64], kTs[:, h, :], qTs[:, h, :], True, True, ['kTs', 'qTs'], ['PM0'])
                stt('dve', Sps, PMS[0:64, 0:64], e1x[:, h:h + 1], bm_t, ALU.mult, ALU.mult, ['PM0', 'e1x', 'bm'], ['Sps'])
                mm(PMN[0:64, 0:129], Sps, v1s[:, h, 0:129], True, True, ['Sps', 'v1s'], ['PM1'])
                for b_ in range(16):
                    mm(PF[:, b_ * 4:(b_ + 1) * 4], C0all[:, b_, h, :], qTf[:, h, b_ * 4:(b_ + 1) * 4], True, True,
                       [('C0', b_), 'qTf'], ['PF'])
                cp('act', itf, PF[:, 0:64], ['PF'], ['itf'])
                mm(PM[0:64, 0:128], itf, identf[:], True, True, ['itf', 'identf'], ['PM2'])
                cp('act', itT, PM[0:64, 0:128], ['PM2'], ['itT'])
                tt('dve', dn[0:64, 0:1], PMN[0:64, 128:129], qn[:, h:h + 1], ALU.add, ['PM1', 'qn'], ['dn'])
                act(dn[0:64, 0:1], dn[0:64, 0:1], AF.Abs, ['dn'], ['dn'])
                ts('dve', dn[0:64, 0:1], dn[0:64, 0:1], flx[:, h:h + 1], None, ALU.max, None, ['dn', 'flx'], ['dn'])
                P.op('dve', lambda e: e.reciprocal(out=dn[0:64, 1:2], in_=dn[0:64, 0:1]), ['dn'], ['dn'])
                tt('dve', hhs, PMN[0:64, 0:128], itT, ALU.add, ['PM1', 'itT'], ['hhs'])
                ts('dve', hhs, hhs, dn[0:64, 1:2], None, ALU.mult, None, ['hhs', 'dn'], ['hhs'])
                act(hsq[0:64, :], hhs, AF.Square, ['hhs'], ['hsq', 'hss'], accum_out=hss[0:64, 0:1])
                rstd_of(hss[0:64, :], 64, 128)
                stt('dve', mixTMs[:, h * 128:(h + 1) * 128], hhs, hss[0:64, 1:2], osigs[:, h * 128:(h + 1) * 128],
                    ALU.mult, ALU.mult, ['hhs', 'hss', 'osigs'], ['mixTMs'])
                ts('dve', wk2s[:, h, :], kTMs[:, h * 128:(h + 1) * 128], e2x[:, h:h + 1], None, ALU.mult, None,
                   ['kTMs', 'e2x'], ['wk2s'])
                tt('pool', vx, v1s[:, h, 0:128].unsqueeze(1).to_broadcast([64, 16, 128]),
                   bmaskb.unsqueeze(2).to_broadcast([64, 16, 128]), ALU.mult, ['v1s', 'bmaskb'], ['vx'])
                for g in range(4):
                    z = zc[0] % 2
                    zc[0] += 1
                    mm(PZ[z][:, :], wk2s[:, h, :], vx[:, g * 4:(g + 1) * 4, :].rearrange("p a b -> p (a b)"), True, True,
                       ['wk2s', 'vx'], [('PZ', z)])
                    co = Cout[g % 2]
                    for bb in range(4):
                        b_ = g * 4 + bb
                        stt('dve', co[:, bb, :], C0all[:, b_, h, :], decbs[:, h, b_ * 4:b_ * 4 + 1],
                            PZ[z][:, bb * 128:(bb + 1) * 128], ALU.mult, ALU.add,
                            [('C0', b_), 'decbs', ('PZ', z)], [('Cout', g % 2)])
                    P.dma('sp', s_C[g * 4:(g + 1) * 4, h].rearrange("b d e -> d b e"), co, reads=[('Cout', g % 2)])
                mm(PMS[:, 192 + h * 16:192 + (h + 1) * 16], wk2s[:, h, :], bmaskb, True, True, ['wk2s', 'bmaskb'], ['PM0'])
            n0T3 = n0T.rearrange("p (b h) -> p b h", h=4)
            nnT3 = nnT.rearrange("p (b h) -> p b h", h=4)
            for h in range(4):
                tt('dve', nnT3[:, :, h], n0T3[:, :, h], decbs[:, h, 0:64:4], ALU.mult, ['n0T', 'decbs'], ['nnT'])
                tt('dve', nnT3[:, :, h], nnT3[:, :, h], PMS[:, 192 + h * 16:192 + (h + 1) * 16], ALU.add, ['nnT', 'PM0'],
                   ['nnT'])
            mm(PM[0:64, 0:128], nnT, identf[:], True, True, ['nnT', 'identf'], ['PM2'])
            cp('dve', nout, PM[0:64, 0:128], ['PM2'], ['nout'])
            P.dma('sp', s_n.rearrange("b h d -> (b h) d"), nout, reads=['nout'])
            pT = PST[:].bitcast(BF16)
            for kc in range(4):
                tr(pT[:, kc * 128:kc * 128 + 64], mixTMs[:, kc * 128:(kc + 1) * 128], identb[0:64, 0:64],
                   ['mixTMs', 'identb'], ['PST'])
            cp('act', mixTs[:, 0:4, :], pT[:, 0:512].rearrange("p (a b) -> p a b", a=4)[:, :, 0:64], ['PST'], ['mixTs0'])
        for h in range(4):
            P.dma('sp', p_C[h], Cst[:, h, 0:128], reads=['Cst'])
            P.dma('sp', p_n[h:h + 1, :].rearrange("o d -> d o"), Cst[:, h, 128:129], reads=['Cst'])
        P.barrier()

        if stage in (2, 15):
            P.emit()
            return nc
        apos[0] = 0
        wda = ar("wda", [128, 8, NDA], BF16)
        KT = ar("KT", [128, 4, NB * 128], BF16)
        V1 = ar("V1", [128, NB, 4, 136], BF16)
        keep = apos[0]
        WDA = load_w(wda, C_DA, NDA, 0, 'wda')
        P.barrier()
        apos[0] = keep
        nb = norm_bufs()
        hT = nb[3]
        dkst = [ar("dkst%d" % i, [128, 512]) for i in range(2)]
        dvst = [ar("dvst%d" % i, [128, 512]) for i in range(2)]
        dqst = ar("dqst", [128, 512])
        rt = ar("rt", [128, 4, 64])
        dkb = ar("dkb", [128, 512], BF16)
        dqb = ar("dqb", [128, 512], BF16)
        Qblk = ar("Qblk", [128, 4, 256], BF16)
        Pt = [ar("Pt%d" % i, [128, 256], BF16) for i in range(2)]
        t1a = ar("t1a", [128, 128])
        aa = ar("aa", [128, 128])
        hsq = ar("hsq", [128, 128])
        mixTM = ar("mixTM", [128, 512], BF16)
        ms('pool', Qblk.rearrange("p a b -> p (a b)"), 0.0, ['Qblk'])
        PSSB = [PS[4], PS[3]]
        PO = [PS[5], PS[6]]

        rcs = [ar("rcs%d" % i, [128, 2, 64]) for i in range(2)]
        rctr = [0]

        def rope_load(tok0, n, src_c, src_s):
            i = rctr[0] % 2
            rctr[0] += 1
            P.dma('sp', rcs[i][0:n, 0, :], src_c[tok0:tok0 + n, :], writes=[('rcs', i)])
            P.dma('sp', rcs[i][0:n, 1, :], src_s[tok0:tok0 + n, :], writes=[('rcs', i)])
            return i

        def rope(buf, bname, i, n=128):
            if 'rope' in SKIP:
                return
            b3 = buf.rearrange("p (a b) -> p a b", b=64)
            x1 = b3[:, :, 0:8]
            x2 = b3[:, :, 8:16]
            cb = rcs[i][0:n, 0, :].rearrange("p (a b) -> p a b", b=8)
            sbb = rcs[i][0:n, 1, :].rearrange("p (a b) -> p a b", b=8)
            r = rt[0:n].rearrange("p a (b c) -> p a b c", c=8)
            rn = ('rcs', i)
            tt('dve', r[:, 0], x1, cb, ALU.mult, [bname, rn], ['rt0'])
            tt('pool', r[:, 1], x2, sbb, ALU.mult, [bname, rn], ['rt1'])
            tt('dve', r[:, 2], x2, cb, ALU.mult, [bname, rn], ['rt2'])
            tt('pool', r[:, 3], x1, sbb, ALU.mult, [bname, rn], ['rt3'])
            tt('dve', x1, r[:, 0], r[:, 1], ALU.subtract, ['rt0', 'rt1', 'rt3'], [bname])
            tt('pool', x2, r[:, 2], r[:, 3], ALU.add, ['rt2', 'rt3', 'rt0'], [bname])

        for kb in range(NB):
            win = kb >= W0
            own = kb > W0
            wb = kb - W0
            ob = kb - W0 - 1
            s = kb % 2
            norm_T(nb, xall[kb * 128:(kb + 1) * 128, :], 128, hT[s], ('hT', s))
            hn = ('hT', s)
            z = proj_tm(wda, WDA, hT[s], hn, 1024)
            for h_ in (range(4) if 'v1ms' not in SKIP else ()):
                ms('pool', V1[:, kb, h_, 128:130], 1.0, [('V1', kb)])
            cp('act', V1[:, kb, :, 0:128], PZ[z][:].rearrange("p (a b) -> p a b", a=4), [('PZ', z)], [('V1', kb)])
            if own and 'outd' not in SKIP:
                cp('act', dvst[s], PZ[z][:], [('PZ', z)], [('dvst', s)])
                P.dma('sp', p_dv[ob * 128:(ob + 1) * 128, :], dvst[s], reads=[('dvst', s)])
            z = proj_tm(wda, WDA, hT[s], hn, 512)
            cp('act', dkst[s], PZ[z][:], [('PZ', z)], [('dkst', s)])
            ri = rope_load(kb * 128, 128, ropeC_d, ropeS_d) if 'ropeld' not in SKIP else 0
            rope(dkst[s], ('dkst', s), ri)
            if own and 'outd' not in SKIP:
                P.dma('sp', p_dk[ob * 128:(ob + 1) * 128, :], dkst[s], reads=[('dkst', s)])
            cp('dve', dkb, dkst[s], [('dkst', s)], ['dkb'])
            pT = PST[:].bitcast(BF16)
            for h in range(4):
                tr(pT[:, h * 128:(h + 1) * 128], dkb[:, h * 128:(h + 1) * 128], identb[:], ['dkb', 'identb'], ['PST'])
            cp('act', KT[:, :, kb * 128:(kb + 1) * 128], pT[:, 0:512].rearrange("p (a b) -> p a b", a=4),
               ['PST'], [('KT', kb)])
            if not win or 'qproj' in SKIP:
                continue
            z = proj_tm(wda, WDA, hT[s], hn, 0)
            cp('act', dqst, PZ[z][:], [('PZ', z)], ['dqst'])
            rope(dqst, 'dqst', ri)
            cp('dve', dqb, dqst, ['dqst'], ['dqb'])
            for h in range(4):
                tr(pT[:, 512 + h * 128:512 + (h + 1) * 128], dqb[:, h * 128:(h + 1) * 128], identb[:],
                   ['dqb', 'identb'], ['PST'])
            pq = pT[:, 512:1024].rearrange("p (a b) -> p a b", a=4)
            cp('act', Qblk[0:64, :, 0:128], pq[0:64], ['PST'], ['Qblk'])
            cp('act', Qblk[64:128, :, 128:256], pq[64:128], ['PST'], ['Qblk'])
            for h in (range(4) if 'attn' not in SKIP else ()):
                pon = 'PO'

                class _PO3:
                    def __getitem__(self, key):
                        p_, c_, f_ = key
                        return PO[c_][p_, f_]
                po3 = _PO3()
                for kk in range(kb + 1):
                    half = kk % 2
                    sc = PSSB[half][:, 0:256]
                    mm(sc, KT[:, h, kk * 128:(kk + 1) * 128], Qblk[:, h, :], True, True,
                       [('KT', kk), 'Qblk'], [('PSS', half)])
                    pt = Pt[half]
                    if kk < kb:
                        act(pt, sc, AF.Exp, [('PSS', half), 'kbias'], [('Pt', half)], scale=0.125,
                            bias=kbias[:, kk:kk + 1])
                    else:
                        act(pt, sc, AF.Exp, [('PSS', half)], [('Pt', half)], scale=0.125)
                        tt('pool', pt, pt, tril2b[:], ALU.mult, [('Pt', half), 'tril2b'], [('Pt', half)])
                    for c in range(2):
                        mm(po3[:, c, 0:129], pt[:, c * 128:(c + 1) * 128], V1[:, kk, h, 0:129], kk == 0, kk == kb,
                           [('Pt', half), ('V1', kk)], [pon])
                ts('dve', dn[:, 0:1], po3[:, 0, 128:129], 1e-30, None, ALU.max, None, [pon], ['dn'])
                ts('dve', dn[:, 2:3], po3[:, 1, 128:129], 1e-30, None, ALU.max, None, [pon], ['dn'])
                P.op('dve', lambda e: e.reciprocal(out=dn[:, 1:2], in_=dn[:, 0:1]), ['dn'], ['dn'])
                P.op('dve', lambda e: e.reciprocal(out=dn[:, 3:4], in_=dn[:, 2:3]), ['dn'], ['dn'])
                tt('dve', dn[:, 3:4], dn[:, 3:4], lam_t[:], ALU.mult, ['dn', 'lam'], ['dn'])
                act(t1a, po3[:, 1, 0:128], AF.Copy, [pon, 'dn'], ['t1a'], scale=dn[:, 3:4])
                stt('dve', aa, po3[:, 0, 0:128], dn[:, 1:2], t1a, ALU.mult, ALU.subtract,
                    [pon, 'dn', 't1a'], ['aa'])
                act(hsq, aa, AF.Square, ['aa'], ['hsq', 'hss'], accum_out=hss[:, 0:1])
                rstd_of(hss[:], 128, 128)
                stt('dve', mixTM[:, h * 128:(h + 1) * 128], aa, hss[:, 1:2],
                    ghead_t[:, 1, h * 128:(h + 1) * 128], ALU.mult, ALU.mult, ['aa', 'hss', 'ghead'], ['mixTM'])
            for kc in range(4):
                tr(pT[:, kc * 128:(kc + 1) * 128], mixTM[:, kc * 128:(kc + 1) * 128], identb[:],
                   ['mixTM', 'identb'], ['PST'])
            cp('act', mixT[:, 4:8, wb * 128:(wb + 1) * 128], pT[:, 0:512].rearrange("p (a b) -> p a b", a=4),
               ['PST'], [('mixT', wb, 1)])
        if 'sda' not in SKIP:
            P.barrier()
            apos[0] = NDA * 4
            hTs2 = ar("hTs2", [128, 8, 64], BF16)
            dvs = ar("dvs", [64, 512])
            dks = ar("dks", [64, 512])
            dqs = ar("dqs", [64, 512])
            vsb = ar("vsb", [64, 4, 128], BF16)
            dksb = ar("dksb", [64, 512], BF16)
            dqsb = ar("dqsb", [64, 512], BF16)
            KTs = ar("KTs", [128, 4, 64], BF16)
            Qbs = ar("Qbs", [128, 4, 2, 64], BF16)
            bm2 = ar("bm2", [64, 128], BF16)
            bmf = ar("bmf", [64, 64])
            Pself = ar("Pself", [64, 128], BF16)
            onesb2 = ar("onesb2", [128, 128], BF16)
            accS = ar("accS", [128, 4, 128])
            denS = ar("denS", [128, 4, 128])
            accP = ar("accP", [128, 16, 32])
            denP = ar("denP", [128, 16, 32])
            ptI = ar("ptI", [128, 256], I32)
            ptF = ar("ptF", [128, 256])
            idxI = ar("idxI", [128, 256], I32)
            iot = ar("iot", [128, 1])
            kpg = [ar("kpg%d" % i, [128, 512]) for i in range(2)]
            vpg = [ar("vpg%d" % i, [128, 512]) for i in range(2)]
            kpb = ar("kpb", [128, 512], BF16)
            vpb = [ar("vpb%d" % i, [128, 512], BF16) for i in range(2)]
            KTp = ar("KTp", [128, 4, 128], BF16)
            Pp = ar("Pp", [128, 32], BF16)
            numT = ar("numT", [128, 4, 2, 64])
            denT = ar("denT", [128, 4, 2, 64])
            aT = ar("aT", [128, 4, 64])
            aTM = ar("aTM", [64, 128])
            mixTMs2 = ar("mixTMs2", [64, 512], BF16)
            P.dma('sp', bmf, bm_d, writes=['bmf'])
            P.dma('sp', ptI, ptab.partition_broadcast(128), writes=['ptI'])
            P.dma('sp', iot, iota_d, writes=['iot'])
            cp('dve', bm2[:, 0:64], bmf, ['bmf'], ['bm2'])
            cp('dve', bm2[:, 64:128], bmf, ['bmf'], ['bm2'])
            ms('pool', onesb2, 1.0, ['onesb2'])
            ms('pool', Qbs.rearrange("p a b c -> p (a b c)"), 0.0, ['Qbs'])
            ms('pool', accP.rearrange("p a b -> p (a b)"), 0.0, ['accP'])
            ms('pool', denP.rearrange("p a b -> p (a b)"), 0.0, ['denP'])
            cp('dve', ptF, ptI, ['ptI'], ['ptF'])
            ts('dve', ptF, ptF, 128.0, iot[:, 0:1], ALU.mult, ALU.add, ['ptF', 'iot'], ['ptF'])
            cp('dve', idxI, ptF, ['ptF'], ['idxI'])
            norm_T(nb, xsm, 64, hTs2, 'hTs2')
            z = proj_tm(wda, WDA, hTs2, 'hTs2', 1024, 64)
            cp('act', dvs, PZ[z][0:64, :], [('PZ', z)], ['dvs'])
            P.dma('sp', s_dv, dvs, reads=['dvs'])
            cp('dve', vsb.rearrange("p a b -> p (a b)"), dvs, ['dvs'], ['vsb'])
            ri = rope_load(0, 64, ropeCs_d, ropeSs_d)
            z = proj_tm(wda, WDA, hTs2, 'hTs2', 512, 64)
            cp('act', dks, PZ[z][0:64, :], [('PZ', z)], ['dks'])
            rope(dks, 'dks', ri, 64)
            P.dma('sp', s_dk, dks, reads=['dks'])
            cp('dve', dksb, dks, ['dks'], ['dksb'])
            z = proj_tm(wda, WDA, hTs2, 'hTs2', 0, 64)
            cp('act', dqs, PZ[z][0:64, :], [('PZ', z)], ['dqs'])
            rope(dqs, 'dqs', ri, 64)
            cp('dve', dqsb, dqs, ['dqs'], ['dqsb'])
            pT = PST[:].bitcast(BF16)
            for h in range(4):
                tr(pT[:, h * 64:(h + 1) * 64], dksb[:, h * 128:(h + 1) * 128], identb[0:64, 0:64], ['dksb', 'identb'], ['PST'])
                tr(pT[:, 256 + h * 64:256 + (h + 1) * 64], dqsb[:, h * 128:(h + 1) * 128], identb[0:64, 0:64],
                   ['dqsb', 'identb'], ['PST'])
            cp('act', KTs.rearrange("p a b -> p (a b)"), pT[:, 0:256], ['PST'], ['KTs'])
            pq = pT[:, 256:512].rearrange("p (a b) -> p a b", a=4)
            cp('act', Qbs[0:64, :, 0, :], pq[0:64], ['PST'], ['Qbs'])
            cp('act', Qbs[64:128, :, 1, :], pq[64:128], ['PST'], ['Qbs'])
            for h in range(4):
                mm(PSSB[0][0:64, 0:128], KTs[:, h, :], Qbs[:, h, :, :].rearrange("p a b -> p (a b)"), True, True,
                   ['KTs', 'Qbs'], [('PSS', 0)])
                act(Pself, PSSB[0][0:64, 0:128], AF.Exp, [('PSS', 0)], ['Pself'], scale=0.125)
                tt('pool', Pself, Pself, bm2, ALU.mult, ['Pself', 'bm2'], ['Pself'])
                mm(PO[0][:, 0:128], vsb[:, h, :], Pself, True, True, ['vsb', 'Pself'], ['PO'])
                mm(PO[1][:, 0:128], onesb2[0:64, :], Pself, True, True, ['onesb2', 'Pself'], ['PO'])
                cp('act', accS[:, h, :], PO[0][:, 0:128], ['PO'], ['accS'])
                cp('act', denS[:, h, :], PO[1][:, 0:128], ['PO'], ['denS'])
            for b_ in range(16):
                for j in range(16):
                    pi = b_ * 16 + j
                    s2 = pi % 2

                    def gk(e, pi=pi, s2=s2):
                        return e.indirect_dma_start(out=kpg[s2], out_offset=None, in_=ckp,
                                                    in_offset=bass.IndirectOffsetOnAxis(ap=idxI[:, pi:pi + 1], axis=0))

                    def gv(e, pi=pi, s2=s2):
                        return e.indirect_dma_start(out=vpg[s2], out_offset=None, in_=cvp,
                                                    in_offset=bass.IndirectOffsetOnAxis(ap=idxI[:, pi:pi + 1], axis=0))
                    P.dmaf('pool', gk, reads=['idxI'], writes=[('kpg', s2)])
                    P.dmaf('pool', gv, reads=['idxI'], writes=[('vpg', s2)])
                    cp('dve', kpb, kpg[s2], [('kpg', s2)], ['kpb'])
                    cp('act', vpb[s2], vpg[s2], [('vpg', s2)], [('vpb', s2)])
                    for h in range(4):
                        tr(pT[:, h * 128:(h + 1) * 128], kpb[:, h * 128:(h + 1) * 128], identb[:], ['kpb', 'identb'], ['PST'])
                    cp('act', KTp.rearrange("p a b -> p (a b)"), pT[:, 0:512], ['PST'], ['KTp'])
                    for h in range(4):
                        mm(PSSB[1][:, h * 8:(h + 1) * 8], KTp[:, h, :], Qbs[:, h, :, b_ * 4:(b_ + 1) * 4], True, True,
                           ['KTp', 'Qbs'], [('PSS', 1)])
                    act(Pp, PSSB[1][:, 0:32], AF.Exp, [('PSS', 1)], ['Pp'], scale=0.125)
                    for h in range(4):
                        mm(PO[0][:, h * 8:(h + 1) * 8], vpb[s2][:, h * 128:(h + 1) * 128], Pp[:, h * 8:(h + 1) * 8], True, True,
                           [('vpb', s2), 'Pp'], ['PO'])
                    mm(PO[1][:, 0:32], onesb2, Pp, True, True, ['onesb2', 'Pp'], ['PO'])
                    tt('dve', accP[:, b_, :], accP[:, b_, :], PO[0][:, 0:32], ALU.add, ['accP', 'PO'], ['accP'])
                    tt('dve', denP[:, b_, :], denP[:, b_, :], PO[1][:, 0:32], ALU.add, ['denP', 'PO'], ['denP'])
            accP5 = accP.rearrange("p b (h c q) -> p h c b q", h=4, c=2)
            denP5 = denP.rearrange("p b (h c q) -> p h c b q", h=4, c=2)
            for h in range(4):
                for c in range(2):
                    tt('dve', numT[:, h, c, :].rearrange("p (b q) -> p b q", q=4), accP5[:, h, c],
                       accS[:, h, c * 64:(c + 1) * 64].rearrange("p (b q) -> p b q", q=4), ALU.add, ['accP', 'accS'], ['numT'])
                    tt('dve', denT[:, h, c, :].rearrange("p (b q) -> p b q", q=4), denP5[:, h, c],
                       denS[:, h, c * 64:(c + 1) * 64].rearrange("p (b q) -> p b q", q=4), ALU.add, ['denP', 'denS'], ['denT'])
            nT2 = numT.rearrange("p a b c -> p (a b c)")
            dT2 = denT.rearrange("p a b c -> p (a b c)")
            P.op('dve', lambda e: e.reciprocal(out=dT2, in_=dT2), ['denT'], ['denT'])
            tt('dve', nT2, nT2, dT2, ALU.mult, ['numT', 'denT'], ['numT'])
            for h in range(4):
                stt('dve', aT[:, h, :], numT[:, h, 1, :], lam_t[:, 0:1], numT[:, h, 0, :], ALU.mult, ALU.subtract,
                    ['numT', 'lam'], ['aT'])
                mm(PSSB[0][0:64, 0:128], aT[:, h, :], identf[:], True, True, ['aT', 'identf'], [('PSS', 0)])
                act(aTM, PSSB[0][0:64, 0:128], AF.Copy, [('PSS', 0)], ['aTM'], scale=-1.0)
                act(hsq[0:64, :], aTM, AF.Square, ['aTM'], ['hsq', 'hss'], accum_out=hss[0:64, 0:1])
                rstd_of(hss[0:64, :], 64, 128)
                stt('dve', mixTMs2[:, h * 128:(h + 1) * 128], aTM, hss[0:64, 1:2], ghead_t[0:64, 1, h * 128:(h + 1) * 128],
                    ALU.mult, ALU.mult, ['aTM', 'hss', 'ghead'], ['mixTMs2'])
            for kc in range(4):
                tr(pT[:, kc * 128:kc * 128 + 64], mixTMs2[:, kc * 128:(kc + 1) * 128], identb[0:64, 0:64],
                   ['mixTMs2', 'identb'], ['PST'])
            cp('act', mixTs[:, 4:8, :], pT[:, 0:512].rearrange("p (a b) -> p a b", a=4)[:, :, 0:64], ['PST'], ['mixTs1'])
            assert apos[0] <= keep, apos[0]
        P.barrier()

        if stage == 3:
            P.emit()
            return nc
        apos[0] = 0
        w_outb = ar("w_outb", [128, 8, D], BF16)
        w_mqb = ar("w_mqb", [128, 8, D], BF16)
        w_mob = ar("w_mob", [128, 8, D], BF16)
        mkT = ar("mkT", [128, 8, 256], BF16)
        mvb = ar("mvb", [128, 2, D], BF16)
        gpost_t = ar("gpost_t", [128, 3, D])
        gffn_t = ar("gffn_t", [128, D])
        cwa_t = ar("cwa_t", [128, NCF, 4])
        cwg_t = ar("cwg_t", [128, NCF, 4])
        onesb = ar("onesb", [128, 128], BF16)
        keep2 = apos[0]
        for i in range(3):
            P.dma('sp', gpost_t[:, i, :], gpost[i:i + 1, :].partition_broadcast(128), writes=['gpost'])
        P.dma('sp', gffn_t, gffn_d.partition_broadcast(128), writes=['gffn'])
        P.dma('sp', cwa_t, cwa, writes=['cwa'])
        P.dma('sp', cwg_t, cwg, writes=['cwg'])
        ms('pool', onesb, 1.0, ['onesb'])

        def load_w2(dst, src, ncols, gidx, nm):
            hw = ncols // 2
            st = [ar("%s_st%d" % (nm, i), [128, hw]) for i in range(2)]
            for kc in range(8):
                for s in range(2):
                    P.dma('sp', st[s], src[kc * 128:(kc + 1) * 128, s * hw:(s + 1) * hw], writes=[('w2st', s)])
                    if gidx is None:
                        cp('act' if s == 0 else 'pool', dst[:, kc, s * hw:(s + 1) * hw], st[s], [('w2st', s)],
                           [(nm, kc, s)])
                    elif s == 0:
                        act(dst[:, kc, 0:hw], st[s], AF.Copy, [('w2st', s), 'gpre'], [(nm, kc, s)],
                            scale=gpre_t[:, gidx, kc:kc + 1])
                    else:
                        ts('pool', dst[:, kc, hw:], st[s], gpre_t[:, gidx, kc:kc + 1], None, ALU.mult, None,
                           [('w2st', s), 'gpre'], [(nm, kc, s)])
            apos[0] -= 2 * hw
        load_w2(w_outb, w_out, D, None, 'w_out')
        load_w2(w_mqb, w_mq, D, 1, 'w_mq')
        load_w2(w_mob, w_mo, D, None, 'w_mo')
        w_mkb = ar("w_mkb", [128, 8, D], BF16)
        w_mvb = ar("w_mvb", [128, 8, D], BF16)
        load_w2(w_mkb, w_mk, D, 2, 'w_mk')
        load_w2(w_mvb, w_mv, D, 2, 'w_mv')
        P.barrier()
        nb = norm_bufs()
        hTm = ar("hTm", [128, 8, 256], BF16)
        mst = [ar("mst%d" % i, [128, 512]) for i in range(2)]
        for mb in range(2):
            norm_T(nb, memp[mb * 128:(mb + 1) * 128, :], 128, hTm[:, :, mb * 128:(mb + 1) * 128], 'hTm')
        PA = [PS[0], PS[1]]
        PT2 = PS[2]
        PQ = PS[3]
        for c in range(8):
            for kc in range(8):
                mm(PQ[:, 0:256], w_mkb[:, kc, c * 128:(c + 1) * 128], hTm[:, kc, :], kc == 0, kc == 7,
                   ['hTm', 'w_mk'], ['PQ'])
            cp('act', mkT[:, c, :], PQ[:, 0:256], ['PQ'], ['mkT'])
        for (wv, dstb, dout_, nm) in ((w_mvb, mvb, p_mv, 'w_mv'), (w_mkb, None, p_mk, 'w_mk')):
            for mb in range(2):
                for half in range(2):
                    for kc in range(8):
                        mm(PA[half][:], hTm[:, kc, mb * 128:(mb + 1) * 128], wv[:, kc, half * 512:(half + 1) * 512],
                           kc == 0, kc == 7, ['hTm', nm], [('PA', half)])
                    if dstb is not None:
                        cp('act', dstb[:, mb, half * 512:(half + 1) * 512], PA[half][:], [('PA', half)], ['mvb'])
                    cp('act', mst[half], PA[half][:], [('PA', half)], [('mst', half)])
                    P.dma('sp', dout_[mb * 128:(mb + 1) * 128, half * 512:(half + 1) * 512], mst[half],
                          reads=[('mst', half)])
        P.barrier()
        apos[0] = keep2
        x2buf = ar("x2buf", [128, 3, D])
        xnT = ar("xnT", [128, 8, 384], BF16)
        xr = ar("xr", [128, D])
        tmpf = ar("tmpf", [128, D])
        xsb2 = ar("xsb2", [128, D], BF16)
        xsq2 = ar("xsq2", [128, D], BF16)
        x1nT = ar("x1nT", [128, 8, 128], BF16)
        qT = ar("qT", [128, 8, 128], BF16)
        PmT = ar("PmT", [128, 4, 2, 128], BF16)
        rden = ar("rden", [128, 4, 128])
        oT = ar("oT", [128, 8, 128], BF16)
        ss2 = ar("ss2", [128, 4])
        wupst = [ar("wupst%d" % i, [128, 8, 128]) for i in range(2)]
        wupb = [ar("wupb%d" % i, [128, 8, 2, 128], BF16) for i in range(2)]
        wdst = [ar("wdst0", [128, D])] * 2
        wdb = [ar("wdb%d" % i, [128, D], BF16) for i in range(2)]
        uxa = ar("uxa", [128, 386])
        uxg = ar("uxg", [128, 386])
        o_c1a = apos[0]
        c1a = ar("c1a", [128, 384])
        c1g = ar("c1g", [128, 384])
        sgt = ar("sgt", [128, 384])
        actT = [ar("actT%d" % i, [128, 384], BF16) for i in range(2)]
        cara = ar("cara", [128, NCF, 2])
        carg = ar("carg", [128, NCF, 2])
        yst = [ar("yst0", [128, D])] * 2
        o_pcst = apos[0]
        pcst = ar("pcst", [2, 512])
        ms('pool', cara.rearrange("p a b -> p (a b)"), 0.0, ['cara'])
        ms('pool', carg.rearrange("p a b -> p (a b)"), 0.0, ['carg'])
        PSC2 = [PS[4], PS[5]]
        PDN = PS[6]

        def sumsq_rstd(srcs, srcnames, n, width, col):
            for i, (sap, snm) in enumerate(zip(srcs, srcnames)):
                act(xsq2[0:n, 0:sap.shape[1]], sap, AF.Square, [snm], ['xsq2', ('ss2', i)], accum_out=ss2[0:n, i:i + 1])
            if len(srcs) == 2:
                tt('dve', ss2[0:n, 0:1], ss2[0:n, 0:1], ss2[0:n, 1:2], ALU.add, [('ss2', 0), ('ss2', 1)], [('ss2', 0)])
            act(ss2[0:n, 2:3], ss2[0:n, 0:1], AF.Ln, [('ss2', 0), 'epsc'], ['ss2r'], scale=1.0 / width, bias=epsc[0:n, :])
            act(ss2[0:n, 2:3], ss2[0:n, 2:3], AF.Exp, ['ss2r'], ['ss2r'], scale=-0.5)
            return ss2[0:n, 2:3]

        def post_norm_res(gi_, resid, resname, dst, dstname, n=128):
            r = sumsq_rstd([PA[0][0:n, :], PA[1][0:n, :]], [('PA', 0), ('PA', 1)], n, D, 0)
            for half in range(2):
                sl = slice(half * 512, (half + 1) * 512)
                stt('dve', tmpf[0:n, sl], PA[half][0:n, :], r, gpost_t[0:n, gi_, sl], ALU.mult, ALU.mult,
                    [('PA', half), 'ss2r', 'gpost'], [('tmpf', half)])
                tt('pool', dst[:, sl], tmpf[0:n, sl], resid[:, sl], ALU.add, [('tmpf', half), resname], [dstname])

        def pre_norm_T(src, srcname, gbc, gname, dst, dstname, n=128):
            r = sumsq_rstd([src], [srcname], n, D, 0)
            if gbc is None:
                ts('dve', xsb2[0:n, :], src, r, None, ALU.mult, None, [srcname, 'ss2r'], ['xsb2'])
            else:
                stt('dve', xsb2[0:n, :], src, r, gbc[0:n, :], ALU.mult, ALU.mult, [srcname, 'ss2r', gname], ['xsb2'])
            pT = PT2[:].bitcast(BF16)
            for kc in range(8):
                tr(pT[:, kc * 128:kc * 128 + n], xsb2[0:n, kc * 128:(kc + 1) * 128], identb[0:n, 0:n],
                   ['xsb2', 'identb'], ['PT2'])
            cp('act', dst, pT[:, :].rearrange("p (a b) -> p a b", a=8)[:, :, 0:n], ['PT2'], [dstname])

        def mem_attend(n, mk_of, mv_of, tok_groups):
            for (c0, c1, mkT_, mvb_, mnames) in tok_groups:
                w = c1 - c0
                for h in range(4):
                    for mb in range(2):
                        for j in range(2):
                            mm(PSC2[h // 2][:, ((h % 2) * 2 + mb) * 128 + c0:((h % 2) * 2 + mb) * 128 + c1],
                               mkT_[:, 2 * h + j, mb * 128:(mb + 1) * 128], qT[:, 2 * h + j, c0:c1], j == 0, j == 1,
                               ['qT'] + mnames, [('PSC2', h // 2)])
                for hp in range(2):
                    act(PmT[:, 2 * hp:2 * hp + 2, :, c0:c1],
                        PSC2[hp][:].rearrange("p (a b c) -> p a b c", a=2, b=2)[:, :, :, c0:c1],
                        AF.Exp, [('PSC2', hp)], ['PmT'], scale=1.0 / 16.0)
                for h in range(4):
                    for mb in range(2):
                        mm(PDN[:, h * 128 + c0:h * 128 + c1], onesb, PmT[:, h, mb, c0:c1], mb == 0, mb == 1,
                           ['PmT', 'onesb'], ['PDN'])
                P.op('dve', lambda e, c0=c0, c1=c1: e.reciprocal(
                    out=rden[:, :, c0:c1], in_=PDN[:].rearrange("p (a b) -> p a b", a=4)[:, :, c0:c1]), ['PDN'], ['rden'])
                for rnd in range(2):
                    for cc in range(4):
                        c = rnd * 4 + cc
                        h = c // 2
                        for mb in range(2):
                            mm(PQ[:, cc * 128 + c0:cc * 128 + c1], mvb_[:, mb, c * 128:(c + 1) * 128],
                               PmT[:, h, mb, c0:c1], mb == 0, mb == 1, ['PmT'] + mnames, ['PQ'])
                    tt('dve', oT[:, rnd * 4:rnd * 4 + 4, c0:c1].rearrange("p (h j) t -> p h j t", j=2),
                       PQ[:].rearrange("p (h j t) -> p h j t", h=2, j=2)[:, :, :, c0:c1],
                       rden[:, rnd * 2:rnd * 2 + 2, c0:c1].unsqueeze(2).to_broadcast([128, 2, 2, w]), ALU.mult,
                       ['PQ', 'rden'], ['oT'])

        def part_a(mixsrc, mixname, xsrc_dram, n, x2dst, x2name, xnTdst, xnTname, tok_groups):
            for half in range(2):
                for kc in range(8):
                    mm(PA[half][0:n, :], mixsrc[:, kc, :], w_outb[:, kc, half * 512:(half + 1) * 512], kc == 0, kc == 7,
                       [mixname], [('PA', half)])
            P.dma('sp', xr[0:n, :], xsrc_dram, writes=['xr'])
            post_norm_res(0, xr[0:n, :], 'xr', x2dst, x2name, n)
            pre_norm_T(x2dst, x2name, None, None, x1nT[:, :, 0:n], 'x1nT', n)
            for rnd in range(2):
                for cc in range(4):
                    c = rnd * 4 + cc
                    for kc in range(8):
                        mm(PQ[:, cc * 128:cc * 128 + n], w_mqb[:, kc, c * 128:(c + 1) * 128], x1nT[:, kc, 0:n],
                           kc == 0, kc == 7, ['x1nT'], ['PQ'])
                cp('act', qT[:, rnd * 4:rnd * 4 + 4, 0:n], PQ[:].rearrange("p (a b) -> p a b", a=4)[:, :, 0:n], ['PQ'], ['qT'])
            if callable(tok_groups):
                tok_groups()
            else:
                mem_attend(n, None, None, tok_groups)
            for half in range(2):
                for c in range(8):
                    mm(PA[half][0:n, :], oT[:, c, 0:n], w_mob[:, c, half * 512:(half + 1) * 512], c == 0, c == 7,
                       ['oT'], [('PA', half)])
            post_norm_res(1, x2dst, x2name, x2dst, x2name, n)
            pre_norm_T(x2dst, x2name, gffn_t, 'gffn', xnTdst, xnTname, n)

        def part_b(blocks, n_per, out_fn, first_tile, last_tile, conv_init=None):
            nblk = len(blocks)
            ntok = nblk * n_per
            PU = [PS[0], PS[1]]
            PD = [[PS[2 + 2 * b_ + hf] for hf in range(2)] for b_ in range(nblk)]
            for cf in range(NCF):
                sz = 128 if cf < 21 else 64
                s = cf % 2
                for j, off in enumerate((0, DFF)):
                    P.dma('sp', wupst[j][:, :, 0:sz],
                          w_up[:, off + cf * 128:off + cf * 128 + sz].rearrange("(kc p) n -> p kc n", p=128),
                          writes=[('wupst', j)])
                    cp('pool' if j == 0 else 'dve', wupb[s][:, :, j, 0:sz], wupst[j][:, :, 0:sz], [('wupst', j)],
                       [('wupb', s, j)])
                P.dma('sp', wdst[s][0:sz, :], w_down[cf * 128:cf * 128 + sz, :], writes=['wdst'])
                cp('pool', wdb[s][0:sz, :], wdst[s][0:sz, :], ['wdst'], [('wdb', s)])
                for j, (ux, car, cw_, c1, nm) in enumerate(((uxa, cara, cwa_t, c1a, 'a'), (uxg, carg, cwg_t, c1g, 'g'))):
                    for kc in range(8):
                        mm(PU[j][0:sz, 0:ntok], wupb[s][:, kc, j, 0:sz], xnT[:, kc, 0:ntok], kc == 0, kc == 7,
                           [('wupb', s, j), 'xnT'], [('PU', j)])
                    if conv_init is None:
                        cp('pool', ux[0:sz, 0:2], car[0:sz, cf, :], ['car' + nm], ['ux' + nm])
                        cp('act', ux[0:sz, 2:2 + ntok], PU[j][0:sz, 0:ntok], [('PU', j)], ['ux' + nm])
                        if first_tile:
                            ts('pool', ux[0:sz, 2 + 126:2 + 128], ux[0:sz, 2 + 126:2 + 128], hflag[0:sz, :], None, ALU.mult,
                               None, ['ux' + nm, 'hflag'], ['ux' + nm])
                        cp('pool', car[0:sz, cf, :], ux[0:sz, ntok:ntok + 2], ['ux' + nm], ['car' + nm])
                        eng = 'dve' if j == 0 else 'pool'
                        ts(eng, c1[0:sz, 0:ntok], ux[0:sz, 2:2 + ntok], cw_[0:sz, cf, 2:3], cw_[0:sz, cf, 3:4], ALU.mult,
                           ALU.add, ['ux' + nm, 'cw' + nm], ['c1' + nm])
                        stt('dve', c1[0:sz, 0:ntok], ux[0:sz, 1:1 + ntok], cw_[0:sz, cf, 1:2], c1[0:sz, 0:ntok], ALU.mult,
                            ALU.add, ['ux' + nm, 'cw' + nm, 'c1' + nm], ['c1' + nm])
                        stt('dve', c1[0:sz, 0:ntok], ux[0:sz, 0:ntok], cw_[0:sz, cf, 0:1], c1[0:sz, 0:ntok], ALU.mult,
                            ALU.add, ['ux' + nm, 'cw' + nm, 'c1' + nm], ['c1' + nm])
                    else:
                        conv_init(j, cf, sz, PU[j], ux, c1, cw_, nm)
                act(sgt[0:sz, 0:ntok], c1g[0:sz, 0:ntok], AF.Silu, ['c1g'], ['sgt'])
                tt('dve', actT[s][0:sz, 0:ntok], c1a[0:sz, 0:ntok], sgt[0:sz, 0:ntok], ALU.mult, ['c1a', 'sgt'],
                   [('actT', s)])
                for b_ in range(nblk):
                    for hf in range(2):
                        mm(PD[b_][hf][0:n_per, :], actT[s][0:sz, b_ * n_per:(b_ + 1) * n_per],
                           wdb[s][0:sz, hf * 512:(hf + 1) * 512], cf == 0, cf == NCF - 1,
                           [('actT', s), ('wdb', s)], [('PD', b_, hf)])
            for b_, slot in enumerate(blocks):
                if slot is None:
                    continue
                r = sumsq_rstd([PD[b_][0][0:n_per, :], PD[b_][1][0:n_per, :]], [('PD', b_, 0), ('PD', b_, 1)], n_per, D, 0)
                ys = yst[b_ % 2]
                for hf in range(2):
                    sl = slice(hf * 512, (hf + 1) * 512)
                    stt('dve', tmpf[0:n_per, sl], PD[b_][hf][0:n_per, :], r, gpost_t[0:n_per, 2, sl], ALU.mult, ALU.mult,
                        [('PD', b_, hf), 'ss2r', 'gpost'], [('tmpf', hf)])
                    tt('pool', ys[0:n_per, sl], tmpf[0:n_per, sl], x2buf[0:n_per, slot, sl], ALU.add,
                       [('tmpf', hf), ('x2', slot)], ['yst'])
                out_fn(b_, slot, ys)

        own_tiles = [[0, 1, 2]] + [[w_, w_ + 1] for w_ in range(3, 17, 2)]
        PGRP = [(0, 128, mkT, mvb, ['mkT', 'mvb'])]
        for ti, tile_ in enumerate(own_tiles):
            for j, wb in enumerate(tile_):
                part_a(mixT[:, :, wb * 128:(wb + 1) * 128], ('mixT', wb), xall[(W0 + wb) * 128:(W0 + wb + 1) * 128, :], 128,
                       x2buf[:, j, :], ('x2', j), xnT[:, :, j * 128:(j + 1) * 128], 'xnT', PGRP)
            P.barrier()

            def out_p(b_, slot, ys, tile_=tile_):
                ob = tile_[b_] - 1
                P.dma('sp', y_p[ob * 128:(ob + 1) * 128, :], ys, reads=['yst'])
            blocks = [None if wb == 0 else j for j, wb in enumerate(tile_)]
            part_b(blocks, 128, out_p, ti == 0, ti == len(own_tiles) - 1)
            P.barrier()
        for j, (car, off) in enumerate(((cara, 0), (carg, DFF))):
            for g0 in range(0, NCF, 4):
                ncf = min(4, NCF - g0)
                wid = 0
                for cf in range(g0, g0 + ncf):
                    sz = 128 if cf < 21 else 64
                    mm(PS[0][0:2, wid:wid + sz], car[0:sz, cf, :], identf[0:sz, 0:sz], True, True, ['cara', 'carg', 'identf'],
                       ['PCV'])
                    wid += sz
                cp('dve', pcst[:, 0:wid], PS[0][0:2, 0:wid], ['PCV'], ['pcst'])
                P.dma('sp', p_cv[:, off + g0 * 128:off + g0 * 128 + wid], pcst[:, 0:wid], reads=['pcst'])
        P.barrier()

        if 'sp2' not in SKIP:
            _mk = x2buf[:, 1, :].bitcast(BF16).rearrange("p (a b) -> p a b", a=8)
            mkTs = [_mk, _mk]
            _mv = arena[:, o_c1a:o_c1a + 1024].bitcast(BF16).rearrange("p (a b) -> p a b", a=2)
            mvbs = [_mv, _mv]
            cms = [yst[0], wdst[0]]
            cmb = x2buf[:, 2, :].bitcast(BF16)[:, 0:D]
            scst = ar("scst", [32, 128])
            scT = ar("scT", [128, 32])
            scout = arena[0:32, o_pcst:o_pcst + 512]
            lctr = [0]

            def sample_mem():
                for b_ in range(16):
                    s2 = 0
                    for mb in range(2):
                        i = lctr[0] % 2
                        lctr[0] += 1
                        P.dma('sp', cms[i], cmk[b_, mb * 128:(mb + 1) * 128, :], writes=[('cms', i)])
                        cp('pool', cmb, cms[i], [('cms', i)], ['cmb'])
                        pT = PT2[:].bitcast(BF16)
                        for c in range(8):
                            tr(pT[:, c * 128:(c + 1) * 128], cmb[:, c * 128:(c + 1) * 128], identb[:], ['cmb', 'identb'], ['PT2'])
                        cp('act', mkTs[s2][:, :, mb * 128:(mb + 1) * 128], pT[:, :].rearrange("p (a b) -> p a b", a=8),
                           ['PT2'], [('mkTs', s2)])
                        i = lctr[0] % 2
                        lctr[0] += 1
                        P.dma('sp', cms[i], cmv[b_, mb * 128:(mb + 1) * 128, :], writes=[('cms', i)])
                        cp('pool', mvbs[s2][:, mb, :], cms[i], [('cms', i)], [('mvbs', s2)])
                    mem_attend(64, None, None, [(b_ * 4, b_ * 4 + 4, mkTs[s2], mvbs[s2], [('mkTs', s2), ('mvbs', s2)])])
            part_a(mixTs[:, :, :], 'mixTs', xsm, 64, x2buf[0:64, 0, :], ('x2', 0), xnT[:, :, 0:64], 'xnT', sample_mem)
            P.barrier()
            stcv2 = stcv.rearrange("b r f -> (b r) f")
            scv2 = s_cv.rearrange("b r f -> (b r) f")
            wacc = [0]

            def sample_conv(j, cf, sz, PUj, ux, c1, cw_, nm):
                off = j * DFF + cf * 128
                P.dma('sp', scst[:, 0:sz], stcv2[:, off:off + sz], writes=['scst'])
                mm(PS[7][0:sz, 0:32], scst[:, 0:sz], identf[0:32, 0:32], True, True, ['scst', 'identf'], ['PS7'])
                ux6 = ux[0:sz, 0:96].rearrange("p (b t) -> p b t", t=6)
                cp('act', ux6[:, :, 0:2], PS[7][0:sz, 0:32].rearrange("p (b t) -> p b t", t=2), ['PS7'], ['ux' + nm])
                cp('act', ux6[:, :, 2:6], PUj[0:sz, 0:64].rearrange("p (b t) -> p b t", t=4), [('PU', j)], ['ux' + nm])
                c1v = c1[0:sz, 0:64].rearrange("p (b t) -> p b t", t=4)
                ts('dve' if j == 0 else 'pool', c1v, ux6[:, :, 2:6], cw_[0:sz, cf, 2:3], cw_[0:sz, cf, 3:4], ALU.mult, ALU.add,
                   ['ux' + nm, 'cw' + nm], ['c1' + nm])
                stt('dve', c1v, ux6[:, :, 1:5], cw_[0:sz, cf, 1:2], c1v, ALU.mult, ALU.add, ['ux' + nm, 'cw' + nm, 'c1' + nm],
                    ['c1' + nm])
                stt('dve', c1v, ux6[:, :, 0:4], cw_[0:sz, cf, 0:1], c1v, ALU.mult, ALU.add, ['ux' + nm, 'cw' + nm, 'c1' + nm],
                    ['c1' + nm])
                cp('pool', scT[0:sz, :].rearrange("p (b t) -> p b t", t=2), ux6[:, :, 4:6], ['ux' + nm], ['scT'])
                g0 = (cf // 4) * 4
                col = (cf - g0) * 128
                psj = PS[6] if j == 0 else PS[5]
                mm(psj[0:32, col:col + sz], scT[0:sz, :], identf[0:sz, 0:sz], True, True, ['scT', 'identf'], [('PS6', j)])
                if cf % 4 == 3 or cf == NCF - 1:
                    wid = col + sz
                    cp('act', scout[:, 0:wid], psj[0:32, 0:wid], [('PS6', j)], ['scout'])
                    P.dma('sp', scv2[:, j * DFF + g0 * 128:j * DFF + g0 * 128 + wid], scout[:, 0:wid], reads=['scout'])

            def out_s(b_, slot, ys):
                P.dma('sp', y_s, ys[0:64, :], reads=['yst'])
            part_b([0], 64, out_s, False, False, conv_init=sample_conv)
            P.barrier()
        P.emit()
    return nc


def _consts(half):
    c = {}
    c["ident"] = np.eye(128, dtype=np.float32)
    k = np.arange(128)
    c["tril"] = (k[:, None] <= k[None, :]).astype(np.float32)
    halfd = 8
    inv = np.exp(-math.log(500000.0) * (2.0 * np.arange(halfd, dtype=np.float32) / 16)).astype(np.float32)
    pos = np.arange(4096, dtype=np.float32) - (2048.0 if half == 0 else 0.0)
    ang = (pos[:, None] * inv[None, :]).astype(np.float32)
    c["ropeC"] = np.ascontiguousarray(np.tile(np.cos(ang).astype(np.float32), (1, 8)))
    c["ropeS"] = np.ascontiguousarray(np.tile(np.sin(ang).astype(np.float32), (1, 8)))
    poss = (2048.0 + np.arange(4, dtype=np.float32))
    angs = (poss[:, None] * inv[None, :]).astype(np.float32)
    c["ropeCs"] = np.ascontiguousarray(np.tile(np.cos(angs).astype(np.float32), (16, 8)))
    c["ropeSs"] = np.ascontiguousarray(np.tile(np.sin(angs).astype(np.float32), (16, 8)))
    kb = np.zeros((128, NB), np.float32)
    gv = np.ones((4, NB, 128), np.float32)
    if half == 0:
        kb[:, :16] = -BIG
        gv[:, :16, :] = 0.0
    gv = gv.reshape(128, 128)
    hk = np.arange(128)
    c["lt"] = ((hk[:, None] // NB == hk[None, :] // NB) & (hk[:, None] < hk[None, :])).astype(np.float32)
    c["kbias"] = kb
    c["gvalid"] = gv
    c["hflag"] = np.full((128, 1), float(half), np.float32)
    c["iota"] = np.arange(128, dtype=np.float32).reshape(128, 1)
    t = np.arange(64)
    c["bm"] = ((t[:, None] // 4 == t[None, :] // 4) & (t[:, None] <= t[None, :])).astype(np.float32)
    c["bmask"] = (t[:, None] // 4 == np.arange(16)[None, :]).astype(np.float32)
    c["bmT"] = np.ascontiguousarray(c["bmask"].T)
    c["bmn"] = np.where(t[:, None] // 4 == t[None, :] // 4, 0.0, -BIG).astype(np.float32)
    sel = np.zeros((4, 4, 128), np.float32)
    for h in range(4):
        sel[h, h, :] = 1.0
    c["sel"] = sel
    return c


def make_in_maps(inp, pool_k=None, pool_v=None, ptab=None):
    f = lambda a: np.ascontiguousarray(np.asarray(a, dtype=np.float32))
    xp = f(inp["x_prompt"])
    xs = f(inp["x_sample"])
    ck = f(inp["cache_dk"])[0].reshape(-1, 512) if pool_k is None else pool_k
    cv = f(inp["cache_dv"])[0].reshape(-1, 512) if pool_v is None else pool_v
    pt = np.asarray(inp["page_table"], dtype=np.int32) if ptab is None else ptab
    gp = np.stack([f(inp[n])[0].reshape(8, 128).T for n in ("g_mix_pre", "g_mem_pre", "g_mem_src", "g_ffn_pre")], 1)
    gpost = np.stack([f(inp[n])[0] for n in ("g_mix_post", "g_mem_post", "g_ffn_post")], 0)
    ghead = np.stack([f(inp["g_ml_head"])[0].reshape(512), f(inp["g_da_head"])[0].reshape(512)], 0)
    b = f(inp["b_if"])[0]
    bif = np.stack([b[:4], b[4:]], 1)
    wdw = f(inp["w_dw"])[0]
    bdw = f(inp["b_dw"])[0]

    def cw(off):
        o = np.zeros((128, NCF, 4), np.float32)
        for cf in range(NCF):
            sz = 128 if cf < 21 else 64
            o[:sz, cf, 0:3] = wdw[:, off + cf * 128: off + cf * 128 + sz].T
            o[:sz, cf, 3] = bdw[off + cf * 128: off + cf * 128 + sz]
        return o
    shared = {
        "w_in": f(inp["w_in"])[0], "w_out": f(inp["w_out"])[0], "w_mq": f(inp["w_mq"])[0],
        "w_mk": f(inp["w_mk"])[0], "w_mv": f(inp["w_mv"])[0], "w_mo": f(inp["w_mo"])[0],
        "w_up": f(inp["w_up"])[0], "w_down": f(inp["w_down"])[0],
        "gpre": np.ascontiguousarray(gp), "gpost": np.ascontiguousarray(gpost), "ghead": np.ascontiguousarray(ghead),
        "gffn": np.ascontiguousarray(f(inp["g_ffn_pre"])[0].reshape(1, D)),
        "bif": np.ascontiguousarray(bif), "bifr": np.ascontiguousarray(b.reshape(1, 8)), "dlam": f(inp["da_lambda"])[0].reshape(1, 256),
        "cwa": cw(0), "cwg": cw(DFF), "ckp": ck, "cvp": cv,
    }
    cst = [_consts(0), _consts(1)]
    maps = []
    for c in range(8):
        bb, half = c // 2, c % 2
        m = dict(shared)
        m.update(cst[half])
        if half == 1:
            m["xall"] = np.ascontiguousarray(xp[bb])
        else:
            xa = np.zeros((4096, D), np.float32)
            xa[2048:] = xp[bb, :2048]
            m["xall"] = xa
        sl = slice(c * 16, (c + 1) * 16)
        m["xsm"] = np.ascontiguousarray(xs[sl].reshape(64, D))
        m["memp"] = np.ascontiguousarray(f(inp["mem_prompt"])[bb])
        m["ptab"] = np.ascontiguousarray(pt[sl].reshape(1, 256))
        m["cmk"] = np.ascontiguousarray(f(inp["cache_mem_k"])[0, sl].reshape(16, 256, D))
        m["cmv"] = np.ascontiguousarray(f(inp["cache_mem_v"])[0, sl].reshape(16, 256, D))
        m["stC"] = np.ascontiguousarray(f(inp["state_ml_C"])[0, sl])
        m["stn"] = np.ascontiguousarray(f(inp["state_ml_n"])[0, sl])
        m["stm"] = np.ascontiguousarray(f(inp["state_ml_m"])[0, sl])
        m["stcv"] = np.ascontiguousarray(f(inp["state_conv"])[0, sl])
        maps.append(m)
    return maps


def assemble(res):
    r = res
    yp = np.zeros((4, 4096, D), np.float32)
    pdk = np.zeros((1, 4, 4096, 4, 128), np.float32)
    pdv = np.zeros((1, 4, 4096, 4, 128), np.float32)
    pmk = np.zeros((1, 4, 256, 4, 256), np.float32)
    pmv = np.zeros((1, 4, 256, 4, 256), np.float32)
    pC = np.zeros((1, 4, 4, 128, 128), np.float32)
    pn = np.zeros((1, 4, 4, 128), np.float32)
    pm = np.zeros((1, 4, 4), np.float32)
    pcv = np.zeros((1, 4, 2, 2 * DFF), np.float32)
    ys = np.zeros((128, 4, D), np.float32)
    sdk = np.zeros((1, 128, 4, 4, 128), np.float32)
    sdv = np.zeros((1, 128, 4, 4, 128), np.float32)
    sC = np.zeros((1, 128, 4, 128, 128), np.float32)
    sn = np.zeros((1, 128, 4, 128), np.float32)
    sm = np.zeros((1, 128, 4), np.float32)
    scv = np.zeros((1, 128, 2, 2 * DFF), np.float32)
    for c in range(8):
        bb, half = c // 2, c % 2
        o = r[c]
        ts_ = slice(half * 2048, (half + 1) * 2048)
        yp[bb, ts_] = o["y_p"]
        pdk[0, bb, ts_] = o["p_dk"].reshape(2048, 4, 128)
        pdv[0, bb, ts_] = o["p_dv"].reshape(2048, 4, 128)
        if half == 1:
            pmk[0, bb] = o["p_mk"].reshape(256, 4, 256)
            pmv[0, bb] = o["p_mv"].reshape(256, 4, 256)
            pC[0, bb] = o["p_C"]
            pn[0, bb] = o["p_n"]
            pm[0, bb] = o["p_m"].reshape(4)
            pcv[0, bb] = o["p_cv"]
        sl = slice(c * 16, (c + 1) * 16)
        ys[sl] = o["y_s"].reshape(16, 4, D)
        sdk[0, sl] = o["s_dk"].reshape(16, 4, 4, 128)
        sdv[0, sl] = o["s_dv"].reshape(16, 4, 4, 128)
        sC[0, sl] = o["s_C"]
        sn[0, sl] = o["s_n"]
        sm[0, sl] = o["s_m"]
        scv[0, sl] = o["s_cv"]
    return (yp, ys, pdk, pdv, pmk, pmv, pC, pn, pm, pcv, sdk, sdv, sC, sn, sm, scv)


STAGE = 99


def kernel(**inputs):
    ck = np.asarray(inputs["cache_dk"], dtype=np.float32)
    cv = np.asarray(inputs["cache_dv"], dtype=np.float32)
    n_pool = int(ck.shape[1])
    pk = np.ascontiguousarray(ck[0].reshape(-1, 512))
    pv = np.ascontiguousarray(cv[0].reshape(-1, 512))
    nc = build(n_pool, stage=STAGE)
    maps = make_in_maps(inputs, pool_k=pk, pool_v=pv)
    res = run_bass_kernel_spmd(nc, maps, core_ids=list(range(8)))
    return assemble(res.results)
```

```python
import math
import numpy as np
from contextlib import ExitStack
import concourse.bass as bass
import concourse.mybir as mybir
from concourse.bass_utils import run_bass_kernel_spmd

F32 = mybir.dt.float32
BF16 = mybir.dt.bfloat16
I32 = mybir.dt.int32
AF = mybir.ActivationFunctionType
ALU = mybir.AluOpType
AX = mybir.AxisListType

D = 1024
NB = 32
W0 = 15
IN_COLS = 3592
DFF = 2752
NCF = 22
BIG = 30000.0
LNC = math.log(128.0 ** -0.5)
LAM_INIT = 0.2
EPS = 1e-6
SKIP = set()


class Prog:
    CENG = ['pe', 'act', 'dve', 'pool']
    ENG = ['pe', 'act', 'dve', 'pool', 'sp']
    DQ = ['sp', 'act', 'pool']
    NDS = 8

    def __init__(self, nc, es):
        self.nc = nc
        self.ins = {e: [] for e in self.ENG}
        self.NEP = {'pe': 32, 'act': 18, 'dve': 18, 'pool': 8}
        self.EPOCH = 1000
        self.csem = {e: [es.enter_context(nc.semaphore("c_%s_%d" % (e, k))) for k in range(self.NEP[e])]
                     for e in self.CENG}
        self.ccount = {e: 0 for e in self.CENG}
        self.dsems = {q: [es.enter_context(nc.semaphore("d_%s_%d" % (q, i))) for i in range(self.NDS)]
                      for q in self.DQ}
        self.dcount = {q: [0] * self.NDS for q in self.DQ}
        self.dpos = {q: 0 for q in self.DQ}
        self.semobj = {}
        for e in self.CENG:
            for k in range(self.NEP[e]):
                self.semobj[('c', e, k)] = self.csem[e][k]
        for q in self.dsems:
            for i, s in enumerate(self.dsems[q]):
                self.semobj[('d', q, i)] = s
        self.last_w = {}
        self.readers = {}
        self.seen = {e: {} for e in self.ENG}

    def _deps(self, eng, reads, writes, ev, is_pe):
        deps = {}

        def add(e):
            if e is None:
                return
            key, val = e
            if is_pe and key[0] == 'c' and key[1] == 'pe':
                return
            if deps.get(key, 0) < val:
                deps[key] = val
        for r in reads:
            add(self.last_w.get(r))
        for w in writes:
            add(self.last_w.get(w))
            for e in self.readers.get(w, ()):
                add(e)
        for r in reads:
            self.readers.setdefault(r, []).append(ev)
        for w in writes:
            self.last_w[w] = ev
            self.readers[w] = []
        waits = []
        seen = self.seen[eng]
        for key, val in deps.items():
            if seen.get(key, 0) >= val:
                continue
            seen[key] = val
            waits.append((key, val))
        return waits

    def op(self, eng, fn, reads=(), writes=()):
        ep = self.ccount[eng] // self.EPOCH
        assert ep < self.NEP[eng], eng
        ev = (('c', eng, ep), self.ccount[eng] % self.EPOCH + 1)
        self.ccount[eng] += 1
        waits = self._deps(eng, reads, writes, ev, eng == 'pe')
        self.ins[eng].append((fn, waits, ev, 1))
        return ev

    def dmaf(self, q, fn, reads=(), writes=()):
        i = self.dpos[q]
        self.dpos[q] = (i + 1) % self.NDS
        key = ('d', q, i)
        prev = self.dcount[q][i]
        self.dcount[q][i] += 16
        ev = (key, self.dcount[q][i])
        waits = self._deps(q, reads, writes, ev, False)
        seen = self.seen[q]
        if prev > 0 and seen.get(key, 0) < prev:
            seen[key] = prev
            waits.append((key, prev))
        self.ins[q].append((fn, waits, ev, 16))
        return ev

    def dma(self, q, out, in_, reads=(), writes=(), **kw):
        return self.dmaf(q, (lambda e: e.dma_start(out=out, in_=in_, **kw)), reads, writes)

    def barrier(self):
        cur = {}
        for e in self.CENG:
            n = self.ccount[e]
            if n:
                ep = (n - 1) // self.EPOCH
                cur[('c', e, ep)] = (n - 1) % self.EPOCH + 1
                if ep > 0:
                    cur[('c', e, ep - 1)] = self.EPOCH
        for q in self.DQ:
            for i in range(self.NDS):
                if self.dcount[q][i]:
                    cur[('d', q, i)] = self.dcount[q][i]
        for e in self.ENG:
            waits = []
            for key, val in cur.items():
                if key[0] == 'c' and key[1] == 'pe' and e == 'pe':
                    continue
                if self.seen[e].get(key, 0) < val:
                    self.seen[e][key] = val
                    waits.append((key, val))
            if waits:
                self.ins[e].append((None, waits, None, 0))
        self.last_w = {}
        self.readers = {}

    def emit(self):
        nc = self.nc
        self.barrier()

        def run(engname, engobj):
            for fn, waits, ev, inc in self.ins[engname]:
                for key, val in waits:
                    engobj.wait_ge(self.semobj[key], val)
                if fn is None:
                    continue
                inst = fn(engobj)
                inst.then_inc(self.semobj[ev[0]], inc)

        with nc.Block() as block:
            @block.tensor
            def _(e):
                run('pe', e)

            @block.scalar
            def _(e):
                run('act', e)

            @block.vector
            def _(e):
                run('dve', e)

            @block.gpsimd
            def _(e):
                run('pool', e)

            @block.sync
            def _(e):
                run('sp', e)


def build(n_pool, stage=99):
    nc = bass.Bass("TRN2", target_bir_lowering=False)

    def din(name, shape, dt=F32):
        return nc.dram_tensor(name, list(shape), dt, kind="ExternalInput").ap()

    def dout(name, shape, dt=F32):
        return nc.dram_tensor(name, list(shape), dt, kind="ExternalOutput").ap()

    xall = din("xall", [4096, D])
    xsm = din("xsm", [64, D])
    memp = din("memp", [256, D])
    w_in = din("w_in", [D, IN_COLS])
    w_out = din("w_out", [D, D])
    w_mq = din("w_mq", [D, D])
    w_mk = din("w_mk", [D, D])
    w_mv = din("w_mv", [D, D])
    w_mo = din("w_mo", [D, D])
    w_up = din("w_up", [D, 2 * DFF])
    w_down = din("w_down", [DFF, D])
    gpre = din("gpre", [128, 4, 8])
    gpost = din("gpost", [3, D])
    ghead = din("ghead", [2, 512])
    bif = din("bif", [4, 2])
    dlam = din("dlam", [1, 256])
    gffn_d = din("gffn", [1, D])
    cwa = din("cwa", [128, NCF, 4])
    cwg = din("cwg", [128, NCF, 4])
    ckp = din("ckp", [n_pool * 128, 512])
    cvp = din("cvp", [n_pool * 128, 512])
    ptab = din("ptab", [1, 256], I32)
    cmk = din("cmk", [16, 256, D])
    cmv = din("cmv", [16, 256, D])
    stC = din("stC", [16, 4, 128, 128])
    stn = din("stn", [16, 4, 128])
    stm = din("stm", [16, 4])
    stcv = din("stcv", [16, 2, 2 * DFF])
    ident_d = din("ident", [128, 128])
    tril_d = din("tril", [128, 128])
    ropeC_d = din("ropeC", [4096, 64])
    ropeS_d = din("ropeS", [4096, 64])
    ropeCs_d = din("ropeCs", [64, 64])
    ropeSs_d = din("ropeSs", [64, 64])
    kbias_d = din("kbias", [128, NB])
    gvalid_d = din("gvalid", [128, 128])
    lt_d = din("lt", [128, 128])
    hflag_d = din("hflag", [128, 1])
    iota_d = din("iota", [128, 1])
    bm_d = din("bm", [64, 64])
    bmask_d = din("bmask", [64, 16])
    sel_d = din("sel", [4, 4, 128])
    bmn_d = din("bmn", [64, 64])
    bmT_d = din("bmT", [16, 64])
    bifr_d = din("bifr", [1, 8])

    y_p = dout("y_p", [2048, D])
    p_dk = dout("p_dk", [2048, 512])
    p_dv = dout("p_dv", [2048, 512])
    p_mk = dout("p_mk", [256, D])
    p_mv = dout("p_mv", [256, D])
    p_C = dout("p_C", [4, 128, 128])
    p_n = dout("p_n", [4, 128])
    p_m = dout("p_m", [4, 1])
    p_cv = dout("p_cv", [2, 2 * DFF])
    y_s = dout("y_s", [64, D])
    s_dk = dout("s_dk", [64, 512])
    s_dv = dout("s_dv", [64, 512])
    s_C = dout("s_C", [16, 4, 128, 128])
    s_n = dout("s_n", [16, 4, 128])
    s_m = dout("s_m", [16, 4])
    s_cv = dout("s_cv", [16, 2, 2 * DFF])

    with ExitStack() as es:
        P = Prog(nc, es)

        def sb(name, shape, dt=F32):
            return es.enter_context(nc.sbuf_tensor("sb_" + name, list(shape), dt))

        PS = [es.enter_context(nc.psum_tensor("ps%d" % i, [128, 512], F32)) for i in range(8)]
        ARENA_W = 39300
        arena = sb("arena", [128, ARENA_W])
        apos = [0]
        amax = [0]

        def ar(name, shape, dt=F32):
            n = 1
            for d_ in shape[1:]:
                n *= d_
            words = n if dt in (F32, I32) else (n + 1) // 2
            o = apos[0]
            apos[0] += words
            amax[0] = max(amax[0], apos[0])
            assert apos[0] <= ARENA_W, (name, apos[0])
            v = arena[0:shape[0], o:o + words]
            if dt != F32:
                v = v.bitcast(dt)[:, 0:n]
            if len(shape) == 3:
                v = v.rearrange("p (a b) -> p a b", a=shape[1])
            elif len(shape) == 4:
                v = v.rearrange("p (a b c) -> p a b c", a=shape[1], b=shape[2])
            return v

        def act(out, in_, func, reads, writes, **kw):
            return P.op('act', lambda e: e.activation(out=out, in_=in_, func=func, **kw), reads, writes)

        def mm(out, lhsT, rhs, start, stop, reads, writes):
            return P.op('pe', lambda e: e.matmul(out, lhsT=lhsT, rhs=rhs, start=start, stop=stop), reads, writes)

        def tr(out, in_, idn, reads, writes):
            return P.op('pe', lambda e: e.transpose(out=out, in_=in_, identity=idn), reads, writes)

        def cp(eng, out, in_, reads, writes):
            if eng == 'act':
                return act(out, in_, AF.Copy, reads, writes)
            return P.op(eng, lambda e: e.tensor_copy(out=out, in_=in_), reads, writes)

        def ts(eng, out, in0, s1, s2, op0, op1, reads, writes):
            if op1 is None:
                return P.op(eng, lambda e: e.tensor_scalar(out=out, in0=in0, scalar1=s1, scalar2=None, op0=op0),
                            reads, writes)
            return P.op(eng, lambda e: e.tensor_scalar(out=out, in0=in0, scalar1=s1, scalar2=s2, op0=op0, op1=op1),
                        reads, writes)

        def tt(eng, out, in0, in1, op, reads, writes):
            return P.op(eng, lambda e: e.tensor_tensor(out=out, in0=in0, in1=in1, op=op), reads, writes)

        def stt(eng, out, in0, scalar, in1, op0, op1, reads, writes):
            return P.op(eng, lambda e: e.scalar_tensor_tensor(out=out, in0=in0, scalar=scalar, in1=in1,
                                                              op0=op0, op1=op1), reads, writes)

        def ms(eng, ap, val, writes):
            return P.op(eng, lambda e: e.memset(ap, val), (), writes)

        identf = sb("identf", [128, 128])
        identb = sb("identb", [128, 128], BF16)
        trilf = sb("trilf", [128, 128])
        trilb = sb("trilb", [128, 128], BF16)
        tril2b = sb("tril2b", [128, 256], BF16)
        epsc = sb("epsc", [128, 1])
        onec = sb("onec", [128, 1])
        lncc = sb("lncc", [128, 1])
        kbias = sb("kbias", [128, NB])
        hflag = sb("hflag", [128, 1])
        gpre_t = sb("gpre_t", [128, 4, 8])
        ghead_t = sb("ghead_t", [128, 2, 512])
        bif_t = sb("bif_t", [4, 2])
        nbf = sb("nbf", [4, 1])
        lam_t = sb("lam_t", [128, 1])
        dl_t = sb("dl_t", [128, 256])
        dl_p = sb("dl_p", [128, 128])
        dl_s = sb("dl_s", [128, 2])
        sel_t = sb("sel_t", [4, 4, 128])
        P.dma('sp', identf[:], ident_d, writes=['identf'])
        P.dma('sp', trilf[:], tril_d, writes=['trilf'])
        P.dma('sp', kbias[:], kbias_d, writes=['kbias'])
        P.dma('sp', hflag[:], hflag_d, writes=['hflag'])
        P.dma('sp', gpre_t[:], gpre, writes=['gpre'])
        P.dma('sp', bif_t[:], bif, writes=['bif'])
        P.dma('sp', sel_t[:], sel_d.rearrange("k h m -> k h m"), writes=['sel'])
        for i in range(2):
            P.dma('sp', ghead_t[:, i, :], ghead[i:i + 1, :].partition_broadcast(128), writes=['ghead'])
        P.dma('sp', dl_t[:], dlam.partition_broadcast(128), writes=['dl_t'])
        cp('dve', identb[:], identf[:], ['identf'], ['identb'])
        cp('dve', trilb[:], trilf[:], ['trilf'], ['trilb'])
        cp('dve', tril2b[:, 0:128], trilf[:], ['trilf'], ['tril2b'])
        cp('dve', tril2b[:, 128:256], trilf[:], ['trilf'], ['tril2b'])
        ms('pool', epsc[:], EPS, ['epsc'])
        ms('pool', onec[:], 1.0, ['onec'])
        ms('pool', lncc[:], LNC, ['lncc'])
        ts('dve', nbf[:], bif_t[:, 1:2], -1.0, None, ALU.mult, None, ['bif'], ['nbf'])
        dl3 = dl_t[:].rearrange("p (a b) -> p a b", a=4)
        tt('dve', dl_p[:, 0:64], dl3[:, 0, :], dl3[:, 1, :], ALU.mult, ['dl_t'], ['dl_p'])
        tt('dve', dl_p[:, 64:128], dl3[:, 2, :], dl3[:, 3, :], ALU.mult, ['dl_t'], ['dl_p'])
        P.op('dve', lambda e: e.tensor_reduce(out=dl_s[:], in_=dl_p[:].rearrange("p (a b) -> p a b", a=2),
                                              axis=AX.X, op=ALU.add), ['dl_p'], ['dl_s'])
        act(dl_s[:], dl_s[:], AF.Exp, ['dl_s'], ['dl_s'])
        tt('dve', lam_t[:], dl_s[:, 0:1], dl_s[:, 1:2], ALU.subtract, ['dl_s'], ['lam'])
        ts('dve', lam_t[:], lam_t[:], LAM_INIT, None, ALU.add, None, ['lam'], ['lam'])
        ts('dve', ghead_t[:, 1, :], ghead_t[:, 1, :], 1.0 - LAM_INIT, None, ALU.mult, None, ['ghead'], ['ghead'])

        mixT = sb("mixT", [128, 8, 17 * 128], BF16)
        mixTs = sb("mixTs", [128, 8, 64], BF16)
        Cst = sb("Cst", [128, 4, 132])
        Cbf = sb("Cbf", [128, 4, 136], BF16)
        colsT = sb("colsT", [128, 3, 4, NB])
        decb = sb("decb", [128, 4, NB])
        ssn = sb("ssn", [128, 2])
        dn = sb("dn", [128, 4])
        hss = sb("hss", [128, 2])
        ms('pool', Cst[:], 0.0, ['Cst'])
        ms('pool', Cbf[:], 0.0, ['Cbf'])
        PST = PS[0]
        C_ML = 0
        C_DA = 2056
        NML = 2056
        NDA = 1536

        def load_w(dst, src_cols0, ncols, gidx, nm):
            hw = (ncols + 1) // 2
            st = [ar("%s_st%d" % (nm, i), [128, hw]) for i in range(2)]
            for kc in range(8):
                for s in range(2):
                    c0 = s * hw
                    c1 = min(ncols, c0 + hw)
                    P.dma('sp', st[s][:, 0:c1 - c0], w_in[kc * 128:(kc + 1) * 128, src_cols0 + c0:src_cols0 + c1],
                          writes=[(nm + 'st', s)])
                    if s == 0:
                        act(dst[:, kc, c0:c1], st[s][:, 0:c1 - c0], AF.Copy, [(nm + 'st', s), 'gpre'], [(nm, kc, s)],
                            scale=gpre_t[:, gidx, kc:kc + 1])
                    else:
                        ts('pool', dst[:, kc, c0:c1], st[s][:, 0:c1 - c0], gpre_t[:, gidx, kc:kc + 1], None,
                           ALU.mult, None, [(nm + 'st', s), 'gpre'], [(nm, kc, s)])
            return [(nm, kc, j) for kc in range(8) for j in range(2)]

        def norm_bufs():
            xt = [ar("xt%d" % i, [128, D]) for i in range(2)]
            xsq = ar("xsq", [128, D], BF16)
            xsb = ar("xsb", [128, D], BF16)
            hT = [ar("hT%d" % i, [128, 8, 128], BF16) for i in range(2)]
            return xt, xsq, xsb, hT
        nctr = [0]

        def norm_T(nb, src, n, dst, dstname):
            xt, xsq, xsb, _ = nb
            s = nctr[0] % 2
            nctr[0] += 1
            P.dma('sp', xt[s][0:n, :], src, writes=[('xt', s)])
            act(xsq[0:n, :], xt[s][0:n, :], AF.Square, [('xt', s)], ['xsq', 'ssn'], accum_out=ssn[0:n, 0:1])
            act(ssn[0:n, 1:2], ssn[0:n, 0:1], AF.Ln, ['ssn', 'epsc'], ['ssn'], scale=1.0 / D, bias=epsc[0:n, :])
            act(ssn[0:n, 1:2], ssn[0:n, 1:2], AF.Exp, ['ssn'], ['ssn'], scale=-0.5)
            ts('dve', xsb[0:n, :], xt[s][0:n, :], ssn[0:n, 1:2], None, ALU.mult, None,
               [('xt', s), 'ssn'], ['xsb'])
            pT = PST[:].bitcast(BF16)
            for kc in range(8):
                tr(pT[:, kc * 128:kc * 128 + n], xsb[0:n, kc * 128:(kc + 1) * 128], identb[0:n, 0:n],
                   ['xsb', 'identb'], ['PST'])
            cp('act', dst, pT[:, :].rearrange("p (a b) -> p a b", a=8)[:, :, 0:n], ['PST'], [dstname])

        def rstd_of(ssap, n, width):
            act(ssap[:, 1:2], ssap[:, 0:1], AF.Ln, ['hss', 'epsc'], ['hss'], scale=1.0 / width, bias=epsc[0:n, :])
            act(ssap[:, 1:2], ssap[:, 1:2], AF.Exp, ['hss'], ['hss'], scale=-0.5)

        apos[0] = 0
        wml = ar("wml", [128, 8, NML], BF16)
        WML = load_w(wml, C_ML, NML, 0, 'wml')
        P.barrier()
        apos[0] = NML * 4
        nb = norm_bufs()
        hT = nb[3]
        gsc = nc.dram_tensor("gsc", [2, 4, 4096], F32, kind="Internal").ap()
        gst = [ar("gst%d" % i, [4, 512]) for i in range(2)]
        GI = ar("GI", [128, 128])
        GF = ar("GF", [128, 128])
        gval = ar("gval", [128, 128])
        gneg = ar("gneg", [128, 128])
        gzero = ar("gzero", [128, 128])
        bgt = ar("bgt", [128, 128])
        agt = ar("agt", [128, 128])
        cml = ar("cml", [128, 128])
        e1t = ar("e1t", [128, 128])
        e2t = ar("e2t", [128, 128])
        flt = ar("flt", [128, 128])
        LT = ar("LT", [128, 128])
        decm = ar("decm", [128, 128])
        gcol = ar("gcol", [128, 8])
        grow = ar("grow", [1, 3, 128])
        mfin = ar("mfin", [128, 1])
        P.dma('sp', gval, gvalid_d, writes=['gval'])
        P.dma('sp', LT, lt_d, writes=['LT'])
        ms('pool', gzero, 0.0, ['gzero'])
        ts('dve', gneg, gval, -1.0, BIG, ALU.add, ALU.mult, ['gval'], ['gneg'])
        PSG = PS[1]
        PSG2 = PS[2]
        for kb in range(NB if 'mlpass' not in SKIP else 0):
            s = kb % 2
            norm_T(nb, xall[kb * 128:(kb + 1) * 128, :], 128, hT[s], ('hT', s))
            q = kb % 4
            for kc in range(8):
                mm(PSG[0:4, q * 128:(q + 1) * 128], wml[:, kc, 2048:2052], hT[s][:, kc, :], kc == 0, kc == 7,
                   [('hT', s)] + WML, ['PSG'])
            for kc in range(8):
                mm(PSG2[0:4, q * 128:(q + 1) * 128], wml[:, kc, 2052:2056], hT[s][:, kc, :], kc == 0, kc == 7,
                   [('hT', s)] + WML, ['PSG2'])
            if q == 3:
                t0 = (kb - 3) * 128
                act(gst[0], PSG[0:4, :], AF.Identity, ['PSG', 'bif'], [('gst', 0)], bias=bif_t[:, 0:1])
                act(gst[1], PSG2[0:4, :], AF.Exp, ['PSG2', 'nbf'], [('gst', 1)], scale=-1.0, bias=nbf[:, 0:1])
                P.dma('sp', gsc[0, :, t0:t0 + 512], gst[0], reads=[('gst', 0)], writes=[('gsc', 0, kb)])
                P.dma('sp', gsc[1, :, t0:t0 + 512], gst[1], reads=[('gst', 1)], writes=[('gsc', 1, kb)])
        P.dma('sp', GI, gsc[0].rearrange("h (k t) -> (h k) t", t=128), reads=[('gsc', 0, k_) for k_ in range(3, NB, 4)], writes=['GI'])
        P.dma('sp', GF, gsc[1].rearrange("h (k t) -> (h k) t", t=128), reads=[('gsc', 1, k_) for k_ in range(3, NB, 4)], writes=['GF'])
        act(GF, GF, AF.Ln, ['GF', 'onec'], ['GF'], bias=onec[:, :])
        stt('dve', GF, GF, -1.0, gval, ALU.mult, ALU.mult, ['GF', 'gval'], ['GF'])
        P.op('dve', lambda e: e.tensor_tensor_scan(out=bgt, data0=GF, data1=gzero, initial=0.0,
                                                   op0=ALU.add, op1=ALU.add), ['GF', 'gzero'], ['bgt'])
        cp('dve', gcol[:, 0:1], bgt[:, 127:128], ['bgt'], ['gcol0'])
        mm(PSG[:, 0:1], LT, gcol[:, 0:1], True, True, ['LT', 'gcol0'], ['PSG'])
        cp('dve', gcol[:, 1:2], PSG[:, 0:1], ['PSG'], ['gcol1'])
        ts('dve', bgt, bgt, gcol[:, 1:2], None, ALU.add, None, ['bgt', 'gcol1'], ['bgt'])
        tt('dve', GI, GI, gval, ALU.mult, ['GI', 'gval'], ['GI'])
        tt('dve', GI, GI, gneg, ALU.add, ['GI', 'gneg'], ['GI'])
        tt('dve', agt, GI, bgt, ALU.subtract, ['GI', 'bgt'], ['agt'])
        P.op('dve', lambda e: e.tensor_tensor_scan(out=cml, data0=agt, data1=agt, initial=-1e30,
                                                   op0=ALU.max, op1=ALU.max), ['agt'], ['cml'])
        cp('dve', gcol[:, 2:3], cml[:, 127:128], ['cml'], ['gcol2'])
        mm(PSG[0:1, 128:256], gcol[:, 2:3], identf[:], True, True, ['gcol2', 'identf'], ['PSG'])
        cp('dve', grow[:, 0, :], PSG[0:1, 128:256], ['PSG'], ['grow0'])
        ms('pool', grow[:, 2, :], 0.0, ['grow2'])
        for h in range(4):
            P.op('dve', lambda e, h=h: e.tensor_tensor_scan(
                out=grow[:, 1, h * NB:(h + 1) * NB], data0=grow[:, 0, h * NB:(h + 1) * NB],
                data1=grow[:, 0, h * NB:(h + 1) * NB], initial=0.0, op0=ALU.max, op1=ALU.max),
                ['grow0'], ['grow1'])
            cp('dve', grow[:, 2, h * NB + 1:(h + 1) * NB], grow[:, 1, h * NB:(h + 1) * NB - 1], ['grow1', 'grow2'],
               ['grow2'])
        mm(PSG[:, 256:257], grow[:, 2, :], onec[0:1, 0:1], True, True, ['grow2', 'onec'], ['PSG'])
        mm(PSG[:, 257:258], grow[:, 1, :], onec[0:1, 0:1], True, True, ['grow1', 'onec'], ['PSG'])
        cp('dve', gcol[:, 3:5], PSG[:, 256:258], ['PSG'], ['gcol34'])
        ts('dve', gcol[:, 5:6], gcol[:, 3:4], -1.0, LNC, ALU.mult, ALU.add, ['gcol34'], ['gcol5'])
        ts('dve', gcol[:, 6:7], gcol[:, 4:5], -1.0, LNC, ALU.mult, ALU.add, ['gcol34'], ['gcol6'])
        ts('dve', gcol[:, 7:8], gcol[:, 3:4], -1.0, None, ALU.mult, None, ['gcol34'], ['gcol7'])
        act(e1t, agt, AF.Exp, ['agt', 'gcol5'], ['e1t'], bias=gcol[:, 5:6])
        act(e2t, agt, AF.Exp, ['agt', 'gcol6'], ['e2t'], bias=gcol[:, 6:7])
        act(flt, bgt, AF.Exp, ['bgt', 'gcol7'], ['flt'], scale=-1.0, bias=gcol[:, 7:8])
        tt('dve', mfin, bgt[:, 127:128], gcol[:, 4:5], ALU.add, ['bgt', 'gcol34'], ['mfin'])
        for h in range(4):
            P.dma('sp', p_m[h:h + 1, :], mfin[h * NB + NB - 1:h * NB + NB, :], reads=['mfin'])
        tt('dve', gcol[:, 0:1], gcol[:, 3:4], gcol[:, 4:5], ALU.subtract, ['gcol34', 'gcol0'], ['gcol0'])
        act(gcol[:, 0:1], gcol[:, 0:1], AF.Exp, ['gcol0'], ['gcol0'])
        ts('dve', decm, identf[:], 0.0, gcol[:, 0:1], ALU.mult, ALU.add, ['identf', 'gcol0'], ['decm'])
        mm(PSG2[:, 0:128], decm, identf[:], True, True, ['decm', 'identf'], ['PSG2'])
        cp('dve', decb[:].rearrange("p a b -> p (a b)"), PSG2[:, 0:128], ['PSG2'], ['decb'])
        PSC = PS[3]
        for j, r in enumerate((e1t, e2t, flt)):
            mm(PSC[:, j * 128:(j + 1) * 128], r, identf[:], True, True, ['e1t', 'e2t', 'flt', 'identf'], ['PSC'])
        cp('dve', colsT[:].rearrange("p a b c -> p (a b c)"), PSC[:, 0:384], ['PSC'], ['cols'])
        P.barrier()

        if stage == 1:
            P.emit()
            return nc
        apos[0] = NML * 4
        nb = norm_bufs()
        hT = nb[3]
        kTM = ar("kTM", [128, 512], BF16)
        wk2 = ar("wk2", [128, 4, 128], BF16)
        v1 = ar("v1", [128, 4, 136], BF16)
        osig = ar("osig", [128, 512])
        qTm = ar("qTm", [128, 4, 128], BF16)
        kTm = ar("kTm", [128, 4, 128], BF16)
        Sp = ar("Sp", [128, 128], BF16)
        hh = ar("hh", [128, 128])
        hsq = ar("hsq", [128, 128])
        mixTM = ar("mixTM", [128, 512], BF16)
        ms('pool', v1.rearrange("p a b -> p (a b)"), 1.0, ['v1'])
        PZ = [PS[1], PS[2]]
        PF = PS[3]
        PM = PS[7]
        PMS = PS[4]
        PMN = PS[5]
        zc = [0]

        def proj_tm(wv, wnames, hTs, hname, c0, n=128):
            z = zc[0] % 2
            zc[0] += 1
            for kc in range(8):
                mm(PZ[z][0:n, :], hTs[:, kc, 0:n], wv[:, kc, c0:c0 + 512], kc == 0, kc == 7,
                   [hname] + wnames, [('PZ', z)])
            return z

        for kb in range(NB if 'mlpass' not in SKIP else 0):
            if stage == 15 and kb == 15:
                break
            win = kb >= W0
            wb = kb - W0
            s = kb % 2
            norm_T(nb, xall[kb * 128:(kb + 1) * 128, :], 128, hT[s], ('hT', s))
            hn = ('hT', s)
            z = proj_tm(wml, WML, hT[s], hn, 512)
            cp('act', kTM, PZ[z][:], [('PZ', z)], ['kTM'])
            for h in range(4):
                ts('dve', wk2[:, h, :], kTM[:, h * 128:(h + 1) * 128], colsT[:, 1, h, kb:kb + 1], None, ALU.mult, None,
                   ['kTM', 'cols'], ['wk2'])
            z = proj_tm(wml, WML, hT[s], hn, 1024)
            cp('act', v1[:, :, 0:128], PZ[z][:].rearrange("p (a b) -> p a b", a=4), [('PZ', z)], ['v1'])
            if win and 'osig' not in SKIP:
                z = proj_tm(wml, WML, hT[s], hn, 1536)
                act(osig, PZ[z][:], AF.Exp, [('PZ', z)], ['osig'], scale=-1.0)
                ts('dve', osig, osig, 1.0, None, ALU.add, None, ['osig'], ['osig'])
                P.op('dve', lambda e: e.reciprocal(out=osig, in_=osig), ['osig'], ['osig'])
                tt('dve', osig, osig, ghead_t[:, 0, :], ALU.mult, ['osig', 'ghead'], ['osig'])
                for (c0, dst, nm) in (((0, qTm, 'qTm'), (512, kTm, 'kTm')) if 'fm' not in SKIP else ()):
                    for h in range(4):
                        for kc in range(8):
                            mm(PF[:, h * 128:(h + 1) * 128], wml[:, kc, c0 + h * 128:c0 + (h + 1) * 128],
                               hT[s][:, kc, :], kc == 0, kc == 7, [hn] + WML, ['PF'])
                    cp('act', dst.rearrange("p a b -> p (a b)"), PF[:], ['PF'], [nm])
            for h in range(4):
                if win and 'out' not in SKIP:
                    mm(PMS[:, 0:128], kTm[:, h, :], qTm[:, h, :], True, True, ['kTm', 'qTm'], ['PM0'])
                    stt('dve', Sp, PMS[:, 0:128], colsT[:, 0, h, kb:kb + 1], trilf[:], ALU.mult, ALU.mult,
                        ['PM0', 'cols', 'trilf'], ['Sp'])
                    mm(PMN[:, 0:129], qTm[:, h, :], Cbf[:, h, 0:129], True, False, ['qTm', ('Cbf', h)], ['PM1'])
                    mm(PMN[:, 0:129], Sp, v1[:, h, 0:129], False, True, ['Sp', 'v1'], ['PM1'])
                    act(dn[:, 0:1], PMN[:, 128:129], AF.Abs, ['PM1'], ['dn'])
                    ts('dve', dn[:, 0:1], dn[:, 0:1], colsT[:, 2, h, kb:kb + 1], None, ALU.max, None,
                       ['dn', 'cols'], ['dn'])
                    P.op('dve', lambda e: e.reciprocal(out=dn[:, 1:2], in_=dn[:, 0:1]), ['dn'], ['dn'])
                    act(hh, PMN[:, 0:128], AF.Copy, ['PM1', 'dn'], ['hh'], scale=dn[:, 1:2])
                    act(hsq, hh, AF.Square, ['hh'], ['hsq', 'hss'], accum_out=hss[:, 0:1])
                    rstd_of(hss[:], 128, 128)
                    stt('dve', mixTM[:, h * 128:(h + 1) * 128], hh, hss[:, 1:2], osig[:, h * 128:(h + 1) * 128],
                        ALU.mult, ALU.mult, ['hh', 'hss', 'osig'], ['mixTM'])
                mm(PM[:, 0:129], wk2[:, h, :], v1[:, h, 0:129], True, True, ['wk2', 'v1'], ['PM2'])
                stt('dve', Cst[:, h, 0:129], Cst[:, h, 0:129], decb[:, h, kb:kb + 1], PM[:, 0:129], ALU.mult, ALU.add,
                    ['Cst', 'decb', 'PM2'], ['Cst'])
                cp('act', Cbf[:, h, 0:129], Cst[:, h, 0:129], ['Cst'], [('Cbf', h)])
            if win and 'trans' not in SKIP:
                pT = PST[:].bitcast(BF16)
                for kc in range(4):
                    tr(pT[:, kc * 128:(kc + 1) * 128], mixTM[:, kc * 128:(kc + 1) * 128], identb[:],
                       ['mixTM', 'identb'], ['PST'])
                cp('act', mixT[:, 0:4, wb * 128:(wb + 1) * 128], pT[:, 0:512].rearrange("p (a b) -> p a b", a=4),
                   ['PST'], [('mixT', wb, 0)])
        if 'sml' not in SKIP:
            hTs = ar("hTs", [128, 8, 64], BF16)
            C0all = ar("C0all", [128, 16, 4, 128])
            kTMs = ar("kTMs", [64, 512], BF16)
            v1s = ar("v1s", [64, 4, 136], BF16)
            osigs = ar("osigs", [64, 512])
            qTMs = ar("qTMs", [64, 512])
            qTs = ar("qTs", [128, 4, 64], BF16)
            kTs = ar("kTs", [128, 4, 64], BF16)
            qTf = ar("qTf", [128, 4, 64])
            gs = ar("gs", [64, 8])
            bifb = ar("bifb", [64, 8])
            lfs = ar("lfs", [64, 4])
            bs = ar("bs", [64, 4])
            a_s = ar("a_s", [64, 4])
            m0s = ar("m0s", [16, 4])
            n0s = ar("n0s", [16, 512])
            n0f = ar("n0f", [64, 128])
            m0x = ar("m0x", [64, 4])
            n0x = ar("n0x", [64, 512])
            G1x = ar("G1x", [64, 4])
            cmx = ar("cmx", [64, 4])
            dgs = ar("dgs", [64, 64])
            Ams = ar("Ams", [64, 64])
            e1x = ar("e1x", [64, 4])
            e2x = ar("e2x", [64, 4])
            flx = ar("flx", [64, 4])
            dcx = ar("dcx", [64, 4])
            mnx = ar("mnx", [64, 4])
            tb = ar("tb", [64, 4])
            qn = ar("qn", [64, 4])
            prod = ar("prod", [64, 512])
            bm_t = ar("bm_t", [64, 64])
            bmn_t = ar("bmn_t", [64, 64])
            bmask_t = ar("bmask_t", [64, 16])
            bmaskb = ar("bmaskb", [64, 16], BF16)
            bmT_t = ar("bmT_t", [16, 64])
            onesf = ar("onesf", [64, 128])
            decbs = ar("decbs", [128, 4, 64])
            Sps = ar("Sps", [64, 64], BF16)
            itf = ar("itf", [128, 64])
            itT = ar("itT", [64, 128])
            hhs = ar("hhs", [64, 128])
            wk2s = ar("wk2s", [64, 4, 128], BF16)
            vx = ar("vx", [64, 16, 128], BF16)
            Cout = [ar("Cout%d" % i, [128, 4, 128]) for i in range(2)]
            n0T = ar("n0T", [128, 64])
            nnT = ar("nnT", [128, 64])
            nout = ar("nout", [64, 128])
            mixTMs = ar("mixTMs", [64, 512], BF16)
            P.dma('sp', bm_t, bm_d, writes=['bm'])
            P.dma('sp', bmn_t, bmn_d, writes=['bmn'])
            P.dma('sp', bmask_t, bmask_d, writes=['bmask'])
            P.dma('sp', bmT_t, bmT_d, writes=['bmT'])
            P.dma('sp', m0s, stm, writes=['m0s'])
            P.dma('sp', n0s, stn.rearrange("b h d -> b (h d)"), writes=['n0s'])
            P.dma('sp', n0f, stn.rearrange("b h d -> (b h) d"), writes=['n0f'])
            P.dma('sp', bifb, bifr_d.partition_broadcast(64), writes=['bifb'])
            for b_ in range(16):
                P.dma('sp', C0all[:, b_, :, :], stC[b_].rearrange("h d e -> d h e"), writes=[('C0', b_)])
            ms('pool', onesf, 1.0, ['onesf'])
            ms('pool', v1s.rearrange("p a b -> p (a b)"), 1.0, ['v1s'])
            cp('dve', bmaskb, bmask_t, ['bmask'], ['bmaskb'])
            norm_T(nb, xsm, 64, hTs, 'hTs')
            z = proj_tm(wml, WML, hTs, 'hTs', 512, 64)
            cp('act', kTMs, PZ[z][0:64, :], [('PZ', z)], ['kTMs'])
            z = proj_tm(wml, WML, hTs, 'hTs', 1024, 64)
            cp('act', v1s[:, :, 0:128], PZ[z][0:64, :].rearrange("p (a b) -> p a b", a=4), [('PZ', z)], ['v1s'])
            z = proj_tm(wml, WML, hTs, 'hTs', 1536, 64)
            act(osigs, PZ[z][0:64, :], AF.Exp, [('PZ', z)], ['osigs'], scale=-1.0)
            ts('dve', osigs, osigs, 1.0, None, ALU.add, None, ['osigs'], ['osigs'])
            P.op('dve', lambda e: e.reciprocal(out=osigs, in_=osigs), ['osigs'], ['osigs'])
            tt('dve', osigs, osigs, ghead_t[0:64, 0, :], ALU.mult, ['osigs', 'ghead'], ['osigs'])
            z = proj_tm(wml, WML, hTs, 'hTs', 0, 64)
            cp('act', qTMs, PZ[z][0:64, :], [('PZ', z)], ['qTMs'])
            for (c0, dst, nm) in ((0, qTs, 'qTs'), (512, kTs, 'kTs')):
                for h in range(4):
                    for kc in range(8):
                        mm(PF[:, h * 64:(h + 1) * 64], wml[:, kc, c0 + h * 128:c0 + (h + 1) * 128], hTs[:, kc, :],
                           kc == 0, kc == 7, ['hTs'] + WML, ['PF'])
                cp('act', dst.rearrange("p a b -> p (a b)"), PF[:, 0:256], ['PF'], [nm])
                if nm == 'qTs':
                    cp('act', qTf.rearrange("p a b -> p (a b)"), PF[:, 0:256], ['PF'], ['qTf'])
            for kc in range(8):
                mm(PM[0:64, 0:8], hTs[:, kc, :], wml[:, kc, 2048:2056], kc == 0, kc == 7, ['hTs'] + WML, ['PM2'])
            tt('dve', gs, PM[0:64, 0:8], bifb, ALU.add, ['PM2', 'bifb'], ['gs'])
            act(lfs, gs[:, 4:8], AF.Exp, ['gs'], ['lfs'], scale=-1.0)
            act(lfs, lfs, AF.Ln, ['lfs', 'onec'], ['lfs'], bias=onec[0:64, :])
            ts('dve', lfs, lfs, -1.0, None, ALU.mult, None, ['lfs'], ['lfs'])
            mm(PM[0:64, 8:12], bm_t, lfs, True, True, ['bm', 'lfs'], ['PM2'])
            cp('dve', bs, PM[0:64, 8:12], ['PM2'], ['bs'])
            tt('dve', a_s, gs[:, 0:4], bs, ALU.subtract, ['gs', 'bs'], ['a_s'])
            mm(PM[0:64, 12:16], bmT_t, m0s, True, True, ['bmT', 'm0s'], ['PM2'])
            cp('dve', m0x, PM[0:64, 12:16], ['PM2'], ['m0x'])
            mm(PZ[0][0:64, :], bmT_t, n0s, True, True, ['bmT', 'n0s'], [('PZ', 0)])
            cp('act', n0x, PZ[0][0:64, :], [('PZ', 0)], ['n0x'])
            for h in range(4):
                ts('dve', dgs, identf[0:64, 0:64], a_s[:, h:h + 1], None, ALU.mult, None, ['identf', 'a_s'], ['dgs'])
                mm(PMS[0:64, 0:64], onesf[:, 0:64], dgs, True, True, ['onesf', 'dgs'], ['PM0'])
                tt('dve', Ams, PMS[0:64, 0:64], bmn_t, ALU.add, ['PM0', 'bmn'], ['Ams'])
                P.op('dve', lambda e, h=h: e.tensor_reduce(out=cmx[:, h:h + 1], in_=Ams, axis=AX.X, op=ALU.max),
                     ['Ams'], ['cmx'])
            tt('dve', G1x, cmx, m0x, ALU.max, ['cmx', 'm0x'], ['G1x'])
            tt('dve', tb, a_s, m0x, ALU.subtract, ['a_s', 'm0x'], ['tb'])
            act(e1x, tb, AF.Exp, ['tb', 'lncc'], ['e1x'], bias=lncc[0:64, :])
            tt('dve', tb, a_s, G1x, ALU.subtract, ['a_s', 'G1x', 'e1x'], ['tb'])
            act(e2x, tb, AF.Exp, ['tb', 'lncc'], ['e2x'], bias=lncc[0:64, :])
            tt('dve', tb, bs, m0x, ALU.add, ['bs', 'm0x', 'e2x'], ['tb'])
            act(flx, tb, AF.Exp, ['tb'], ['flx'], scale=-1.0)
            tt('dve', tb, m0x, G1x, ALU.subtract, ['m0x', 'G1x', 'flx'], ['tb'])
            act(dcx, tb, AF.Exp, ['tb'], ['dcx'])
            tt('dve', mnx, bs, G1x, ALU.add, ['bs', 'G1x'], ['mnx'])
            P.dma('sp', s_m, mnx[3:64:4, :], reads=['mnx'])
            tt('dve', prod, qTMs, n0x, ALU.mult, ['qTMs', 'n0x'], ['prod'])
            P.op('dve', lambda e: e.tensor_reduce(out=qn, in_=prod.rearrange("p (a b) -> p a b", a=4), axis=AX.X,
                                                  op=ALU.add), ['prod'], ['qn'])
            for h in range(4):
                ts('dve', dgs, identf[0:64, 0:64], dcx[:, h:h + 1], None, ALU.mult, None, ['identf', 'dcx'], ['dgs'])
                mm(PMS[:, 64:128], onesf, dgs, True, True, ['onesf', 'dgs'], ['PM0'])
                cp('dve', decbs[:, h, :], PMS[:, 64:128], ['PM0'], ['decbs'])
            mm(PMS[:, 128:192], n0f, identf[0:64, 0:64], True, True, ['n0f', 'identf'], ['PM0'])
            cp('dve', n0T, PMS[:, 128:192], ['PM0'], ['n0T'])
            for h in range(4):
                mm(PMS[0:64, 0:64], kTs[:, h, :], qTs[:, h, :], True, True, ['kTs', 'qTs'], ['PM0'])
                stt('dve', Sps, PMS[0:64, 0:64], e1x[:, h:h + 1], bm_t, ALU.mult, ALU.mult, ['PM0', 'e1x', 'bm'], ['Sps'])
                mm(PMN[0:64, 0:129], Sps, v1s[:, h, 0:129], True, True, ['Sps', 'v1s'], ['PM1'])
                for b_ in range(16):
                    mm(PF[:, b_ * 4:(b_ + 1) * 4], C0all[:, b_, h, :], qTf[:, h, b_ * 4:(b_ + 1) * 4], True, True,
                       [('C0', b_), 'qTf'], ['PF'])
                cp('act', itf, PF[:, 0:64], ['PF'], ['itf'])
                mm(PM[0:64, 0:128], itf, identf[:], True, True, ['itf', 'identf'], ['PM2'])
                cp('act', itT, PM[0:64, 0:128], ['PM2'], ['itT'])
                tt('dve', dn[0:64, 0:1], PMN[0:64, 128:129], qn[:, h:h + 1], ALU.add, ['PM1', 'qn'], ['dn'])
                act(dn[0:64, 0:1], dn[0:64, 0:1], AF.Abs, ['dn'], ['dn'])
                ts('dve', dn[0:64, 0:1], dn[0:64, 0:1], flx[:, h:h + 1], None, ALU.max, None, ['dn', 'flx'], ['dn'])
                P.op('dve', lambda e: e.reciprocal(out=dn[0:64, 1:2], in_=dn[0:64, 0:1]), ['dn'], ['dn'])
                tt('dve', hhs, PMN[0:64, 0:128], itT, ALU.add, ['PM1', 'itT'], ['hhs'])
                ts('dve', hhs, hhs, dn[0:64, 1:2], None, ALU.mult, None, ['hhs', 'dn'], ['hhs'])
                act(hsq[0:64, :], hhs, AF.Square, ['hhs'], ['hsq', 'hss'], accum_out=hss[0:64, 0:1])
                rstd_of(hss[0:64, :], 64, 128)
                stt('dve', mixTMs[:, h * 128:(h + 1) * 128], hhs, hss[0:64, 1:2], osigs[:, h * 128:(h + 1) * 128],
                    ALU.mult, ALU.mult, ['hhs', 'hss', 'osigs'], ['mixTMs'])
                ts('dve', wk2s[:, h, :], kTMs[:, h * 128:(h + 1) * 128], e2x[:, h:h + 1], None, ALU.mult, None,
                   ['kTMs', 'e2x'], ['wk2s'])
                tt('pool', vx, v1s[:, h, 0:128].unsqueeze(1).to_broadcast([64, 16, 128]),
                   bmaskb.unsqueeze(2).to_broadcast([64, 16, 128]), ALU.mult, ['v1s', 'bmaskb'], ['vx'])
                for g in range(4):
                    z = zc[0] % 2
                    zc[0] += 1
                    mm(PZ[z][:, :], wk2s[:, h, :], vx[:, g * 4:(g + 1) * 4, :].rearrange("p a b -> p (a b)"), True, True,
                       ['wk2s', 'vx'], [('PZ', z)])
                    co = Cout[g % 2]
                    for bb in range(4):
                        b_ = g * 4 + bb
                        stt('dve', co[:, bb, :], C0all[:, b_, h, :], decbs[:, h, b_ * 4:b_ * 4 + 1],
                            PZ[z][:, bb * 128:(bb + 1) * 128], ALU.mult, ALU.add,
                            [('C0', b_), 'decbs', ('PZ', z)], [('Cout', g % 2)])
                    P.dma('sp', s_C[g * 4:(g + 1) * 4, h].rearrange("b d e -> d b e"), co, reads=[('Cout', g % 2)])
                mm(PMS[:, 192 + h * 16:192 + (h + 1) * 16], wk2s[:, h, :], bmaskb, True, True, ['wk2s', 'bmaskb'], ['PM0'])
            n0T3 = n0T.rearrange("p (b h) -> p b h", h=4)
            nnT3 = nnT.rearrange("p (b h) -> p b h", h=4)
            for h in range(4):
                tt('dve', nnT3[:, :, h], n0T3[:, :, h], decbs[:, h, 0:64:4], ALU.mult, ['n0T', 'decbs'], ['nnT'])
                tt('dve', nnT3[:, :, h], nnT3[:, :, h], PMS[:, 192 + h * 16:192 + (h + 1) * 16], ALU.add, ['nnT', 'PM0'],
                   ['nnT'])
            mm(PM[0:64, 0:128], nnT, identf[:], True, True, ['nnT', 'identf'], ['PM2'])
            cp('dve', nout, PM[0:64, 0:128], ['PM2'], ['nout'])
            P.dma('sp', s_n.rearrange("b h d -> (b h) d"), nout, reads=['nout'])
            pT = PST[:].bitcast(BF16)
            for kc in range(4):
                tr(pT[:, kc * 128:kc * 128 + 64], mixTMs[:, kc * 128:(kc + 1) * 128], identb[0:64, 0:64],
                   ['mixTMs', 'identb'], ['PST'])
            cp('act', mixTs[:, 0:4, :], pT[:, 0:512].rearrange("p (a b) -> p a b", a=4)[:, :, 0:64], ['PST'], ['mixTs0'])
        for h in range(4):
            P.dma('sp', p_C[h], Cst[:, h, 0:128], reads=['Cst'])
            P.dma('sp', p_n[h:h + 1, :].rearrange("o d -> d o"), Cst[:, h, 128:129], reads=['Cst'])
        P.barrier()

        if stage in (2, 15):
            P.emit()
            return nc
        apos[0] = 0
        wda = ar("wda", [128, 8, NDA], BF16)
        KT = ar("KT", [128, 4, NB * 128], BF16)
        V1 = ar("V1", [128, NB, 4, 136], BF16)
        keep = apos[0]
        WDA = load_w(wda, C_DA, NDA, 0, 'wda')
        P.barrier()
        apos[0] = keep
        nb = norm_bufs()
        hT = nb[3]
        dkst = [ar("dkst%d" % i, [128, 512]) for i in range(2)]
        dvst = [ar("dvst%d" % i, [128, 512]) for i in range(2)]
        dqst = ar("dqst", [128, 512])
        rt = ar("rt", [128, 4, 64])
        dkb = ar("dkb", [128, 512], BF16)
        dqb = ar("dqb", [128, 512], BF16)
        Qblk = ar("Qblk", [128, 4, 256], BF16)
        Pt = [ar("Pt%d" % i, [128, 256], BF16) for i in range(2)]
        t1a = ar("t1a", [128, 128])
        aa = ar("aa", [128, 128])
        hsq = ar("hsq", [128, 128])
        mixTM = ar("mixTM", [128, 512], BF16)
        ms('pool', Qblk.rearrange("p a b -> p (a b)"), 0.0, ['Qblk'])
        PSSB = [PS[4], PS[3]]
        PO = [PS[5], PS[6]]

        rcs = [ar("rcs%d" % i, [128, 2, 64]) for i in range(2)]
        rctr = [0]

        def rope_load(tok0, n, src_c, src_s):
            i = rctr[0] % 2
            rctr[0] += 1
            P.dma('sp', rcs[i][0:n, 0, :], src_c[tok0:tok0 + n, :], writes=[('rcs', i)])
            P.dma('sp', rcs[i][0:n, 1, :], src_s[tok0:tok0 + n, :], writes=[('rcs', i)])
            return i

        def rope(buf, bname, i, n=128):
            if 'rope' in SKIP:
                return
            b3 = buf.rearrange("p (a b) -> p a b", b=64)
            x1 = b3[:, :, 0:8]
            x2 = b3[:, :, 8:16]
            cb = rcs[i][0:n, 0, :].rearrange("p (a b) -> p a b", b=8)
            sbb = rcs[i][0:n, 1, :].rearrange("p (a b) -> p a b", b=8)
            r = rt[0:n].rearrange("p a (b c) -> p a b c", c=8)
            rn = ('rcs', i)
            tt('dve', r[:, 0], x1, cb, ALU.mult, [bname, rn], ['rt0'])
            tt('pool', r[:, 1], x2, sbb, ALU.mult, [bname, rn], ['rt1'])
            tt('dve', r[:, 2], x2, cb, ALU.mult, [bname, rn], ['rt2'])
            tt('pool', r[:, 3], x1, sbb, ALU.mult, [bname, rn], ['rt3'])
            tt('dve', x1, r[:, 0], r[:, 1], ALU.subtract, ['rt0', 'rt1', 'rt3'], [bname])
            tt('pool', x2, r[:, 2], r[:, 3], ALU.add, ['rt2', 'rt3', 'rt0'], [bname])

        for kb in range(NB):
            win = kb >= W0
            own = kb > W0
            wb = kb - W0
            ob = kb - W0 - 1
            s = kb % 2
            norm_T(nb, xall[kb * 128:(kb + 1) * 128, :], 128, hT[s], ('hT', s))
            hn = ('hT', s)
            z = proj_tm(wda, WDA, hT[s], hn, 1024)
            for h_ in (range(4) if 'v1ms' not in SKIP else ()):
                ms('pool', V1[:, kb, h_, 128:130], 1.0, [('V1', kb)])
            cp('act', V1[:, kb, :, 0:128], PZ[z][:].rearrange("p (a b) -> p a b", a=4), [('PZ', z)], [('V1', kb)])
            if own and 'outd' not in SKIP:
                cp('act', dvst[s], PZ[z][:], [('PZ', z)], [('dvst', s)])
                P.dma('sp', p_dv[ob * 128:(ob + 1) * 128, :], dvst[s], reads=[('dvst', s)])
            z = proj_tm(wda, WDA, hT[s], hn, 512)
            cp('act', dkst[s], PZ[z][:], [('PZ', z)], [('dkst', s)])
            ri = rope_load(kb * 128, 128, ropeC_d, ropeS_d) if 'ropeld' not in SKIP else 0
            rope(dkst[s], ('dkst', s), ri)
            if own and 'outd' not in SKIP:
                P.dma('sp', p_dk[ob * 128:(ob + 1) * 128, :], dkst[s], reads=[('dkst', s)])
            cp('dve', dkb, dkst[s], [('dkst', s)], ['dkb'])
            pT = PST[:].bitcast(BF16)
            for h in range(4):
                tr(pT[:, h * 128:(h + 1) * 128], dkb[:, h * 128:(h + 1) * 128], identb[:], ['dkb', 'identb'], ['PST'])
            cp('act', KT[:, :, kb * 128:(kb + 1) * 128], pT[:, 0:512].rearrange("p (a b) -> p a b", a=4),
               ['PST'], [('KT', kb)])
            if not win or 'qproj' in SKIP:
                continue
            z = proj_tm(wda, WDA, hT[s], hn, 0)
            cp('act', dqst, PZ[z][:], [('PZ', z)], ['dqst'])
            rope(dqst, 'dqst', ri)
            cp('dve', dqb, dqst, ['dqst'], ['dqb'])
            for h in range(4):
                tr(pT[:, 512 + h * 128:512 + (h + 1) * 128], dqb[:, h * 128:(h + 1) * 128], identb[:],
                   ['dqb', 'identb'], ['PST'])
            pq = pT[:, 512:1024].rearrange("p (a b) -> p a b", a=4)
            cp('act', Qblk[0:64, :, 0:128], pq[0:64], ['PST'], ['Qblk'])
            cp('act', Qblk[64:128, :, 128:256], pq[64:128], ['PST'], ['Qblk'])
            for h in (range(4) if 'attn' not in SKIP else ()):
                pon = 'PO'

                class _PO3:
                    def __getitem__(self, key):
                        p_, c_, f_ = key
                        return PO[c_][p_, f_]
                po3 = _PO3()
                def _score(kk):
                    half = kk % 2
                    sc = PSSB[half][:, 0:256]
                    mm(sc, KT[:, h, kk * 128:(kk + 1) * 128], Qblk[:, h, :], True, True,
                       [('KT', kk), 'Qblk'], [('PSS', half)])
                    pt = Pt[half]
                    if kk < kb:
                        act(pt, sc, AF.Exp, [('PSS', half), 'kbias'], [('Pt', half)], scale=0.125,
                            bias=kbias[:, kk:kk + 1])
                    else:
                        act(pt, sc, AF.Exp, [('PSS', half)], [('Pt', half)], scale=0.125)
                        tt('pool', pt, pt, tril2b[:], ALU.mult, [('Pt', half), 'tril2b'], [('Pt', half)])

                def _pv(kk):
                    half = kk % 2
                    pt = Pt[half]
                    for c in range(2):
                        mm(po3[:, c, 0:129], pt[:, c * 128:(c + 1) * 128], V1[:, kk, h, 0:129], kk == 0, kk == kb,
                           [('Pt', half), ('V1', kk)], [pon])
                for kk in range(kb + 1):
                    _score(kk)
                    if kk >= 1:
                        _pv(kk - 1)
                _pv(kb)
                ts('dve', dn[:, 0:1], po3[:, 0, 128:129], 1e-30, None, ALU.max, None, [pon], ['dn'])
                ts('dve', dn[:, 2:3], po3[:, 1, 128:129], 1e-30, None, ALU.max, None, [pon], ['dn'])
                P.op('dve', lambda e: e.reciprocal(out=dn[:, 1:2], in_=dn[:, 0:1]), ['dn'], ['dn'])
                P.op('dve', lambda e: e.reciprocal(out=dn[:, 3:4], in_=dn[:, 2:3]), ['dn'], ['dn'])
                tt('dve', dn[:, 3:4], dn[:, 3:4], lam_t[:], ALU.mult, ['dn', 'lam'], ['dn'])
                act(t1a, po3[:, 1, 0:128], AF.Copy, [pon, 'dn'], ['t1a'], scale=dn[:, 3:4])
                stt('dve', aa, po3[:, 0, 0:128], dn[:, 1:2], t1a, ALU.mult, ALU.subtract,
                    [pon, 'dn', 't1a'], ['aa'])
                act(hsq, aa, AF.Square, ['aa'], ['hsq', 'hss'], accum_out=hss[:, 0:1])
                rstd_of(hss[:], 128, 128)
                stt('dve', mixTM[:, h * 128:(h + 1) * 128], aa, hss[:, 1:2],
                    ghead_t[:, 1, h * 128:(h + 1) * 128], ALU.mult, ALU.mult, ['aa', 'hss', 'ghead'], ['mixTM'])
            for kc in range(4):
                tr(pT[:, kc * 128:(kc + 1) * 128], mixTM[:, kc * 128:(kc + 1) * 128], identb[:],
                   ['mixTM', 'identb'], ['PST'])
            cp('act', mixT[:, 4:8, wb * 128:(wb + 1) * 128], pT[:, 0:512].rearrange("p (a b) -> p a b", a=4),
               ['PST'], [('mixT', wb, 1)])
        if 'sda' not in SKIP:
            P.barrier()
            apos[0] = NDA * 4
            hTs2 = ar("hTs2", [128, 8, 64], BF16)
            dvs = ar("dvs", [64, 512])
            dks = ar("dks", [64, 512])
            dqs = ar("dqs", [64, 512])
            vsb = ar("vsb", [64, 4, 128], BF16)
            dksb = ar("dksb", [64, 512], BF16)
            dqsb = ar("dqsb", [64, 512], BF16)
            KTs = ar("KTs", [128, 4, 64], BF16)
            Qbs = ar("Qbs", [128, 4, 2, 64], BF16)
            bm2 = ar("bm2", [64, 128], BF16)
            bmf = ar("bmf", [64, 64])
            Pself = ar("Pself", [64, 128], BF16)
            onesb2 = ar("onesb2", [128, 128], BF16)
            accS = ar("accS", [128, 4, 128])
            denS = ar("denS", [128, 4, 128])
            accP = ar("accP", [128, 16, 32])
            denP = ar("denP", [128, 16, 32])
            ptI = ar("ptI", [128, 256], I32)
            ptF = ar("ptF", [128, 256])
            idxI = ar("idxI", [128, 256], I32)
            iot = ar("iot", [128, 1])
            kpg = [ar("kpg%d" % i, [128, 512]) for i in range(2)]
            vpg = [ar("vpg%d" % i, [128, 512]) for i in range(2)]
            kpb2 = [ar("kpb%d" % i, [128, 512], BF16) for i in range(2)]
            vpb = [ar("vpb%d" % i, [128, 512], BF16) for i in range(2)]
            KTp2 = [ar("KTp%d" % i, [128, 4, 128], BF16) for i in range(2)]
            Pp2 = [ar("Pp%d" % i, [128, 32], BF16) for i in range(2)]
            numT = ar("numT", [128, 4, 2, 64])
            denT = ar("denT", [128, 4, 2, 64])
            aT = ar("aT", [128, 4, 64])
            aTM = ar("aTM", [64, 128])
            mixTMs2 = ar("mixTMs2", [64, 512], BF16)
            P.dma('sp', bmf, bm_d, writes=['bmf'])
            P.dma('sp', ptI, ptab.partition_broadcast(128), writes=['ptI'])
            P.dma('sp', iot, iota_d, writes=['iot'])
            cp('dve', bm2[:, 0:64], bmf, ['bmf'], ['bm2'])
            cp('dve', bm2[:, 64:128], bmf, ['bmf'], ['bm2'])
            ms('pool', onesb2, 1.0, ['onesb2'])
            ms('pool', Qbs.rearrange("p a b c -> p (a b c)"), 0.0, ['Qbs'])
            ms('pool', accP.rearrange("p a b -> p (a b)"), 0.0, ['accP'])
            ms('pool', denP.rearrange("p a b -> p (a b)"), 0.0, ['denP'])
            cp('dve', ptF, ptI, ['ptI'], ['ptF'])
            ts('dve', ptF, ptF, 128.0, iot[:, 0:1], ALU.mult, ALU.add, ['ptF', 'iot'], ['ptF'])
            cp('dve', idxI, ptF, ['ptF'], ['idxI'])
            norm_T(nb, xsm, 64, hTs2, 'hTs2')
            z = proj_tm(wda, WDA, hTs2, 'hTs2', 1024, 64)
            cp('act', dvs, PZ[z][0:64, :], [('PZ', z)], ['dvs'])
            P.dma('sp', s_dv, dvs, reads=['dvs'])
            cp('dve', vsb.rearrange("p a b -> p (a b)"), dvs, ['dvs'], ['vsb'])
            ri = rope_load(0, 64, ropeCs_d, ropeSs_d)
            z = proj_tm(wda, WDA, hTs2, 'hTs2', 512, 64)
            cp('act', dks, PZ[z][0:64, :], [('PZ', z)], ['dks'])
            rope(dks, 'dks', ri, 64)
            P.dma('sp', s_dk, dks, reads=['dks'])
            cp('dve', dksb, dks, ['dks'], ['dksb'])
            z = proj_tm(wda, WDA, hTs2, 'hTs2', 0, 64)
            cp('act', dqs, PZ[z][0:64, :], [('PZ', z)], ['dqs'])
            rope(dqs, 'dqs', ri, 64)
            cp('dve', dqsb, dqs, ['dqs'], ['dqsb'])
            pT = PST[:].bitcast(BF16)
            for h in range(4):
                tr(pT[:, h * 64:(h + 1) * 64], dksb[:, h * 128:(h + 1) * 128], identb[0:64, 0:64], ['dksb', 'identb'], ['PST'])
                tr(pT[:, 256 + h * 64:256 + (h + 1) * 64], dqsb[:, h * 128:(h + 1) * 128], identb[0:64, 0:64],
                   ['dqsb', 'identb'], ['PST'])
            cp('act', KTs.rearrange("p a b -> p (a b)"), pT[:, 0:256], ['PST'], ['KTs'])
            pq = pT[:, 256:512].rearrange("p (a b) -> p a b", a=4)
            cp('act', Qbs[0:64, :, 0, :], pq[0:64], ['PST'], ['Qbs'])
            cp('act', Qbs[64:128, :, 1, :], pq[64:128], ['PST'], ['Qbs'])
            for h in range(4):
                mm(PSSB[0][0:64, 0:128], KTs[:, h, :], Qbs[:, h, :, :].rearrange("p a b -> p (a b)"), True, True,
                   ['KTs', 'Qbs'], [('PSS', 0)])
                act(Pself, PSSB[0][0:64, 0:128], AF.Exp, [('PSS', 0)], ['Pself'], scale=0.125)
                tt('pool', Pself, Pself, bm2, ALU.mult, ['Pself', 'bm2'], ['Pself'])
                mm(PO[0][:, 0:128], vsb[:, h, :], Pself, True, True, ['vsb', 'Pself'], ['PO'])
                mm(PO[1][:, 0:128], onesb2[0:64, :], Pself, True, True, ['onesb2', 'Pself'], ['PO'])
                cp('act', accS[:, h, :], PO[0][:, 0:128], ['PO'], ['accS'])
                cp('act', denS[:, h, :], PO[1][:, 0:128], ['PO'], ['denS'])
            for b_ in range(16):
                for j in range(16):
                    pi = b_ * 16 + j
                    s2 = pi % 2
                    kpb, KTp, Pp = kpb2[s2], KTp2[s2], Pp2[s2]
                    kpbn, KTpn, Ppn = ('kpb', s2), ('KTp', s2), ('Pp', s2)

                    def gk(e, pi=pi, s2=s2):
                        return e.indirect_dma_start(out=kpg[s2], out_offset=None, in_=ckp,
                                                    in_offset=bass.IndirectOffsetOnAxis(ap=idxI[:, pi:pi + 1], axis=0))

                    def gv(e, pi=pi, s2=s2):
                        return e.indirect_dma_start(out=vpg[s2], out_offset=None, in_=cvp,
                                                    in_offset=bass.IndirectOffsetOnAxis(ap=idxI[:, pi:pi + 1], axis=0))
                    P.dmaf('pool', gk, reads=['idxI'], writes=[('kpg', s2)])
                    P.dmaf('pool', gv, reads=['idxI'], writes=[('vpg', s2)])
                    cp('dve', kpb, kpg[s2], [('kpg', s2)], [kpbn])
                    cp('act', vpb[s2], vpg[s2], [('vpg', s2)], [('vpb', s2)])
                    for h in range(4):
                        tr(pT[:, h * 128:(h + 1) * 128], kpb[:, h * 128:(h + 1) * 128], identb[:], [kpbn, 'identb'], ['PST'])
                    cp('act', KTp.rearrange("p a b -> p (a b)"), pT[:, 0:512], ['PST'], [KTpn])
                    for h in range(4):
                        mm(PSSB[1][:, h * 8:(h + 1) * 8], KTp[:, h, :], Qbs[:, h, :, b_ * 4:(b_ + 1) * 4], True, True,
                           [KTpn, 'Qbs'], [('PSS', 1)])
                    act(Pp, PSSB[1][:, 0:32], AF.Exp, [('PSS', 1)], [Ppn], scale=0.125)
                    for h in range(4):
                        mm(PO[0][:, h * 8:(h + 1) * 8], vpb[s2][:, h * 128:(h + 1) * 128], Pp[:, h * 8:(h + 1) * 8], True, True,
                           [('vpb', s2), Ppn], ['PO'])
                    mm(PO[1][:, 0:32], onesb2, Pp, True, True, ['onesb2', Ppn], ['PO'])
                    tt('dve', accP[:, b_, :], accP[:, b_, :], PO[0][:, 0:32], ALU.add, ['accP', 'PO'], ['accP'])
                    tt('dve', denP[:, b_, :], denP[:, b_, :], PO[1][:, 0:32], ALU.add, ['denP', 'PO'], ['denP'])
            accP5 = accP.rearrange("p b (h c q) -> p h c b q", h=4, c=2)
            denP5 = denP.rearrange("p b (h c q) -> p h c b q", h=4, c=2)
            for h in range(4):
                for c in range(2):
                    tt('dve', numT[:, h, c, :].rearrange("p (b q) -> p b q", q=4), accP5[:, h, c],
                       accS[:, h, c * 64:(c + 1) * 64].rearrange("p (b q) -> p b q", q=4), ALU.add, ['accP', 'accS'], ['numT'])
                    tt('dve', denT[:, h, c, :].rearrange("p (b q) -> p b q", q=4), denP5[:, h, c],
                       denS[:, h, c * 64:(c + 1) * 64].rearrange("p (b q) -> p b q", q=4), ALU.add, ['denP', 'denS'], ['denT'])
            nT2 = numT.rearrange("p a b c -> p (a b c)")
            dT2 = denT.rearrange("p a b c -> p (a b c)")
            P.op('dve', lambda e: e.reciprocal(out=dT2, in_=dT2), ['denT'], ['denT'])
            tt('dve', nT2, nT2, dT2, ALU.mult, ['numT', 'denT'], ['numT'])
            for h in range(4):
                stt('dve', aT[:, h, :], numT[:, h, 1, :], lam_t[:, 0:1], numT[:, h, 0, :], ALU.mult, ALU.subtract,
                    ['numT', 'lam'], ['aT'])
                mm(PSSB[0][0:64, 0:128], aT[:, h, :], identf[:], True, True, ['aT', 'identf'], [('PSS', 0)])
                act(aTM, PSSB[0][0:64, 0:128], AF.Copy, [('PSS', 0)], ['aTM'], scale=-1.0)
                act(hsq[0:64, :], aTM, AF.Square, ['aTM'], ['hsq', 'hss'], accum_out=hss[0:64, 0:1])
                rstd_of(hss[0:64, :], 64, 128)
                stt('dve', mixTMs2[:, h * 128:(h + 1) * 128], aTM, hss[0:64, 1:2], ghead_t[0:64, 1, h * 128:(h + 1) * 128],
                    ALU.mult, ALU.mult, ['aTM', 'hss', 'ghead'], ['mixTMs2'])
            for kc in range(4):
                tr(pT[:, kc * 128:kc * 128 + 64], mixTMs2[:, kc * 128:(kc + 1) * 128], identb[0:64, 0:64],
                   ['mixTMs2', 'identb'], ['PST'])
            cp('act', mixTs[:, 4:8, :], pT[:, 0:512].rearrange("p (a b) -> p a b", a=4)[:, :, 0:64], ['PST'], ['mixTs1'])
            assert apos[0] <= keep, apos[0]
        P.barrier()

        if stage == 3:
            P.emit()
            return nc
        apos[0] = 0
        w_outb = ar("w_outb", [128, 8, D], BF16)
        w_mqb = ar("w_mqb", [128, 8, D], BF16)
        w_mob = ar("w_mob", [128, 8, D], BF16)
        mkT = ar("mkT", [128, 8, 256], BF16)
        mvb = ar("mvb", [128, 2, D], BF16)
        gpost_t = ar("gpost_t", [128, 3, D])
        gffn_t = ar("gffn_t", [128, D])
        cwa_t = ar("cwa_t", [128, NCF, 4])
        cwg_t = ar("cwg_t", [128, NCF, 4])
        onesb = ar("onesb", [128, 128], BF16)
        keep2 = apos[0]
        for i in range(3):
            P.dma('sp', gpost_t[:, i, :], gpost[i:i + 1, :].partition_broadcast(128), writes=['gpost'])
        P.dma('sp', gffn_t, gffn_d.partition_broadcast(128), writes=['gffn'])
        P.dma('sp', cwa_t, cwa, writes=['cwa'])
        P.dma('sp', cwg_t, cwg, writes=['cwg'])
        ms('pool', onesb, 1.0, ['onesb'])

        def load_w2(dst, src, ncols, gidx, nm):
            hw = ncols // 2
            st = [ar("%s_st%d" % (nm, i), [128, hw]) for i in range(2)]
            for kc in range(8):
                for s in range(2):
                    P.dma('sp', st[s], src[kc * 128:(kc + 1) * 128, s * hw:(s + 1) * hw], writes=[('w2st', s)])
                    if gidx is None:
                        cp('act' if s == 0 else 'pool', dst[:, kc, s * hw:(s + 1) * hw], st[s], [('w2st', s)],
                           [(nm, kc, s)])
                    elif s == 0:
                        act(dst[:, kc, 0:hw], st[s], AF.Copy, [('w2st', s), 'gpre'], [(nm, kc, s)],
                            scale=gpre_t[:, gidx, kc:kc + 1])
                    else:
                        ts('pool', dst[:, kc, hw:], st[s], gpre_t[:, gidx, kc:kc + 1], None, ALU.mult, None,
                           [('w2st', s), 'gpre'], [(nm, kc, s)])
            apos[0] -= 2 * hw
        load_w2(w_outb, w_out, D, None, 'w_out')
        load_w2(w_mqb, w_mq, D, 1, 'w_mq')
        load_w2(w_mob, w_mo, D, None, 'w_mo')
        w_mkb = ar("w_mkb", [128, 8, D], BF16)
        w_mvb = ar("w_mvb", [128, 8, D], BF16)
        load_w2(w_mkb, w_mk, D, 2, 'w_mk')
        load_w2(w_mvb, w_mv, D, 2, 'w_mv')
        P.barrier()
        nb = norm_bufs()
        hTm = ar("hTm", [128, 8, 256], BF16)
        mst = [ar("mst%d" % i, [128, 512]) for i in range(2)]
        for mb in range(2):
            norm_T(nb, memp[mb * 128:(mb + 1) * 128, :], 128, hTm[:, :, mb * 128:(mb + 1) * 128], 'hTm')
        PA = [PS[0], PS[1]]
        PT2 = PS[2]
        PQ = PS[3]
        for c in range(8):
            for kc in range(8):
                mm(PQ[:, 0:256], w_mkb[:, kc, c * 128:(c + 1) * 128], hTm[:, kc, :], kc == 0, kc == 7,
                   ['hTm', 'w_mk'], ['PQ'])
            cp('act', mkT[:, c, :], PQ[:, 0:256], ['PQ'], ['mkT'])
        for (wv, dstb, dout_, nm) in ((w_mvb, mvb, p_mv, 'w_mv'), (w_mkb, None, p_mk, 'w_mk')):
            for mb in range(2):
                for half in range(2):
                    for kc in range(8):
                        mm(PA[half][:], hTm[:, kc, mb * 128:(mb + 1) * 128], wv[:, kc, half * 512:(half + 1) * 512],
                           kc == 0, kc == 7, ['hTm', nm], [('PA', half)])
                    if dstb is not None:
                        cp('act', dstb[:, mb, half * 512:(half + 1) * 512], PA[half][:], [('PA', half)], ['mvb'])
                    cp('act', mst[half], PA[half][:], [('PA', half)], [('mst', half)])
                    P.dma('sp', dout_[mb * 128:(mb + 1) * 128, half * 512:(half + 1) * 512], mst[half],
                          reads=[('mst', half)])
        P.barrier()
        apos[0] = keep2
        x2buf = ar("x2buf", [128, 3, D])
        xnT = ar("xnT", [128, 8, 384], BF16)
        xr = ar("xr", [128, D])
        tmpf = ar("tmpf", [128, D])
        xsb2 = ar("xsb2", [128, D], BF16)
        xsq2 = ar("xsq2", [128, D], BF16)
        x1nT = ar("x1nT", [128, 8, 128], BF16)
        qT = ar("qT", [128, 8, 128], BF16)
        PmT = ar("PmT", [128, 4, 2, 128], BF16)
        rden = ar("rden", [128, 4, 128])
        oT = ar("oT", [128, 8, 128], BF16)
        ss2 = ar("ss2", [128, 4])
        wupst = [ar("wupst%d" % i, [128, 8, 128]) for i in range(2)]
        wupb = [ar("wupb%d" % i, [128, 8, 2, 128], BF16) for i in range(2)]
        wdst = [ar("wdst0", [128, D])] * 2
        wdb = [ar("wdb%d" % i, [128, D], BF16) for i in range(2)]
        uxa = ar("uxa", [128, 386])
        uxg = ar("uxg", [128, 386])
        o_c1a = apos[0]
        c1a = ar("c1a", [128, 384])
        c1g = ar("c1g", [128, 384])
        sgt = ar("sgt", [128, 384])
        actT = [ar("actT%d" % i, [128, 384], BF16) for i in range(2)]
        cara = ar("cara", [128, NCF, 2])
        carg = ar("carg", [128, NCF, 2])
        yst = [ar("yst0", [128, D])] * 2
        o_pcst = apos[0]
        pcst = ar("pcst", [2, 512])
        ms('pool', cara.rearrange("p a b -> p (a b)"), 0.0, ['cara'])
        ms('pool', carg.rearrange("p a b -> p (a b)"), 0.0, ['carg'])
        PSC2 = [PS[4], PS[5]]
        PDN = PS[6]

        def sumsq_rstd(srcs, srcnames, n, width, col):
            for i, (sap, snm) in enumerate(zip(srcs, srcnames)):
                act(xsq2[0:n, 0:sap.shape[1]], sap, AF.Square, [snm], ['xsq2', ('ss2', i)], accum_out=ss2[0:n, i:i + 1])
            if len(srcs) == 2:
                tt('dve', ss2[0:n, 0:1], ss2[0:n, 0:1], ss2[0:n, 1:2], ALU.add, [('ss2', 0), ('ss2', 1)], [('ss2', 0)])
            act(ss2[0:n, 2:3], ss2[0:n, 0:1], AF.Ln, [('ss2', 0), 'epsc'], ['ss2r'], scale=1.0 / width, bias=epsc[0:n, :])
            act(ss2[0:n, 2:3], ss2[0:n, 2:3], AF.Exp, ['ss2r'], ['ss2r'], scale=-0.5)
            return ss2[0:n, 2:3]

        def post_norm_res(gi_, resid, resname, dst, dstname, n=128):
            r = sumsq_rstd([PA[0][0:n, :], PA[1][0:n, :]], [('PA', 0), ('PA', 1)], n, D, 0)
            for half in range(2):
                sl = slice(half * 512, (half + 1) * 512)
                stt('dve', tmpf[0:n, sl], PA[half][0:n, :], r, gpost_t[0:n, gi_, sl], ALU.mult, ALU.mult,
                    [('PA', half), 'ss2r', 'gpost'], [('tmpf', half)])
                tt('pool', dst[:, sl], tmpf[0:n, sl], resid[:, sl], ALU.add, [('tmpf', half), resname], [dstname])

        def pre_norm_T(src, srcname, gbc, gname, dst, dstname, n=128):
            r = sumsq_rstd([src], [srcname], n, D, 0)
            if gbc is None:
                ts('dve', xsb2[0:n, :], src, r, None, ALU.mult, None, [srcname, 'ss2r'], ['xsb2'])
            else:
                stt('dve', xsb2[0:n, :], src, r, gbc[0:n, :], ALU.mult, ALU.mult, [srcname, 'ss2r', gname], ['xsb2'])
            pT = PT2[:].bitcast(BF16)
            for kc in range(8):
                tr(pT[:, kc * 128:kc * 128 + n], xsb2[0:n, kc * 128:(kc + 1) * 128], identb[0:n, 0:n],
                   ['xsb2', 'identb'], ['PT2'])
            cp('act', dst, pT[:, :].rearrange("p (a b) -> p a b", a=8)[:, :, 0:n], ['PT2'], [dstname])

        def mem_attend(n, mk_of, mv_of, tok_groups):
            for (c0, c1, mkT_, mvb_, mnames) in tok_groups:
                w = c1 - c0
                for h in range(4):
                    for mb in range(2):
                        for j in range(2):
                            mm(PSC2[h // 2][:, ((h % 2) * 2 + mb) * 128 + c0:((h % 2) * 2 + mb) * 128 + c1],
                               mkT_[:, 2 * h + j, mb * 128:(mb + 1) * 128], qT[:, 2 * h + j, c0:c1], j == 0, j == 1,
                               ['qT'] + mnames, [('PSC2', h // 2)])
                for hp in range(2):
                    act(PmT[:, 2 * hp:2 * hp + 2, :, c0:c1],
                        PSC2[hp][:].rearrange("p (a b c) -> p a b c", a=2, b=2)[:, :, :, c0:c1],
                        AF.Exp, [('PSC2', hp)], ['PmT'], scale=1.0 / 16.0)
                for h in range(4):
                    for mb in range(2):
                        mm(PDN[:, h * 128 + c0:h * 128 + c1], onesb, PmT[:, h, mb, c0:c1], mb == 0, mb == 1,
                           ['PmT', 'onesb'], ['PDN'])
                P.op('dve', lambda e, c0=c0, c1=c1: e.reciprocal(
                    out=rden[:, :, c0:c1], in_=PDN[:].rearrange("p (a b) -> p a b", a=4)[:, :, c0:c1]), ['PDN'], ['rden'])
                for rnd in range(2):
                    for cc in range(4):
                        c = rnd * 4 + cc
                        h = c // 2
                        for mb in range(2):
                            mm(PQ[:, cc * 128 + c0:cc * 128 + c1], mvb_[:, mb, c * 128:(c + 1) * 128],
                               PmT[:, h, mb, c0:c1], mb == 0, mb == 1, ['PmT'] + mnames, ['PQ'])
                    tt('dve', oT[:, rnd * 4:rnd * 4 + 4, c0:c1].rearrange("p (h j) t -> p h j t", j=2),
                       PQ[:].rearrange("p (h j t) -> p h j t", h=2, j=2)[:, :, :, c0:c1],
                       rden[:, rnd * 2:rnd * 2 + 2, c0:c1].unsqueeze(2).to_broadcast([128, 2, 2, w]), ALU.mult,
                       ['PQ', 'rden'], ['oT'])

        def part_a(mixsrc, mixname, xsrc_dram, n, x2dst, x2name, xnTdst, xnTname, tok_groups):
            for half in range(2):
                for kc in range(8):
                    mm(PA[half][0:n, :], mixsrc[:, kc, :], w_outb[:, kc, half * 512:(half + 1) * 512], kc == 0, kc == 7,
                       [mixname], [('PA', half)])
            P.dma('sp', xr[0:n, :], xsrc_dram, writes=['xr'])
            post_norm_res(0, xr[0:n, :], 'xr', x2dst, x2name, n)
            pre_norm_T(x2dst, x2name, None, None, x1nT[:, :, 0:n], 'x1nT', n)
            for rnd in range(2):
                for cc in range(4):
                    c = rnd * 4 + cc
                    for kc in range(8):
                        mm(PQ[:, cc * 128:cc * 128 + n], w_mqb[:, kc, c * 128:(c + 1) * 128], x1nT[:, kc, 0:n],
                           kc == 0, kc == 7, ['x1nT'], ['PQ'])
                cp('act', qT[:, rnd * 4:rnd * 4 + 4, 0:n], PQ[:].rearrange("p (a b) -> p a b", a=4)[:, :, 0:n], ['PQ'], ['qT'])
            if callable(tok_groups):
                tok_groups()
            else:
                mem_attend(n, None, None, tok_groups)
            for half in range(2):
                for c in range(8):
                    mm(PA[half][0:n, :], oT[:, c, 0:n], w_mob[:, c, half * 512:(half + 1) * 512], c == 0, c == 7,
                       ['oT'], [('PA', half)])
            post_norm_res(1, x2dst, x2name, x2dst, x2name, n)
            pre_norm_T(x2dst, x2name, gffn_t, 'gffn', xnTdst, xnTname, n)

        def part_b(blocks, n_per, out_fn, first_tile, last_tile, conv_init=None):
            nblk = len(blocks)
            ntok = nblk * n_per
            PU = [PS[0], PS[1]]
            PD = [[PS[2 + 2 * b_ + hf] for hf in range(2)] for b_ in range(nblk)]
            for cf in range(NCF):
                sz = 128 if cf < 21 else 64
                s = cf % 2
                for j, off in enumerate((0, DFF)):
                    P.dma('sp', wupst[j][:, :, 0:sz],
                          w_up[:, off + cf * 128:off + cf * 128 + sz].rearrange("(kc p) n -> p kc n", p=128),
                          writes=[('wupst', j)])
                    cp('pool' if j == 0 else 'dve', wupb[s][:, :, j, 0:sz], wupst[j][:, :, 0:sz], [('wupst', j)],
                       [('wupb', s, j)])
                P.dma('sp', wdst[s][0:sz, :], w_down[cf * 128:cf * 128 + sz, :], writes=['wdst'])
                cp('pool', wdb[s][0:sz, :], wdst[s][0:sz, :], ['wdst'], [('wdb', s)])
                for j, (ux, car, cw_, c1, nm) in enumerate(((uxa, cara, cwa_t, c1a, 'a'), (uxg, carg, cwg_t, c1g, 'g'))):
                    for kc in range(8):
                        mm(PU[j][0:sz, 0:ntok], wupb[s][:, kc, j, 0:sz], xnT[:, kc, 0:ntok], kc == 0, kc == 7,
                           [('wupb', s, j), 'xnT'], [('PU', j)])
                    if conv_init is None:
                        cp('pool', ux[0:sz, 0:2], car[0:sz, cf, :], ['car' + nm], ['ux' + nm])
                        cp('act', ux[0:sz, 2:2 + ntok], PU[j][0:sz, 0:ntok], [('PU', j)], ['ux' + nm])
                        if first_tile:
                            ts('pool', ux[0:sz, 2 + 126:2 + 128], ux[0:sz, 2 + 126:2 + 128], hflag[0:sz, :], None, ALU.mult,
                               None, ['ux' + nm, 'hflag'], ['ux' + nm])
                        cp('pool', car[0:sz, cf, :], ux[0:sz, ntok:ntok + 2], ['ux' + nm], ['car' + nm])
                        eng = 'dve' if j == 0 else 'pool'
                        ts(eng, c1[0:sz, 0:ntok], ux[0:sz, 2:2 + ntok], cw_[0:sz, cf, 2:3], cw_[0:sz, cf, 3:4], ALU.mult,
                           ALU.add, ['ux' + nm, 'cw' + nm], ['c1' + nm])
                        stt('dve', c1[0:sz, 0:ntok], ux[0:sz, 1:1 + ntok], cw_[0:sz, cf, 1:2], c1[0:sz, 0:ntok], ALU.mult,
                            ALU.add, ['ux' + nm, 'cw' + nm, 'c1' + nm], ['c1' + nm])
                        stt('dve', c1[0:sz, 0:ntok], ux[0:sz, 0:ntok], cw_[0:sz, cf, 0:1], c1[0:sz, 0:ntok], ALU.mult,
                            ALU.add, ['ux' + nm, 'cw' + nm, 'c1' + nm], ['c1' + nm])
                    else:
                        conv_init(j, cf, sz, PU[j], ux, c1, cw_, nm)
                act(sgt[0:sz, 0:ntok], c1g[0:sz, 0:ntok], AF.Silu, ['c1g'], ['sgt'])
                tt('dve', actT[s][0:sz, 0:ntok], c1a[0:sz, 0:ntok], sgt[0:sz, 0:ntok], ALU.mult, ['c1a', 'sgt'],
                   [('actT', s)])
                for b_ in range(nblk):
                    for hf in range(2):
                        mm(PD[b_][hf][0:n_per, :], actT[s][0:sz, b_ * n_per:(b_ + 1) * n_per],
                           wdb[s][0:sz, hf * 512:(hf + 1) * 512], cf == 0, cf == NCF - 1,
                           [('actT', s), ('wdb', s)], [('PD', b_, hf)])
            for b_, slot in enumerate(blocks):
                if slot is None:
                    continue
                r = sumsq_rstd([PD[b_][0][0:n_per, :], PD[b_][1][0:n_per, :]], [('PD', b_, 0), ('PD', b_, 1)], n_per, D, 0)
                ys = yst[b_ % 2]
                for hf in range(2):
                    sl = slice(hf * 512, (hf + 1) * 512)
                    stt('dve', tmpf[0:n_per, sl], PD[b_][hf][0:n_per, :], r, gpost_t[0:n_per, 2, sl], ALU.mult, ALU.mult,
                        [('PD', b_, hf), 'ss2r', 'gpost'], [('tmpf', hf)])
                    tt('pool', ys[0:n_per, sl], tmpf[0:n_per, sl], x2buf[0:n_per, slot, sl], ALU.add,
                       [('tmpf', hf), ('x2', slot)], ['yst'])
                out_fn(b_, slot, ys)

        own_tiles = [[0, 1, 2]] + [[w_, w_ + 1] for w_ in range(3, 17, 2)]
        PGRP = [(0, 128, mkT, mvb, ['mkT', 'mvb'])]
        for ti, tile_ in enumerate(own_tiles):
            for j, wb in enumerate(tile_):
                part_a(mixT[:, :, wb * 128:(wb + 1) * 128], ('mixT', wb), xall[(W0 + wb) * 128:(W0 + wb + 1) * 128, :], 128,
                       x2buf[:, j, :], ('x2', j), xnT[:, :, j * 128:(j + 1) * 128], 'xnT', PGRP)
            P.barrier()

            def out_p(b_, slot, ys, tile_=tile_):
                ob = tile_[b_] - 1
                P.dma('sp', y_p[ob * 128:(ob + 1) * 128, :], ys, reads=['yst'])
            blocks = [None if wb == 0 else j for j, wb in enumerate(tile_)]
            part_b(blocks, 128, out_p, ti == 0, ti == len(own_tiles) - 1)
            P.barrier()
        for j, (car, off) in enumerate(((cara, 0), (carg, DFF))):
            for g0 in range(0, NCF, 4):
                ncf = min(4, NCF - g0)
                wid = 0
                for cf in range(g0, g0 + ncf):
                    sz = 128 if cf < 21 else 64
                    mm(PS[0][0:2, wid:wid + sz], car[0:sz, cf, :], identf[0:sz, 0:sz], True, True, ['cara', 'carg', 'identf'],
                       ['PCV'])
                    wid += sz
                cp('dve', pcst[:, 0:wid], PS[0][0:2, 0:wid], ['PCV'], ['pcst'])
                P.dma('sp', p_cv[:, off + g0 * 128:off + g0 * 128 + wid], pcst[:, 0:wid], reads=['pcst'])
        P.barrier()

        if 'sp2' not in SKIP:
            _mk = x2buf[:, 1, :].bitcast(BF16).rearrange("p (a b) -> p a b", a=8)
            mkTs = [_mk, _mk]
            _mv = arena[:, o_c1a:o_c1a + 1024].bitcast(BF16).rearrange("p (a b) -> p a b", a=2)
            mvbs = [_mv, _mv]
            cms = [yst[0], wdst[0]]
            cmb = x2buf[:, 2, :].bitcast(BF16)[:, 0:D]
            scst = ar("scst", [32, 128])
            scT = ar("scT", [128, 32])
            scout = arena[0:32, o_pcst:o_pcst + 512]
            lctr = [0]

            def sample_mem():
                for b_ in range(16):
                    s2 = 0
                    for mb in range(2):
                        i = lctr[0] % 2
                        lctr[0] += 1
                        P.dma('sp', cms[i], cmk[b_, mb * 128:(mb + 1) * 128, :], writes=[('cms', i)])
                        cp('pool', cmb, cms[i], [('cms', i)], ['cmb'])
                        pT = PT2[:].bitcast(BF16)
                        for c in range(8):
                            tr(pT[:, c * 128:(c + 1) * 128], cmb[:, c * 128:(c + 1) * 128], identb[:], ['cmb', 'identb'], ['PT2'])
                        cp('act', mkTs[s2][:, :, mb * 128:(mb + 1) * 128], pT[:, :].rearrange("p (a b) -> p a b", a=8),
                           ['PT2'], [('mkTs', s2)])
                        i = lctr[0] % 2
                        lctr[0] += 1
                        P.dma('sp', cms[i], cmv[b_, mb * 128:(mb + 1) * 128, :], writes=[('cms', i)])
                        cp('pool', mvbs[s2][:, mb, :], cms[i], [('cms', i)], [('mvbs', s2)])
                    mem_attend(64, None, None, [(b_ * 4, b_ * 4 + 4, mkTs[s2], mvbs[s2], [('mkTs', s2), ('mvbs', s2)])])
            part_a(mixTs[:, :, :], 'mixTs', xsm, 64, x2buf[0:64, 0, :], ('x2', 0), xnT[:, :, 0:64], 'xnT', sample_mem)
            P.barrier()
            stcv2 = stcv.rearrange("b r f -> (b r) f")
            scv2 = s_cv.rearrange("b r f -> (b r) f")
            wacc = [0]

            def sample_conv(j, cf, sz, PUj, ux, c1, cw_, nm):
                off = j * DFF + cf * 128
                P.dma('sp', scst[:, 0:sz], stcv2[:, off:off + sz], writes=['scst'])
                mm(PS[7][0:sz, 0:32], scst[:, 0:sz], identf[0:32, 0:32], True, True, ['scst', 'identf'], ['PS7'])
                ux6 = ux[0:sz, 0:96].rearrange("p (b t) -> p b t", t=6)
                cp('act', ux6[:, :, 0:2], PS[7][0:sz, 0:32].rearrange("p (b t) -> p b t", t=2), ['PS7'], ['ux' + nm])
                cp('act', ux6[:, :, 2:6], PUj[0:sz, 0:64].rearrange("p (b t) -> p b t", t=4), [('PU', j)], ['ux' + nm])
                c1v = c1[0:sz, 0:64].rearrange("p (b t) -> p b t", t=4)
                ts('dve' if j == 0 else 'pool', c1v, ux6[:, :, 2:6], cw_[0:sz, cf, 2:3], cw_[0:sz, cf, 3:4], ALU.mult, ALU.add,
                   ['ux' + nm, 'cw' + nm], ['c1' + nm])
                stt('dve', c1v, ux6[:, :, 1:5], cw_[0:sz, cf, 1:2], c1v, ALU.mult, ALU.add, ['ux' + nm, 'cw' + nm, 'c1' + nm],
                    ['c1' + nm])
                stt('dve', c1v, ux6[:, :, 0:4], cw_[0:sz, cf, 0:1], c1v, ALU.mult, ALU.add, ['ux' + nm, 'cw' + nm, 'c1' + nm],
                    ['c1' + nm])
                cp('pool', scT[0:sz, :].rearrange("p (b t) -> p b t", t=2), ux6[:, :, 4:6], ['ux' + nm], ['scT'])
                g0 = (cf // 4) * 4
                col = (cf - g0) * 128
                psj = PS[6] if j == 0 else PS[5]
                mm(psj[0:32, col:col + sz], scT[0:sz, :], identf[0:sz, 0:sz], True, True, ['scT', 'identf'], [('PS6', j)])
                if cf % 4 == 3 or cf == NCF - 1:
                    wid = col + sz
                    cp('act', scout[:, 0:wid], psj[0:32, 0:wid], [('PS6', j)], ['scout'])
                    P.dma('sp', scv2[:, j * DFF + g0 * 128:j * DFF + g0 * 128 + wid], scout[:, 0:wid], reads=['scout'])

            def out_s(b_, slot, ys):
                P.dma('sp', y_s, ys[0:64, :], reads=['yst'])
            part_b([0], 64, out_s, False, False, conv_init=sample_conv)
            P.barrier()
        P.emit()
    return nc


def _consts(half):
    c = {}
    c["ident"] = np.eye(128, dtype=np.float32)
    k = np.arange(128)
    c["tril"] = (k[:, None] <= k[None, :]).astype(np.float32)
    halfd = 8
    inv = np.exp(-math.log(500000.0) * (2.0 * np.arange(halfd, dtype=np.float32) / 16)).astype(np.float32)
    pos = np.arange(4096, dtype=np.float32) - (2048.0 if half == 0 else 0.0)
    ang = (pos[:, None] * inv[None, :]).astype(np.float32)
    c["ropeC"] = np.ascontiguousarray(np.tile(np.cos(ang).astype(np.float32), (1, 8)))
    c["ropeS"] = np.ascontiguousarray(np.tile(np.sin(ang).astype(np.float32), (1, 8)))
    poss = (2048.0 + np.arange(4, dtype=np.float32))
    angs = (poss[:, None] * inv[None, :]).astype(np.float32)
    c["ropeCs"] = np.ascontiguousarray(np.tile(np.cos(angs).astype(np.float32), (16, 8)))
    c["ropeSs"] = np.ascontiguousarray(np.tile(np.sin(angs).astype(np.float32), (16, 8)))
    kb = np.zeros((128, NB), np.float32)
    gv = np.ones((4, NB, 128), np.float32)
    if half == 0:
        kb[:, :16] = -BIG
        gv[:, :16, :] = 0.0
    gv = gv.reshape(128, 128)
    hk = np.arange(128)
    c["lt"] = ((hk[:, None] // NB == hk[None, :] // NB) & (hk[:, None] < hk[None, :])).astype(np.float32)
    c["kbias"] = kb
    c["gvalid"] = gv
    c["hflag"] = np.full((128, 1), float(half), np.float32)
    c["iota"] = np.arange(128, dtype=np.float32).reshape(128, 1)
    t = np.arange(64)
    c["bm"] = ((t[:, None] // 4 == t[None, :] // 4) & (t[:, None] <= t[None, :])).astype(np.float32)
    c["bmask"] = (t[:, None] // 4 == np.arange(16)[None, :]).astype(np.float32)
    c["bmT"] = np.ascontiguousarray(c["bmask"].T)
    c["bmn"] = np.where(t[:, None] // 4 == t[None, :] // 4, 0.0, -BIG).astype(np.float32)
    sel = np.zeros((4, 4, 128), np.float32)
    for h in range(4):
        sel[h, h, :] = 1.0
    c["sel"] = sel
    return c


def make_in_maps(inp, pool_k=None, pool_v=None, ptab=None):
    f = lambda a: np.ascontiguousarray(np.asarray(a, dtype=np.float32))
    xp = f(inp["x_prompt"])
    xs = f(inp["x_sample"])
    ck = f(inp["cache_dk"])[0].reshape(-1, 512) if pool_k is None else pool_k
    cv = f(inp["cache_dv"])[0].reshape(-1, 512) if pool_v is None else pool_v
    pt = np.asarray(inp["page_table"], dtype=np.int32) if ptab is None else ptab
    gp = np.stack([f(inp[n])[0].reshape(8, 128).T for n in ("g_mix_pre", "g_mem_pre", "g_mem_src", "g_ffn_pre")], 1)
    gpost = np.stack([f(inp[n])[0] for n in ("g_mix_post", "g_mem_post", "g_ffn_post")], 0)
    ghead = np.stack([f(inp["g_ml_head"])[0].reshape(512), f(inp["g_da_head"])[0].reshape(512)], 0)
    b = f(inp["b_if"])[0]
    bif = np.stack([b[:4], b[4:]], 1)
    wdw = f(inp["w_dw"])[0]
    bdw = f(inp["b_dw"])[0]

    def cw(off):
        o = np.zeros((128, NCF, 4), np.float32)
        for cf in range(NCF):
            sz = 128 if cf < 21 else 64
            o[:sz, cf, 0:3] = wdw[:, off + cf * 128: off + cf * 128 + sz].T
            o[:sz, cf, 3] = bdw[off + cf * 128: off + cf * 128 + sz]
        return o
    shared = {
        "w_in": f(inp["w_in"])[0], "w_out": f(inp["w_out"])[0], "w_mq": f(inp["w_mq"])[0],
        "w_mk": f(inp["w_mk"])[0], "w_mv": f(inp["w_mv"])[0], "w_mo": f(inp["w_mo"])[0],
        "w_up": f(inp["w_up"])[0], "w_down": f(inp["w_down"])[0],
        "gpre": np.ascontiguousarray(gp), "gpost": np.ascontiguousarray(gpost), "ghead": np.ascontiguousarray(ghead),
        "gffn": np.ascontiguousarray(f(inp["g_ffn_pre"])[0].reshape(1, D)),
        "bif": np.ascontiguousarray(bif), "bifr": np.ascontiguousarray(b.reshape(1, 8)), "dlam": f(inp["da_lambda"])[0].reshape(1, 256),
        "cwa": cw(0), "cwg": cw(DFF), "ckp": ck, "cvp": cv,
    }
    cst = [_consts(0), _consts(1)]
    maps = []
    for c in range(8):
        bb, half = c // 2, c % 2
        m = dict(shared)
        m.update(cst[half])
        if half == 1:
            m["xall"] = np.ascontiguousarray(xp[bb])
        else:
            xa = np.zeros((4096, D), np.float32)
            xa[2048:] = xp[bb, :2048]
            m["xall"] = xa
        sl = slice(c * 16, (c + 1) * 16)
        m["xsm"] = np.ascontiguousarray(xs[sl].reshape(64, D))
        m["memp"] = np.ascontiguousarray(f(inp["mem_prompt"])[bb])
        m["ptab"] = np.ascontiguousarray(pt[sl].reshape(1, 256))
        m["cmk"] = np.ascontiguousarray(f(inp["cache_mem_k"])[0, sl].reshape(16, 256, D))
        m["cmv"] = np.ascontiguousarray(f(inp["cache_mem_v"])[0, sl].reshape(16, 256, D))
        m["stC"] = np.ascontiguousarray(f(inp["state_ml_C"])[0, sl])
        m["stn"] = np.ascontiguousarray(f(inp["state_ml_n"])[0, sl])
        m["stm"] = np.ascontiguousarray(f(inp["state_ml_m"])[0, sl])
        m["stcv"] = np.ascontiguousarray(f(inp["state_conv"])[0, sl])
        maps.append(m)
    return maps


def assemble(res):
    r = res
    yp = np.zeros((4, 4096, D), np.float32)
    pdk = np.zeros((1, 4, 4096, 4, 128), np.float32)
    pdv = np.zeros((1, 4, 4096, 4, 128), np.float32)
    pmk = np.zeros((1, 4, 256, 4, 256), np.float32)
    pmv = np.zeros((1, 4, 256, 4, 256), np.float32)
    pC = np.zeros((1, 4, 4, 128, 128), np.float32)
    pn = np.zeros((1, 4, 4, 128), np.float32)
    pm = np.zeros((1, 4, 4), np.float32)
    pcv = np.zeros((1, 4, 2, 2 * DFF), np.float32)
    ys = np.zeros((128, 4, D), np.float32)
    sdk = np.zeros((1, 128, 4, 4, 128), np.float32)
    sdv = np.zeros((1, 128, 4, 4, 128), np.float32)
    sC = np.zeros((1, 128, 4, 128, 128), np.float32)
    sn = np.zeros((1, 128, 4, 128), np.float32)
    sm = np.zeros((1, 128, 4), np.float32)
    scv = np.zeros((1, 128, 2, 2 * DFF), np.float32)
    for c in range(8):
        bb, half = c // 2, c % 2
        o = r[c]
        ts_ = slice(half * 2048, (half + 1) * 2048)
        yp[bb, ts_] = o["y_p"]
        pdk[0, bb, ts_] = o["p_dk"].reshape(2048, 4, 128)
        pdv[0, bb, ts_] = o["p_dv"].reshape(2048, 4, 128)
        if half == 1:
            pmk[0, bb] = o["p_mk"].reshape(256, 4, 256)
            pmv[0, bb] = o["p_mv"].reshape(256, 4, 256)
            pC[0, bb] = o["p_C"]
            pn[0, bb] = o["p_n"]
            pm[0, bb] = o["p_m"].reshape(4)
            pcv[0, bb] = o["p_cv"]
        sl = slice(c * 16, (c + 1) * 16)
        ys[sl] = o["y_s"].reshape(16, 4, D)
        sdk[0, sl] = o["s_dk"].reshape(16, 4, 4, 128)
        sdv[0, sl] = o["s_dv"].reshape(16, 4, 4, 128)
        sC[0, sl] = o["s_C"]
        sn[0, sl] = o["s_n"]
        sm[0, sl] = o["s_m"]
        scv[0, sl] = o["s_cv"]
    return (yp, ys, pdk, pdv, pmk, pmv, pC, pn, pm, pcv, sdk, sdv, sC, sn, sm, scv)


STAGE = 99


def kernel(**inputs):
    ck = np.asarray(inputs["cache_dk"], dtype=np.float32)
    cv = np.asarray(inputs["cache_dv"], dtype=np.float32)
    n_pool = int(ck.shape[1])
    pk = np.ascontiguousarray(ck[0].reshape(-1, 512))
    pv = np.ascontiguousarray(cv[0].reshape(-1, 512))
    nc = build(n_pool, stage=STAGE)
    maps = make_in_maps(inputs, pool_k=pk, pool_v=pv)
    res = run_bass_kernel_spmd(nc, maps, core_ids=list(range(8)))
    return assemble(res.results)
```

```python
import math
import numpy as np
from contextlib import ExitStack
import concourse.bass as bass
import concourse.mybir as mybir
from concourse.bass_utils import run_bass_kernel_spmd

F32 = mybir.dt.float32
BF16 = mybir.dt.bfloat16
I32 = mybir.dt.int32
AF = mybir.ActivationFunctionType
ALU = mybir.AluOpType
AX = mybir.AxisListType

D = 1024
NB = 32
W0 = 15
IN_COLS = 3592
DFF = 2752
NCF = 22
BIG = 30000.0
LNC = math.log(128.0 ** -0.5)
LAM_INIT = 0.2
EPS = 1e-6
SKIP = set()


class Prog:
    CENG = ['pe', 'act', 'dve', 'pool']
    ENG = ['pe', 'act', 'dve', 'pool', 'sp']
    DQ = ['sp', 'act', 'pool']
    NDS = 8

    def __init__(self, nc, es):
        self.nc = nc
        self.ins = {e: [] for e in self.ENG}
        self.NEP = {'pe': 32, 'act': 18, 'dve': 18, 'pool': 8}
        self.EPOCH = 1000
        self.csem = {e: [es.enter_context(nc.semaphore("c_%s_%d" % (e, k))) for k in range(self.NEP[e])]
                     for e in self.CENG}
        self.ccount = {e: 0 for e in self.CENG}
        self.dsems = {q: [es.enter_context(nc.semaphore("d_%s_%d" % (q, i))) for i in range(self.NDS)]
                      for q in self.DQ}
        self.dcount = {q: [0] * self.NDS for q in self.DQ}
        self.dpos = {q: 0 for q in self.DQ}
        self.semobj = {}
        for e in self.CENG:
            for k in range(self.NEP[e]):
                self.semobj[('c', e, k)] = self.csem[e][k]
        for q in self.dsems:
            for i, s in enumerate(self.dsems[q]):
                self.semobj[('d', q, i)] = s
        self.last_w = {}
        self.readers = {}
        self.seen = {e: {} for e in self.ENG}

    def _deps(self, eng, reads, writes, ev, is_pe):
        deps = {}

        def add(e):
            if e is None:
                return
            key, val = e
            if is_pe and key[0] == 'c' and key[1] == 'pe':
                return
            if deps.get(key, 0) < val:
                deps[key] = val
        for r in reads:
            add(self.last_w.get(r))
        for w in writes:
            add(self.last_w.get(w))
            for e in self.readers.get(w, ()):
                add(e)
        for r in reads:
            self.readers.setdefault(r, []).append(ev)
        for w in writes:
            self.last_w[w] = ev
            self.readers[w] = []
        waits = []
        seen = self.seen[eng]
        for key, val in deps.items():
            if seen.get(key, 0) >= val:
                continue
            seen[key] = val
            waits.append((key, val))
        return waits

    def op(self, eng, fn, reads=(), writes=()):
        ep = self.ccount[eng] // self.EPOCH
        assert ep < self.NEP[eng], eng
        ev = (('c', eng, ep), self.ccount[eng] % self.EPOCH + 1)
        self.ccount[eng] += 1
        waits = self._deps(eng, reads, writes, ev, eng == 'pe')
        self.ins[eng].append((fn, waits, ev, 1))
        return ev

    def dmaf(self, q, fn, reads=(), writes=()):
        i = self.dpos[q]
        self.dpos[q] = (i + 1) % self.NDS
        key = ('d', q, i)
        prev = self.dcount[q][i]
        self.dcount[q][i] += 16
        ev = (key, self.dcount[q][i])
        waits = self._deps(q, reads, writes, ev, False)
        seen = self.seen[q]
        if prev > 0 and seen.get(key, 0) < prev:
            seen[key] = prev
            waits.append((key, prev))
        self.ins[q].append((fn, waits, ev, 16))
        return ev

    def dma(self, q, out, in_, reads=(), writes=(), **kw):
        return self.dmaf(q, (lambda e: e.dma_start(out=out, in_=in_, **kw)), reads, writes)

    def barrier(self):
        cur = {}
        for e in self.CENG:
            n = self.ccount[e]
            if n:
                ep = (n - 1) // self.EPOCH
                cur[('c', e, ep)] = (n - 1) % self.EPOCH + 1
                if ep > 0:
                    cur[('c', e, ep - 1)] = self.EPOCH
        for q in self.DQ:
            for i in range(self.NDS):
                if self.dcount[q][i]:
                    cur[('d', q, i)] = self.dcount[q][i]
        for e in self.ENG:
            waits = []
            for key, val in cur.items():
                if key[0] == 'c' and key[1] == 'pe' and e == 'pe':
                    continue
                if self.seen[e].get(key, 0) < val:
                    self.seen[e][key] = val
                    waits.append((key, val))
            if waits:
                self.ins[e].append((None, waits, None, 0))
        self.last_w = {}
        self.readers = {}

    def emit(self):
        nc = self.nc
        self.barrier()

        def run(engname, engobj):
            for fn, waits, ev, inc in self.ins[engname]:
                for key, val in waits:
                    engobj.wait_ge(self.semobj[key], val)
                if fn is None:
                    continue
                inst = fn(engobj)
                inst.then_inc(self.semobj[ev[0]], inc)

        with nc.Block() as block:
            @block.tensor
            def _(e):
                run('pe', e)

            @block.scalar
            def _(e):
                run('act', e)

            @block.vector
            def _(e):
                run('dve', e)

            @block.gpsimd
            def _(e):
                run('pool', e)

            @block.sync
            def _(e):
                run('sp', e)


def build(n_pool, stage=99):
    nc = bass.Bass("TRN2", target_bir_lowering=False)

    def din(name, shape, dt=F32):
        return nc.dram_tensor(name, list(shape), dt, kind="ExternalInput").ap()

    def dout(name, shape, dt=F32):
        return nc.dram_tensor(name, list(shape), dt, kind="ExternalOutput").ap()

    xall = din("xall", [4096, D])
    xsm = din("xsm", [64, D])
    memp = din("memp", [256, D])
    w_in = din("w_in", [D, IN_COLS])
    w_out = din("w_out", [D, D])
    w_mq = din("w_mq", [D, D])
    w_mk = din("w_mk", [D, D])
    w_mv = din("w_mv", [D, D])
    w_mo = din("w_mo", [D, D])
    w_up = din("w_up", [D, 2 * DFF])
    w_down = din("w_down", [DFF, D])
    gpre = din("gpre", [128, 4, 8])
    gpost = din("gpost", [3, D])
    ghead = din("ghead", [2, 512])
    bif = din("bif", [4, 2])
    dlam = din("dlam", [1, 256])
    gffn_d = din("gffn", [1, D])
    cwa = din("cwa", [128, NCF, 4])
    cwg = din("cwg", [128, NCF, 4])
    ckp = din("ckp", [n_pool * 128, 512])
    cvp = din("cvp", [n_pool * 128, 512])
    ptab = din("ptab", [1, 256], I32)
    cmk = din("cmk", [16, 256, D])
    cmv = din("cmv", [16, 256, D])
    stC = din("stC", [16, 4, 128, 128])
    stn = din("stn", [16, 4, 128])
    stm = din("stm", [16, 4])
    stcv = din("stcv", [16, 2, 2 * DFF])
    ident_d = din("ident", [128, 128])
    tril_d = din("tril", [128, 128])
    ropeC_d = din("ropeC", [4096, 64])
    ropeS_d = din("ropeS", [4096, 64])
    ropeCs_d = din("ropeCs", [64, 64])
    ropeSs_d = din("ropeSs", [64, 64])
    kbias_d = din("kbias", [128, NB])
    gvalid_d = din("gvalid", [128, 128])
    lt_d = din("lt", [128, 128])
    hflag_d = din("hflag", [128, 1])
    iota_d = din("iota", [128, 1])
    bm_d = din("bm", [64, 64])
    bmask_d = din("bmask", [64, 16])
    sel_d = din("sel", [4, 4, 128])
    bmn_d = din("bmn", [64, 64])
    bmT_d = din("bmT", [16, 64])
    bifr_d = din("bifr", [1, 8])

    y_p = dout("y_p", [2048, D])
    p_dk = dout("p_dk", [2048, 512])
    p_dv = dout("p_dv", [2048, 512])
    p_mk = dout("p_mk", [256, D])
    p_mv = dout("p_mv", [256, D])
    p_C = dout("p_C", [4, 128, 128])
    p_n = dout("p_n", [4, 128])
    p_m = dout("p_m", [4, 1])
    p_cv = dout("p_cv", [2, 2 * DFF])
    y_s = dout("y_s", [64, D])
    s_dk = dout("s_dk", [64, 512])
    s_dv = dout("s_dv", [64, 512])
    s_C = dout("s_C", [16, 4, 128, 128])
    s_n = dout("s_n", [16, 4, 128])
    s_m = dout("s_m", [16, 4])
    s_cv = dout("s_cv", [16, 2, 2 * DFF])

    with ExitStack() as es:
        P = Prog(nc, es)

        def sb(name, shape, dt=F32):
            return es.enter_context(nc.sbuf_tensor("sb_" + name, list(shape), dt))

        PS = [es.enter_context(nc.psum_tensor("ps%d" % i, [128, 512], F32)) for i in range(8)]
        ARENA_W = 39300
        arena = sb("arena", [128, ARENA_W])
        apos = [0]
        amax = [0]

        def ar(name, shape, dt=F32):
            n = 1
            for d_ in shape[1:]:
                n *= d_
            words = n if dt in (F32, I32) else (n + 1) // 2
            o = apos[0]
            apos[0] += words
            amax[0] = max(amax[0], apos[0])
            assert apos[0] <= ARENA_W, (name, apos[0])
            v = arena[0:shape[0], o:o + words]
            if dt != F32:
                v = v.bitcast(dt)[:, 0:n]
            if len(shape) == 3:
                v = v.rearrange("p (a b) -> p a b", a=shape[1])
            elif len(shape) == 4:
                v = v.rearrange("p (a b c) -> p a b c", a=shape[1], b=shape[2])
            return v

        def act(out, in_, func, reads, writes, **kw):
            return P.op('act', lambda e: e.activation(out=out, in_=in_, func=func, **kw), reads, writes)

        def mm(out, lhsT, rhs, start, stop, reads, writes):
            return P.op('pe', lambda e: e.matmul(out, lhsT=lhsT, rhs=rhs, start=start, stop=stop), reads, writes)

        def tr(out, in_, idn, reads, writes):
            return P.op('pe', lambda e: e.transpose(out=out, in_=in_, identity=idn), reads, writes)

        def cp(eng, out, in_, reads, writes):
            if eng == 'act':
                return act(out, in_, AF.Copy, reads, writes)
            return P.op(eng, lambda e: e.tensor_copy(out=out, in_=in_), reads, writes)

        def ts(eng, out, in0, s1, s2, op0, op1, reads, writes):
            if op1 is None:
                return P.op(eng, lambda e: e.tensor_scalar(out=out, in0=in0, scalar1=s1, scalar2=None, op0=op0),
                            reads, writes)
            return P.op(eng, lambda e: e.tensor_scalar(out=out, in0=in0, scalar1=s1, scalar2=s2, op0=op0, op1=op1),
                        reads, writes)

        def tt(eng, out, in0, in1, op, reads, writes):
            return P.op(eng, lambda e: e.tensor_tensor(out=out, in0=in0, in1=in1, op=op), reads, writes)

        def stt(eng, out, in0, scalar, in1, op0, op1, reads, writes):
            return P.op(eng, lambda e: e.scalar_tensor_tensor(out=out, in0=in0, scalar=scalar, in1=in1,
                                                              op0=op0, op1=op1), reads, writes)

        def ms(eng, ap, val, writes):
            return P.op(eng, lambda e: e.memset(ap, val), (), writes)

        identf = sb("identf", [128, 128])
        identb = sb("identb", [128, 128], BF16)
        trilf = sb("trilf", [128, 128])
        trilb = sb("trilb", [128, 128], BF16)
        tril2b = sb("tril2b", [128, 256], BF16)
        epsc = sb("epsc", [128, 1])
        onec = sb("onec", [128, 1])
        lncc = sb("lncc", [128, 1])
        kbias = sb("kbias", [128, NB])
        hflag = sb("hflag", [128, 1])
        gpre_t = sb("gpre_t", [128, 4, 8])
        ghead_t = sb("ghead_t", [128, 2, 512])
        bif_t = sb("bif_t", [4, 2])
        nbf = sb("nbf", [4, 1])
        lam_t = sb("lam_t", [128, 1])
        dl_t = sb("dl_t", [128, 256])
        dl_p = sb("dl_p", [128, 128])
        dl_s = sb("dl_s", [128, 2])
        sel_t = sb("sel_t", [4, 4, 128])
        P.dma('sp', identf[:], ident_d, writes=['identf'])
        P.dma('sp', trilf[:], tril_d, writes=['trilf'])
        P.dma('sp', kbias[:], kbias_d, writes=['kbias'])
        P.dma('sp', hflag[:], hflag_d, writes=['hflag'])
        P.dma('sp', gpre_t[:], gpre, writes=['gpre'])
        P.dma('sp', bif_t[:], bif, writes=['bif'])
        P.dma('sp', sel_t[:], sel_d.rearrange("k h m -> k h m"), writes=['sel'])
        for i in range(2):
            P.dma('sp', ghead_t[:, i, :], ghead[i:i + 1, :].partition_broadcast(128), writes=['ghead'])
        P.dma('sp', dl_t[:], dlam.partition_broadcast(128), writes=['dl_t'])
        cp('dve', identb[:], identf[:], ['identf'], ['identb'])
        cp('dve', trilb[:], trilf[:], ['trilf'], ['trilb'])
        cp('dve', tril2b[:, 0:128], trilf[:], ['trilf'], ['tril2b'])
        cp('dve', tril2b[:, 128:256], trilf[:], ['trilf'], ['tril2b'])
        ms('pool', epsc[:], EPS, ['epsc'])
        ms('pool', onec[:], 1.0, ['onec'])
        ms('pool', lncc[:], LNC, ['lncc'])
        ts('dve', nbf[:], bif_t[:, 1:2], -1.0, None, ALU.mult, None, ['bif'], ['nbf'])
        dl3 = dl_t[:].rearrange("p (a b) -> p a b", a=4)
        tt('dve', dl_p[:, 0:64], dl3[:, 0, :], dl3[:, 1, :], ALU.mult, ['dl_t'], ['dl_p'])
        tt('dve', dl_p[:, 64:128], dl3[:, 2, :], dl3[:, 3, :], ALU.mult, ['dl_t'], ['dl_p'])
        P.op('dve', lambda e: e.tensor_reduce(out=dl_s[:], in_=dl_p[:].rearrange("p (a b) -> p a b", a=2),
                                              axis=AX.X, op=ALU.add), ['dl_p'], ['dl_s'])
        act(dl_s[:], dl_s[:], AF.Exp, ['dl_s'], ['dl_s'])
        tt('dve', lam_t[:], dl_s[:, 0:1], dl_s[:, 1:2], ALU.subtract, ['dl_s'], ['lam'])
        ts('dve', lam_t[:], lam_t[:], LAM_INIT, None, ALU.add, None, ['lam'], ['lam'])
        ts('dve', ghead_t[:, 1, :], ghead_t[:, 1, :], 1.0 - LAM_INIT, None, ALU.mult, None, ['ghead'], ['ghead'])

        mixT = sb("mixT", [128, 8, 17 * 128], BF16)
        mixTs = sb("mixTs", [128, 8, 64], BF16)
        Cst = sb("Cst", [128, 4, 132])
        Cbf = sb("Cbf", [128, 4, 136], BF16)
        colsT = sb("colsT", [128, 3, 4, NB])
        decb = sb("decb", [128, 4, NB])
        ssn = sb("ssn", [128, 2])
        dn = sb("dn", [128, 4])
        hss = sb("hss", [128, 2])
        ms('pool', Cst[:], 0.0, ['Cst'])
        ms('pool', Cbf[:], 0.0, ['Cbf'])
        PST = PS[0]
        C_ML = 0
        C_DA = 2056
        NML = 2056
        NDA = 1536

        def load_w(dst, src_cols0, ncols, gidx, nm):
            hw = (ncols + 1) // 2
            st = [ar("%s_st%d" % (nm, i), [128, hw]) for i in range(2)]
            for kc in range(8):
                for s in range(2):
                    c0 = s * hw
                    c1 = min(ncols, c0 + hw)
                    P.dma('sp', st[s][:, 0:c1 - c0], w_in[kc * 128:(kc + 1) * 128, src_cols0 + c0:src_cols0 + c1],
                          writes=[(nm + 'st', s)])
                    if s == 0:
                        act(dst[:, kc, c0:c1], st[s][:, 0:c1 - c0], AF.Copy, [(nm + 'st', s), 'gpre'], [(nm, kc, s)],
                            scale=gpre_t[:, gidx, kc:kc + 1])
                    else:
                        ts('pool', dst[:, kc, c0:c1], st[s][:, 0:c1 - c0], gpre_t[:, gidx, kc:kc + 1], None,
                           ALU.mult, None, [(nm + 'st', s), 'gpre'], [(nm, kc, s)])
            return [(nm, kc, j) for kc in range(8) for j in range(2)]

        def norm_bufs():
            xt = [ar("xt%d" % i, [128, D]) for i in range(2)]
            xsq = ar("xsq", [128, D], BF16)
            xsb = ar("xsb", [128, D], BF16)
            hT = [ar("hT%d" % i, [128, 8, 128], BF16) for i in range(2)]
            return xt, xsq, xsb, hT
        nctr = [0]

        def norm_T(nb, src, n, dst, dstname):
            xt, xsq, xsb, _ = nb
            s = nctr[0] % 2
            nctr[0] += 1
            P.dma('sp', xt[s][0:n, :], src, writes=[('xt', s)])
            act(xsq[0:n, :], xt[s][0:n, :], AF.Square, [('xt', s)], ['xsq', 'ssn'], accum_out=ssn[0:n, 0:1])
            act(ssn[0:n, 1:2], ssn[0:n, 0:1], AF.Ln, ['ssn', 'epsc'], ['ssn'], scale=1.0 / D, bias=epsc[0:n, :])
            act(ssn[0:n, 1:2], ssn[0:n, 1:2], AF.Exp, ['ssn'], ['ssn'], scale=-0.5)
            ts('dve', xsb[0:n, :], xt[s][0:n, :], ssn[0:n, 1:2], None, ALU.mult, None,
               [('xt', s), 'ssn'], ['xsb'])
            pT = PST[:].bitcast(BF16)
            for kc in range(8):
                tr(pT[:, kc * 128:kc * 128 + n], xsb[0:n, kc * 128:(kc + 1) * 128], identb[0:n, 0:n],
                   ['xsb', 'identb'], ['PST'])
            cp('act', dst, pT[:, :].rearrange("p (a b) -> p a b", a=8)[:, :, 0:n], ['PST'], [dstname])

        def rstd_of(ssap, n, width):
            act(ssap[:, 1:2], ssap[:, 0:1], AF.Ln, ['hss', 'epsc'], ['hss'], scale=1.0 / width, bias=epsc[0:n, :])
            act(ssap[:, 1:2], ssap[:, 1:2], AF.Exp, ['hss'], ['hss'], scale=-0.5)

        apos[0] = 0
        wml = ar("wml", [128, 8, NML], BF16)
        WML = load_w(wml, C_ML, NML, 0, 'wml')
        P.barrier()
        apos[0] = NML * 4
        nb = norm_bufs()
        hT = nb[3]
        gsc = nc.dram_tensor("gsc", [2, 4, 4096], F32, kind="Internal").ap()
        gst = [ar("gst%d" % i, [4, 512]) for i in range(2)]
        GI = ar("GI", [128, 128])
        GF = ar("GF", [128, 128])
        gval = ar("gval", [128, 128])
        gneg = ar("gneg", [128, 128])
        gzero = ar("gzero", [128, 128])
        bgt = ar("bgt", [128, 128])
        agt = ar("agt", [128, 128])
        cml = ar("cml", [128, 128])
        e1t = ar("e1t", [128, 128])
        e2t = ar("e2t", [128, 128])
        flt = ar("flt", [128, 128])
        LT = ar("LT", [128, 128])
        decm = ar("decm", [128, 128])
        gcol = ar("gcol", [128, 8])
        grow = ar("grow", [1, 3, 128])
        mfin = ar("mfin", [128, 1])
        P.dma('sp', gval, gvalid_d, writes=['gval'])
        P.dma('sp', LT, lt_d, writes=['LT'])
        ms('pool', gzero, 0.0, ['gzero'])
        ts('dve', gneg, gval, -1.0, BIG, ALU.add, ALU.mult, ['gval'], ['gneg'])
        PSG = PS[1]
        PSG2 = PS[2]
        for kb in range(NB if 'mlpass' not in SKIP else 0):
            s = kb % 2
            norm_T(nb, xall[kb * 128:(kb + 1) * 128, :], 128, hT[s], ('hT', s))
            q = kb % 4
            for kc in range(8):
                mm(PSG[0:4, q * 128:(q + 1) * 128], wml[:, kc, 2048:2052], hT[s][:, kc, :], kc == 0, kc == 7,
                   [('hT', s)] + WML, ['PSG'])
            for kc in range(8):
                mm(PSG2[0:4, q * 128:(q + 1) * 128], wml[:, kc, 2052:2056], hT[s][:, kc, :], kc == 0, kc == 7,
                   [('hT', s)] + WML, ['PSG2'])
            if q == 3:
                t0 = (kb - 3) * 128
                act(gst[0], PSG[0:4, :], AF.Identity, ['PSG', 'bif'], [('gst', 0)], bias=bif_t[:, 0:1])
                act(gst[1], PSG2[0:4, :], AF.Exp, ['PSG2', 'nbf'], [('gst', 1)], scale=-1.0, bias=nbf[:, 0:1])
                P.dma('sp', gsc[0, :, t0:t0 + 512], gst[0], reads=[('gst', 0)], writes=[('gsc', 0, kb)])
                P.dma('sp', gsc[1, :, t0:t0 + 512], gst[1], reads=[('gst', 1)], writes=[('gsc', 1, kb)])
        P.dma('sp', GI, gsc[0].rearrange("h (k t) -> (h k) t", t=128), reads=[('gsc', 0, k_) for k_ in range(3, NB, 4)], writes=['GI'])
        P.dma('sp', GF, gsc[1].rearrange("h (k t) -> (h k) t", t=128), reads=[('gsc', 1, k_) for k_ in range(3, NB, 4)], writes=['GF'])
        act(GF, GF, AF.Ln, ['GF', 'onec'], ['GF'], bias=onec[:, :])
        stt('dve', GF, GF, -1.0, gval, ALU.mult, ALU.mult, ['GF', 'gval'], ['GF'])
        P.op('dve', lambda e: e.tensor_tensor_scan(out=bgt, data0=GF, data1=gzero, initial=0.0,
                                                   op0=ALU.add, op1=ALU.add), ['GF', 'gzero'], ['bgt'])
        cp('dve', gcol[:, 0:1], bgt[:, 127:128], ['bgt'], ['gcol0'])
        mm(PSG[:, 0:1], LT, gcol[:, 0:1], True, True, ['LT', 'gcol0'], ['PSG'])
        cp('dve', gcol[:, 1:2], PSG[:, 0:1], ['PSG'], ['gcol1'])
        ts('dve', bgt, bgt, gcol[:, 1:2], None, ALU.add, None, ['bgt', 'gcol1'], ['bgt'])
        tt('dve', GI, GI, gval, ALU.mult, ['GI', 'gval'], ['GI'])
        tt('dve', GI, GI, gneg, ALU.add, ['GI', 'gneg'], ['GI'])
        tt('dve', agt, GI, bgt, ALU.subtract, ['GI', 'bgt'], ['agt'])
        P.op('dve', lambda e: e.tensor_tensor_scan(out=cml, data0=agt, data1=agt, initial=-1e30,
                                                   op0=ALU.max, op1=ALU.max), ['agt'], ['cml'])
        cp('dve', gcol[:, 2:3], cml[:, 127:128], ['cml'], ['gcol2'])
        mm(PSG[0:1, 128:256], gcol[:, 2:3], identf[:], True, True, ['gcol2', 'identf'], ['PSG'])
        cp('dve', grow[:, 0, :], PSG[0:1, 128:256], ['PSG'], ['grow0'])
        ms('pool', grow[:, 2, :], 0.0, ['grow2'])
        for h in range(4):
            P.op('dve', lambda e, h=h: e.tensor_tensor_scan(
                out=grow[:, 1, h * NB:(h + 1) * NB], data0=grow[:, 0, h * NB:(h + 1) * NB],
                data1=grow[:, 0, h * NB:(h + 1) * NB], initial=0.0, op0=ALU.max, op1=ALU.max),
                ['grow0'], ['grow1'])
            cp('dve', grow[:, 2, h * NB + 1:(h + 1) * NB], grow[:, 1, h * NB:(h + 1) * NB - 1], ['grow1', 'grow2'],
               ['grow2'])
        mm(PSG[:, 256:257], grow[:, 2, :], onec[0:1, 0:1], True, True, ['grow2', 'onec'], ['PSG'])
        mm(PSG[:, 257:258], grow[:, 1, :], onec[0:1, 0:1], True, True, ['grow1', 'onec'], ['PSG'])
        cp('dve', gcol[:, 3:5], PSG[:, 256:258], ['PSG'], ['gcol34'])
        ts('dve', gcol[:, 5:6], gcol[:, 3:4], -1.0, LNC, ALU.mult, ALU.add, ['gcol34'], ['gcol5'])
        ts('dve', gcol[:, 6:7], gcol[:, 4:5], -1.0, LNC, ALU.mult, ALU.add, ['gcol34'], ['gcol6'])
        ts('dve', gcol[:, 7:8], gcol[:, 3:4], -1.0, None, ALU.mult, None, ['gcol34'], ['gcol7'])
        act(e1t, agt, AF.Exp, ['agt', 'gcol5'], ['e1t'], bias=gcol[:, 5:6])
        act(e2t, agt, AF.Exp, ['agt', 'gcol6'], ['e2t'], bias=gcol[:, 6:7])
        act(flt, bgt, AF.Exp, ['bgt', 'gcol7'], ['flt'], scale=-1.0, bias=gcol[:, 7:8])
        tt('dve', mfin, bgt[:, 127:128], gcol[:, 4:5], ALU.add, ['bgt', 'gcol34'], ['mfin'])
        for h in range(4):
            P.dma('sp', p_m[h:h + 1, :], mfin[h * NB + NB - 1:h * NB + NB, :], reads=['mfin'])
        tt('dve', gcol[:, 0:1], gcol[:, 3:4], gcol[:, 4:5], ALU.subtract, ['gcol34', 'gcol0'], ['gcol0'])
        act(gcol[:, 0:1], gcol[:, 0:1], AF.Exp, ['gcol0'], ['gcol0'])
        ts('dve', decm, identf[:], 0.0, gcol[:, 0:1], ALU.mult, ALU.add, ['identf', 'gcol0'], ['decm'])
        mm(PSG2[:, 0:128], decm, identf[:], True, True, ['decm', 'identf'], ['PSG2'])
        cp('dve', decb[:].rearrange("p a b -> p (a b)"), PSG2[:, 0:128], ['PSG2'], ['decb'])
        PSC = PS[3]
        for j, r in enumerate((e1t, e2t, flt)):
            mm(PSC[:, j * 128:(j + 1) * 128], r, identf[:], True, True, ['e1t', 'e2t', 'flt', 'identf'], ['PSC'])
        cp('dve', colsT[:].rearrange("p a b c -> p (a b c)"), PSC[:, 0:384], ['PSC'], ['cols'])
        P.barrier()

        if stage == 1:
            P.emit()
            return nc
        apos[0] = NML * 4
        nb = norm_bufs()
        hT = nb[3]
        kTM = ar("kTM", [128, 512], BF16)
        wk2 = ar("wk2", [128, 4, 128], BF16)
        v1 = ar("v1", [128, 4, 136], BF16)
        osig = ar("osig", [128, 512])
        qTm = ar("qTm", [128, 4, 128], BF16)
        kTm = ar("kTm", [128, 4, 128], BF16)
        Sp = ar("Sp", [128, 128], BF16)
        hh = ar("hh", [128, 128])
        hsq = ar("hsq", [128, 128])
        mixTM = ar("mixTM", [128, 512], BF16)
        ms('pool', v1.rearrange("p a b -> p (a b)"), 1.0, ['v1'])
        PZ = [PS[1], PS[2]]
        PF = PS[3]
        PM = PS[7]
        PMS = PS[4]
        PMN = PS[5]
        zc = [0]

        def proj_tm(wv, wnames, hTs, hname, c0, n=128):
            z = zc[0] % 2
            zc[0] += 1
            for kc in range(8):
                mm(PZ[z][0:n, :], hTs[:, kc, 0:n], wv[:, kc, c0:c0 + 512], kc == 0, kc == 7,
                   [hname] + wnames, [('PZ', z)])
            return z

        for kb in range(NB if 'mlpass' not in SKIP else 0):
            if stage == 15 and kb == 15:
                break
            win = kb >= W0
            wb = kb - W0
            s = kb % 2
            norm_T(nb, xall[kb * 128:(kb + 1) * 128, :], 128, hT[s], ('hT', s))
            hn = ('hT', s)
            z = proj_tm(wml, WML, hT[s], hn, 512)
            cp('act', kTM, PZ[z][:], [('PZ', z)], ['kTM'])
            for h in range(4):
                ts('dve', wk2[:, h, :], kTM[:, h * 128:(h + 1) * 128], colsT[:, 1, h, kb:kb + 1], None, ALU.mult, None,
                   ['kTM', 'cols'], ['wk2'])
            z = proj_tm(wml, WML, hT[s], hn, 1024)
            cp('act', v1[:, :, 0:128], PZ[z][:].rearrange("p (a b) -> p a b", a=4), [('PZ', z)], ['v1'])
            if win and 'osig' not in SKIP:
                z = proj_tm(wml, WML, hT[s], hn, 1536)
                act(osig, PZ[z][:], AF.Exp, [('PZ', z)], ['osig'], scale=-1.0)
                ts('dve', osig, osig, 1.0, None, ALU.add, None, ['osig'], ['osig'])
                P.op('dve', lambda e: e.reciprocal(out=osig, in_=osig), ['osig'], ['osig'])
                tt('dve', osig, osig, ghead_t[:, 0, :], ALU.mult, ['osig', 'ghead'], ['osig'])
                for (c0, dst, nm) in (((0, qTm, 'qTm'), (512, kTm, 'kTm')) if 'fm' not in SKIP else ()):
                    for h in range(4):
                        for kc in range(8):
                            mm(PF[:, h * 128:(h + 1) * 128], wml[:, kc, c0 + h * 128:c0 + (h + 1) * 128],
                               hT[s][:, kc, :], kc == 0, kc == 7, [hn] + WML, ['PF'])
                    cp('act', dst.rearrange("p a b -> p (a b)"), PF[:], ['PF'], [nm])
            for h in range(4):
                if win and 'out' not in SKIP:
                    mm(PMS[:, 0:128], kTm[:, h, :], qTm[:, h, :], True, True, ['kTm', 'qTm'], ['PM0'])
                    stt('dve', Sp, PMS[:, 0:128], colsT[:, 0, h, kb:kb + 1], trilf[:], ALU.mult, ALU.mult,
                        ['PM0', 'cols', 'trilf'], ['Sp'])
                    mm(PMN[:, 0:129], qTm[:, h, :], Cbf[:, h, 0:129], True, False, ['qTm', ('Cbf', h)], ['PM1'])
                    mm(PMN[:, 0:129], Sp, v1[:, h, 0:129], False, True, ['Sp', 'v1'], ['PM1'])
                    act(dn[:, 0:1], PMN[:, 128:129], AF.Abs, ['PM1'], ['dn'])
                    ts('dve', dn[:, 0:1], dn[:, 0:1], colsT[:, 2, h, kb:kb + 1], None, ALU.max, None,
                       ['dn', 'cols'], ['dn'])
                    P.op('dve', lambda e: e.reciprocal(out=dn[:, 1:2], in_=dn[:, 0:1]), ['dn'], ['dn'])
                    act(hh, PMN[:, 0:128], AF.Copy, ['PM1', 'dn'], ['hh'], scale=dn[:, 1:2])
                    act(hsq, hh, AF.Square, ['hh'], ['hsq', 'hss'], accum_out=hss[:, 0:1])
                    rstd_of(hss[:], 128, 128)
                    stt('dve', mixTM[:, h * 128:(h + 1) * 128], hh, hss[:, 1:2], osig[:, h * 128:(h + 1) * 128],
                        ALU.mult, ALU.mult, ['hh', 'hss', 'osig'], ['mixTM'])
                mm(PM[:, 0:129], wk2[:, h, :], v1[:, h, 0:129], True, True, ['wk2', 'v1'], ['PM2'])
                stt('dve', Cst[:, h, 0:129], Cst[:, h, 0:129], decb[:, h, kb:kb + 1], PM[:, 0:129], ALU.mult, ALU.add,
                    ['Cst', 'decb', 'PM2'], ['Cst'])
                cp('act', Cbf[:, h, 0:129], Cst[:, h, 0:129], ['Cst'], [('Cbf', h)])
            if win and 'trans' not in SKIP:
                pT = PST[:].bitcast(BF16)
                for kc in range(4):
                    tr(pT[:, kc * 128:(kc + 1) * 128], mixTM[:, kc * 128:(kc + 1) * 128], identb[:],
                       ['mixTM', 'identb'], ['PST'])
                cp('act', mixT[:, 0:4, wb * 128:(wb + 1) * 128], pT[:, 0:512].rearrange("p (a b) -> p a b", a=4),
                   ['PST'], [('mixT', wb, 0)])
        if 'sml' not in SKIP:
            hTs = ar("hTs", [128, 8, 64], BF16)
            C0all = ar("C0all", [128, 16, 4, 128])
            kTMs = ar("kTMs", [64, 512], BF16)
            v1s = ar("v1s", [64, 4, 136], BF16)
            osigs = ar("osigs", [64, 512])
            qTMs = ar("qTMs", [64, 512])
            qTs = ar("qTs", [128, 4, 64], BF16)
            kTs = ar("kTs", [128, 4, 64], BF16)
            qTf = ar("qTf", [128, 4, 64])
            gs = ar("gs", [64, 8])
            bifb = ar("bifb", [64, 8])
            lfs = ar("lfs", [64, 4])
            bs = ar("bs", [64, 4])
            a_s = ar("a_s", [64, 4])
            m0s = ar("m0s", [16, 4])
            n0s = ar("n0s", [16, 512])
            n0f = ar("n0f", [64, 128])
            m0x = ar("m0x", [64, 4])
            n0x = ar("n0x", [64, 512])
            G1x = ar("G1x", [64, 4])
            cmx = ar("cmx", [64, 4])
            dgs = ar("dgs", [64, 64])
            Ams = ar("Ams", [64, 64])
            e1x = ar("e1x", [64, 4])
            e2x = ar("e2x", [64, 4])
            flx = ar("flx", [64, 4])
            dcx = ar("dcx", [64, 4])
            mnx = ar("mnx", [64, 4])
            tb = ar("tb", [64, 4])
            qn = ar("qn", [64, 4])
            prod = ar("prod", [64, 512])
            bm_t = ar("bm_t", [64, 64])
            bmn_t = ar("bmn_t", [64, 64])
            bmask_t = ar("bmask_t", [64, 16])
            bmaskb = ar("bmaskb", [64, 16], BF16)
            bmT_t = ar("bmT_t", [16, 64])
            onesf = ar("onesf", [64, 128])
            decbs = ar("decbs", [128, 4, 64])
            Sps = ar("Sps", [64, 64], BF16)
            itf = ar("itf", [128, 64])
            itT = ar("itT", [64, 128])
            hhs = ar("hhs", [64, 128])
            wk2s = ar("wk2s", [64, 4, 128], BF16)
            vx = ar("vx", [64, 16, 128], BF16)
            Cout = [ar("Cout%d" % i, [128, 4, 128]) for i in range(2)]
            n0T = ar("n0T", [128, 64])
            nnT = ar("nnT", [128, 64])
            nout = ar("nout", [64, 128])
            mixTMs = ar("mixTMs", [64, 512], BF16)
            P.dma('sp', bm_t, bm_d, writes=['bm'])
            P.dma('sp', bmn_t, bmn_d, writes=['bmn'])
            P.dma('sp', bmask_t, bmask_d, writes=['bmask'])
            P.dma('sp', bmT_t, bmT_d, writes=['bmT'])
            P.dma('sp', m0s, stm, writes=['m0s'])
            P.dma('sp', n0s, stn.rearrange("b h d -> b (h d)"), writes=['n0s'])
            P.dma('sp', n0f, stn.rearrange("b h d -> (b h) d"), writes=['n0f'])
            P.dma('sp', bifb, bifr_d.partition_broadcast(64), writes=['bifb'])
            for b_ in range(16):
                P.dma('sp', C0all[:, b_, :, :], stC[b_].rearrange("h d e -> d h e"), writes=[('C0', b_)])
            ms('pool', onesf, 1.0, ['onesf'])
            ms('pool', v1s.rearrange("p a b -> p (a b)"), 1.0, ['v1s'])
            cp('dve', bmaskb, bmask_t, ['bmask'], ['bmaskb'])
            norm_T(nb, xsm, 64, hTs, 'hTs')
            z = proj_tm(wml, WML, hTs, 'hTs', 512, 64)
            cp('act', kTMs, PZ[z][0:64, :], [('PZ', z)], ['kTMs'])
            z = proj_tm(wml, WML, hTs, 'hTs', 1024, 64)
            cp('act', v1s[:, :, 0:128], PZ[z][0:64, :].rearrange("p (a b) -> p a b", a=4), [('PZ', z)], ['v1s'])
            z = proj_tm(wml, WML, hTs, 'hTs', 1536, 64)
            act(osigs, PZ[z][0:64, :], AF.Exp, [('PZ', z)], ['osigs'], scale=-1.0)
            ts('dve', osigs, osigs, 1.0, None, ALU.add, None, ['osigs'], ['osigs'])
            P.op('dve', lambda e: e.reciprocal(out=osigs, in_=osigs), ['osigs'], ['osigs'])
            tt('dve', osigs, osigs, ghead_t[0:64, 0, :], ALU.mult, ['osigs', 'ghead'], ['osigs'])
            z = proj_tm(wml, WML, hTs, 'hTs', 0, 64)
            cp('act', qTMs, PZ[z][0:64, :], [('PZ', z)], ['qTMs'])
            for (c0, dst, nm) in ((0, qTs, 'qTs'), (512, kTs, 'kTs')):
                for h in range(4):
                    for kc in range(8):
                        mm(PF[:, h * 64:(h + 1) * 64], wml[:, kc, c0 + h * 128:c0 + (h + 1) * 128], hTs[:, kc, :],
                           kc == 0, kc == 7, ['hTs'] + WML, ['PF'])
                cp('act', dst.rearrange("p a b -> p (a b)"), PF[:, 0:256], ['PF'], [nm])
                if nm == 'qTs':
                    cp('act', qTf.rearrange("p a b -> p (a b)"), PF[:, 0:256], ['PF'], ['qTf'])
            for kc in range(8):
                mm(PM[0:64, 0:8], hTs[:, kc, :], wml[:, kc, 2048:2056], kc == 0, kc == 7, ['hTs'] + WML, ['PM2'])
            tt('dve', gs, PM[0:64, 0:8], bifb, ALU.add, ['PM2', 'bifb'], ['gs'])
            act(lfs, gs[:, 4:8], AF.Exp, ['gs'], ['lfs'], scale=-1.0)
            act(lfs, lfs, AF.Ln, ['lfs', 'onec'], ['lfs'], bias=onec[0:64, :])
            ts('dve', lfs, lfs, -1.0, None, ALU.mult, None, ['lfs'], ['lfs'])
            mm(PM[0:64, 8:12], bm_t, lfs, True, True, ['bm', 'lfs'], ['PM2'])
            cp('dve', bs, PM[0:64, 8:12], ['PM2'], ['bs'])
            tt('dve', a_s, gs[:, 0:4], bs, ALU.subtract, ['gs', 'bs'], ['a_s'])
            mm(PM[0:64, 12:16], bmT_t, m0s, True, True, ['bmT', 'm0s'], ['PM2'])
            cp('dve', m0x, PM[0:64, 12:16], ['PM2'], ['m0x'])
            mm(PZ[0][0:64, :], bmT_t, n0s, True, True, ['bmT', 'n0s'], [('PZ', 0)])
            cp('act', n0x, PZ[0][0:64, :], [('PZ', 0)], ['n0x'])
            for h in range(4):
                ts('dve', dgs, identf[0:64, 0:64], a_s[:, h:h + 1], None, ALU.mult, None, ['identf', 'a_s'], ['dgs'])
                mm(PMS[0:64, 0:64], onesf[:, 0:64], dgs, True, True, ['onesf', 'dgs'], ['PM0'])
                tt('dve', Ams, PMS[0:64, 0:64], bmn_t, ALU.add, ['PM0', 'bmn'], ['Ams'])
                P.op('dve', lambda e, h=h: e.tensor_reduce(out=cmx[:, h:h + 1], in_=Ams, axis=AX.X, op=ALU.max),
                     ['Ams'], ['cmx'])
            tt('dve', G1x, cmx, m0x, ALU.max, ['cmx', 'm0x'], ['G1x'])
            tt('dve', tb, a_s, m0x, ALU.subtract, ['a_s', 'm0x'], ['tb'])
            act(e1x, tb, AF.Exp, ['tb', 'lncc'], ['e1x'], bias=lncc[0:64, :])
            tt('dve', tb, a_s, G1x, ALU.subtract, ['a_s', 'G1x', 'e1x'], ['tb'])
            act(e2x, tb, AF.Exp, ['tb', 'lncc'], ['e2x'], bias=lncc[0:64, :])
            tt('dve', tb, bs, m0x, ALU.add, ['bs', 'm0x', 'e2x'], ['tb'])
            act(flx, tb, AF.Exp, ['tb'], ['flx'], scale=-1.0)
            tt('dve', tb, m0x, G1x, ALU.subtract, ['m0x', 'G1x', 'flx'], ['tb'])
            act(dcx, tb, AF.Exp, ['tb'], ['dcx'])
            tt('dve', mnx, bs, G1x, ALU.add, ['bs', 'G1x'], ['mnx'])
            P.dma('sp', s_m, mnx[3:64:4, :], reads=['mnx'])
            tt('dve', prod, qTMs, n0x, ALU.mult, ['qTMs', 'n0x'], ['prod'])
            P.op('dve', lambda e: e.tensor_reduce(out=qn, in_=prod.rearrange("p (a b) -> p a b", a=4), axis=AX.X,
                                                  op=ALU.add), ['prod'], ['qn'])
            for h in range(4):
                ts('dve', dgs, identf[0:64, 0:64], dcx[:, h:h + 1], None, ALU.mult, None, ['identf', 'dcx'], ['dgs'])
                mm(PMS[:, 64:128], onesf, dgs, True, True, ['onesf', 'dgs'], ['PM0'])
                cp('dve', decbs[:, h, :], PMS[:, 64:128], ['PM0'], ['decbs'])
            mm(PMS[:, 128:192], n0f, identf[0:64, 0:64], True, True, ['n0f', 'identf'], ['PM0'])
            cp('dve', n0T, PMS[:, 128:192], ['PM0'], ['n0T'])
            for h in range(4):
                mm(PMS[0:64, 0:64], kTs[:, h, :], qTs[:, h, :], True, True, ['kTs', 'qTs'], ['PM0'])
                stt('dve', Sps, PMS[0:64, 0:64], e1x[:, h:h + 1], bm_t, ALU.mult, ALU.mult, ['PM0', 'e1x', 'bm'], ['Sps'])
                mm(PMN[0:64, 0:129], Sps, v1s[:, h, 0:129], True, True, ['Sps', 'v1s'], ['PM1'])
                for b_ in range(16):
                    mm(PF[:, b_ * 4:(b_ + 1) * 4], C0all[:, b_, h, :], qTf[:, h, b_ * 4:(b_ + 1) * 4], True, True,
                       [('C0', b_), 'qTf'], ['PF'])
                cp('act', itf, PF[:, 0:64], ['PF'], ['itf'])
                mm(PM[0:64, 0:128], itf, identf[:], True, True, ['itf', 'identf'], ['PM2'])
                cp('act', itT, PM[0:64, 0:128], ['PM2'], ['itT'])
                tt('dve', dn[0:64, 0:1], PMN[0:64, 128:129], qn[:, h:h + 1], ALU.add, ['PM1', 'qn'], ['dn'])
                act(dn[0:64, 0:1], dn[0:64, 0:1], AF.Abs, ['dn'], ['dn'])
                ts('dve', dn[0:64, 0:1], dn[0:64, 0:1], flx[:, h:h + 1], None, ALU.max, None, ['dn', 'flx'], ['dn'])
                P.op('dve', lambda e: e.reciprocal(out=dn[0:64, 1:2], in_=dn[0:64, 0:1]), ['dn'], ['dn'])
                tt('dve', hhs, PMN[0:64, 0:128], itT, ALU.add, ['PM1', 'itT'], ['hhs'])
                ts('dve', hhs, hhs, dn[0:64, 1:2], None, ALU.mult, None, ['hhs', 'dn'], ['hhs'])
                act(hsq[0:64, :], hhs, AF.Square, ['hhs'], ['hsq', 'hss'], accum_out=hss[0:64, 0:1])
                rstd_of(hss[0:64, :], 64, 128)
                stt('dve', mixTMs[:, h * 128:(h + 1) * 128], hhs, hss[0:64, 1:2], osigs[:, h * 128:(h + 1) * 128],
                    ALU.mult, ALU.mult, ['hhs', 'hss', 'osigs'], ['mixTMs'])
                ts('dve', wk2s[:, h, :], kTMs[:, h * 128:(h + 1) * 128], e2x[:, h:h + 1], None, ALU.mult, None,
                   ['kTMs', 'e2x'], ['wk2s'])
                tt('pool', vx, v1s[:, h, 0:128].unsqueeze(1).to_broadcast([64, 16, 128]),
                   bmaskb.unsqueeze(2).to_broadcast([64, 16, 128]), ALU.mult, ['v1s', 'bmaskb'], ['vx'])
                for g in range(4):
                    z = zc[0] % 2
                    zc[0] += 1
                    mm(PZ[z][:, :], wk2s[:, h, :], vx[:, g * 4:(g + 1) * 4, :].rearrange("p a b -> p (a b)"), True, True,
                       ['wk2s', 'vx'], [('PZ', z)])
                    co = Cout[g % 2]
                    for bb in range(4):
                        b_ = g * 4 + bb
                        stt('dve', co[:, bb, :], C0all[:, b_, h, :], decbs[:, h, b_ * 4:b_ * 4 + 1],
                            PZ[z][:, bb * 128:(bb + 1) * 128], ALU.mult, ALU.add,
                            [('C0', b_), 'decbs', ('PZ', z)], [('Cout', g % 2)])
                    P.dma('sp', s_C[g * 4:(g + 1) * 4, h].rearrange("b d e -> d b e"), co, reads=[('Cout', g % 2)])
                mm(PMS[:, 192 + h * 16:192 + (h + 1) * 16], wk2s[:, h, :], bmaskb, True, True, ['wk2s', 'bmaskb'], ['PM0'])
            n0T3 = n0T.rearrange("p (b h) -> p b h", h=4)
            nnT3 = nnT.rearrange("p (b h) -> p b h", h=4)
            for h in range(4):
                tt('dve', nnT3[:, :, h], n0T3[:, :, h], decbs[:, h, 0:64:4], ALU.mult, ['n0T', 'decbs'], ['nnT'])
                tt('dve', nnT3[:, :, h], nnT3[:, :, h], PMS[:, 192 + h * 16:192 + (h + 1) * 16], ALU.add, ['nnT', 'PM0'],
                   ['nnT'])
            mm(PM[0:64, 0:128], nnT, identf[:], True, True, ['nnT', 'identf'], ['PM2'])
            cp('dve', nout, PM[0:64, 0:128], ['PM2'], ['nout'])
            P.dma('sp', s_n.rearrange("b h d -> (b h) d"), nout, reads=['nout'])
            pT = PST[:].bitcast(BF16)
            for kc in range(4):
                tr(pT[:, kc * 128:kc * 128 + 64], mixTMs[:, kc * 128:(kc + 1) * 128], identb[0:64, 0:64],
                   ['mixTMs', 'identb'], ['PST'])
            cp('act', mixTs[:, 0:4, :], pT[:, 0:512].rearrange("p (a b) -> p a b", a=4)[:, :, 0:64], ['PST'], ['mixTs0'])
        for h in range(4):
            P.dma('sp', p_C[h], Cst[:, h, 0:128], reads=['Cst'])
            P.dma('sp', p_n[h:h + 1, :].rearrange("o d -> d o"), Cst[:, h, 128:129], reads=['Cst'])
        P.barrier()

        if stage in (2, 15):
            P.emit()
            return nc
        apos[0] = 0
        wda = ar("wda", [128, 8, NDA], BF16)
        KT = ar("KT", [128, 4, NB * 128], BF16)
        V1 = ar("V1", [128, NB, 4, 136], BF16)
        keep = apos[0]
        WDA = load_w(wda, C_DA, NDA, 0, 'wda')
        P.barrier()
        apos[0] = keep
        nb = norm_bufs()
        hT = nb[3]
        dkst = [ar("dkst%d" % i, [128, 512]) for i in range(2)]
        dvst = [ar("dvst%d" % i, [128, 512]) for i in range(2)]
        dqst = ar("dqst", [128, 512])
        rt = ar("rt", [128, 4, 64])
        dkb = ar("dkb", [128, 512], BF16)
        dqb = ar("dqb", [128, 512], BF16)
        Qblk = ar("Qblk", [128, 4, 256], BF16)
        Pt = [ar("Pt%d" % i, [128, 256], BF16) for i in range(2)]
        t1a = ar("t1a", [128, 128])
        aa = ar("aa", [128, 128])
        hsq = ar("hsq", [128, 128])
        mixTM = ar("mixTM", [128, 512], BF16)
        ms('pool', Qblk.rearrange("p a b -> p (a b)"), 0.0, ['Qblk'])
        PSSB = [PS[4], PS[3]]
        PO = [PS[5], PS[6]]

        rcs = [ar("rcs%d" % i, [128, 2, 64]) for i in range(2)]
        rctr = [0]

        def rope_load(tok0, n, src_c, src_s):
            i = rctr[0] % 2
            rctr[0] += 1
            P.dma('sp', rcs[i][0:n, 0, :], src_c[tok0:tok0 + n, :], writes=[('rcs', i)])
            P.dma('sp', rcs[i][0:n, 1, :], src_s[tok0:tok0 + n, :], writes=[('rcs', i)])
            return i

        def rope(buf, bname, i, n=128):
            if 'rope' in SKIP:
                return
            b3 = buf.rearrange("p (a b) -> p a b", b=64)
            x1 = b3[:, :, 0:8]
            x2 = b3[:, :, 8:16]
            cb = rcs[i][0:n, 0, :].rearrange("p (a b) -> p a b", b=8)
            sbb = rcs[i][0:n, 1, :].rearrange("p (a b) -> p a b", b=8)
            r = rt[0:n].rearrange("p a (b c) -> p a b c", c=8)
            rn = ('rcs', i)
            tt('dve', r[:, 0], x1, cb, ALU.mult, [bname, rn], ['rt0'])
            tt('pool', r[:, 1], x2, sbb, ALU.mult, [bname, rn], ['rt1'])
            tt('dve', r[:, 2], x2, cb, ALU.mult, [bname, rn], ['rt2'])
            tt('pool', r[:, 3], x1, sbb, ALU.mult, [bname, rn], ['rt3'])
            tt('dve', x1, r[:, 0], r[:, 1], ALU.subtract, ['rt0', 'rt1', 'rt3'], [bname])
            tt('pool', x2, r[:, 2], r[:, 3], ALU.add, ['rt2', 'rt3', 'rt0'], [bname])

        for kb in range(NB):
            win = kb >= W0
            own = kb > W0
            wb = kb - W0
            ob = kb - W0 - 1
            s = kb % 2
            norm_T(nb, xall[kb * 128:(kb + 1) * 128, :], 128, hT[s], ('hT', s))
            hn = ('hT', s)
            z = proj_tm(wda, WDA, hT[s], hn, 1024)
            for h_ in (range(4) if 'v1ms' not in SKIP else ()):
                ms('pool', V1[:, kb, h_, 128:130], 1.0, [('V1', kb)])
            cp('act', V1[:, kb, :, 0:128], PZ[z][:].rearrange("p (a b) -> p a b", a=4), [('PZ', z)], [('V1', kb)])
            if own and 'outd' not in SKIP:
                cp('act', dvst[s], PZ[z][:], [('PZ', z)], [('dvst', s)])
                P.dma('sp', p_dv[ob * 128:(ob + 1) * 128, :], dvst[s], reads=[('dvst', s)])
            z = proj_tm(wda, WDA, hT[s], hn, 512)
            cp('act', dkst[s], PZ[z][:], [('PZ', z)], [('dkst', s)])
            ri = rope_load(kb * 128, 128, ropeC_d, ropeS_d) if 'ropeld' not in SKIP else 0
            rope(dkst[s], ('dkst', s), ri)
            if own and 'outd' not in SKIP:
                P.dma('sp', p_dk[ob * 128:(ob + 1) * 128, :], dkst[s], reads=[('dkst', s)])
            cp('dve', dkb, dkst[s], [('dkst', s)], ['dkb'])
            pT = PST[:].bitcast(BF16)
            for h in range(4):
                tr(pT[:, h * 128:(h + 1) * 128], dkb[:, h * 128:(h + 1) * 128], identb[:], ['dkb', 'identb'], ['PST'])
            cp('act', KT[:, :, kb * 128:(kb + 1) * 128], pT[:, 0:512].rearrange("p (a b) -> p a b", a=4),
               ['PST'], [('KT', kb)])
            if not win or 'qproj' in SKIP:
                continue
            z = proj_tm(wda, WDA, hT[s], hn, 0)
            cp('act', dqst, PZ[z][:], [('PZ', z)], ['dqst'])
            rope(dqst, 'dqst', ri)
            cp('dve', dqb, dqst, ['dqst'], ['dqb'])
            for h in range(4):
                tr(pT[:, 512 + h * 128:512 + (h + 1) * 128], dqb[:, h * 128:(h + 1) * 128], identb[:],
                   ['dqb', 'identb'], ['PST'])
            pq = pT[:, 512:1024].rearrange("p (a b) -> p a b", a=4)
            cp('act', Qblk[0:64, :, 0:128], pq[0:64], ['PST'], ['Qblk'])
            cp('act', Qblk[64:128, :, 128:256], pq[64:128], ['PST'], ['Qblk'])
            for h in (range(4) if 'attn' not in SKIP else ()):
                pon = 'PO'

                class _PO3:
                    def __getitem__(self, key):
                        p_, c_, f_ = key
                        return PO[c_][p_, f_]
                po3 = _PO3()
                def _score(kk):
                    half = kk % 2
                    sc = PSSB[half][:, 0:256]
                    mm(sc, KT[:, h, kk * 128:(kk + 1) * 128], Qblk[:, h, :], True, True,
                       [('KT', kk), 'Qblk'], [('PSS', half)])
                    pt = Pt[half]
                    if kk < kb:
                        act(pt, sc, AF.Exp, [('PSS', half), 'kbias'], [('Pt', half)], scale=0.125,
                            bias=kbias[:, kk:kk + 1])
                    else:
                        act(pt, sc, AF.Exp, [('PSS', half)], [('Pt', half)], scale=0.125)
                        tt('pool', pt, pt, tril2b[:], ALU.mult, [('Pt', half), 'tril2b'], [('Pt', half)])

                def _pv(kk):
                    half = kk % 2
                    pt = Pt[half]
                    for c in range(2):
                        mm(po3[:, c, 0:129], pt[:, c * 128:(c + 1) * 128], V1[:, kk, h, 0:129], kk == 0, kk == kb,
                           [('Pt', half), ('V1', kk)], [pon])
                for kk in range(kb + 1):
                    _score(kk)
                    if kk >= 1:
                        _pv(kk - 1)
                _pv(kb)
                ts('dve', dn[:, 0:1], po3[:, 0, 128:129], 1e-30, None, ALU.max, None, [pon], ['dn'])
                ts('dve', dn[:, 2:3], po3[:, 1, 128:129], 1e-30, None, ALU.max, None, [pon], ['dn'])
                P.op('dve', lambda e: e.reciprocal(out=dn[:, 1:2], in_=dn[:, 0:1]), ['dn'], ['dn'])
                P.op('dve', lambda e: e.reciprocal(out=dn[:, 3:4], in_=dn[:, 2:3]), ['dn'], ['dn'])
                tt('dve', dn[:, 3:4], dn[:, 3:4], lam_t[:], ALU.mult, ['dn', 'lam'], ['dn'])
                act(t1a, po3[:, 1, 0:128], AF.Copy, [pon, 'dn'], ['t1a'], scale=dn[:, 3:4])
                stt('dve', aa, po3[:, 0, 0:128], dn[:, 1:2], t1a, ALU.mult, ALU.subtract,
                    [pon, 'dn', 't1a'], ['aa'])
                act(hsq, aa, AF.Square, ['aa'], ['hsq', 'hss'], accum_out=hss[:, 0:1])
                rstd_of(hss[:], 128, 128)
                stt('dve', mixTM[:, h * 128:(h + 1) * 128], aa, hss[:, 1:2],
                    ghead_t[:, 1, h * 128:(h + 1) * 128], ALU.mult, ALU.mult, ['aa', 'hss', 'ghead'], ['mixTM'])
            for kc in range(4):
                tr(pT[:, kc * 128:(kc + 1) * 128], mixTM[:, kc * 128:(kc + 1) * 128], identb[:],
                   ['mixTM', 'identb'], ['PST'])
            cp('act', mixT[:, 4:8, wb * 128:(wb + 1) * 128], pT[:, 0:512].rearrange("p (a b) -> p a b", a=4),
               ['PST'], [('mixT', wb, 1)])
        if 'sda' not in SKIP:
            P.barrier()
            apos[0] = NDA * 4
            hTs2 = ar("hTs2", [128, 8, 64], BF16)
            dvs = ar("dvs", [64, 512])
            dks = ar("dks", [64, 512])
            dqs = ar("dqs", [64, 512])
            vsb = ar("vsb", [64, 4, 128], BF16)
            dksb = ar("dksb", [64, 512], BF16)
            dqsb = ar("dqsb", [64, 512], BF16)
            KTs = ar("KTs", [128, 4, 64], BF16)
            Qbs = ar("Qbs", [128, 4, 2, 64], BF16)
            bm2 = ar("bm2", [64, 128], BF16)
            bmf = ar("bmf", [64, 64])
            Pself = ar("Pself", [64, 128], BF16)
            onesb2 = ar("onesb2", [128, 128], BF16)
            accS = ar("accS", [128, 4, 128])
            denS = ar("denS", [128, 4, 128])
            accP = ar("accP", [128, 16, 32])
            denP = ar("denP", [128, 16, 32])
            ptI = ar("ptI", [128, 256], I32)
            ptF = ar("ptF", [128, 256])
            idxI = ar("idxI", [128, 256], I32)
            iot = ar("iot", [128, 1])
            kpg = [ar("kpg%d" % i, [128, 512]) for i in range(2)]
            vpg = [ar("vpg%d" % i, [128, 512]) for i in range(2)]
            kpb2 = [ar("kpb%d" % i, [128, 512], BF16) for i in range(2)]
            vpb = [ar("vpb%d" % i, [128, 512], BF16) for i in range(2)]
            KTp2 = [ar("KTp%d" % i, [128, 4, 128], BF16) for i in range(2)]
            Pp2 = [ar("Pp%d" % i, [128, 32], BF16) for i in range(2)]
            numT = ar("numT", [128, 4, 2, 64])
            denT = ar("denT", [128, 4, 2, 64])
            aT = ar("aT", [128, 4, 64])
            aTM = ar("aTM", [64, 128])
            mixTMs2 = ar("mixTMs2", [64, 512], BF16)
            P.dma('sp', bmf, bm_d, writes=['bmf'])
            P.dma('sp', ptI, ptab.partition_broadcast(128), writes=['ptI'])
            P.dma('sp', iot, iota_d, writes=['iot'])
            cp('dve', bm2[:, 0:64], bmf, ['bmf'], ['bm2'])
            cp('dve', bm2[:, 64:128], bmf, ['bmf'], ['bm2'])
            ms('pool', onesb2, 1.0, ['onesb2'])
            ms('pool', Qbs.rearrange("p a b c -> p (a b c)"), 0.0, ['Qbs'])
            ms('pool', accP.rearrange("p a b -> p (a b)"), 0.0, ['accP'])
            ms('pool', denP.rearrange("p a b -> p (a b)"), 0.0, ['denP'])
            cp('dve', ptF, ptI, ['ptI'], ['ptF'])
            ts('dve', ptF, ptF, 128.0, iot[:, 0:1], ALU.mult, ALU.add, ['ptF', 'iot'], ['ptF'])
            cp('dve', idxI, ptF, ['ptF'], ['idxI'])
            norm_T(nb, xsm, 64, hTs2, 'hTs2')
            z = proj_tm(wda, WDA, hTs2, 'hTs2', 1024, 64)
            cp('act', dvs, PZ[z][0:64, :], [('PZ', z)], ['dvs'])
            P.dma('sp', s_dv, dvs, reads=['dvs'])
            cp('dve', vsb.rearrange("p a b -> p (a b)"), dvs, ['dvs'], ['vsb'])
            ri = rope_load(0, 64, ropeCs_d, ropeSs_d)
            z = proj_tm(wda, WDA, hTs2, 'hTs2', 512, 64)
            cp('act', dks, PZ[z][0:64, :], [('PZ', z)], ['dks'])
            rope(dks, 'dks', ri, 64)
            P.dma('sp', s_dk, dks, reads=['dks'])
            cp('dve', dksb, dks, ['dks'], ['dksb'])
            z = proj_tm(wda, WDA, hTs2, 'hTs2', 0, 64)
            cp('act', dqs, PZ[z][0:64, :], [('PZ', z)], ['dqs'])
            rope(dqs, 'dqs', ri, 64)
            cp('dve', dqsb, dqs, ['dqs'], ['dqsb'])
            pT = PST[:].bitcast(BF16)
            for h in range(4):
                tr(pT[:, h * 64:(h + 1) * 64], dksb[:, h * 128:(h + 1) * 128], identb[0:64, 0:64], ['dksb', 'identb'], ['PST'])
                tr(pT[:, 256 + h * 64:256 + (h + 1) * 64], dqsb[:, h * 128:(h + 1) * 128], identb[0:64, 0:64],
                   ['dqsb', 'identb'], ['PST'])
            cp('act', KTs.rearrange("p a b -> p (a b)"), pT[:, 0:256], ['PST'], ['KTs'])
            pq = pT[:, 256:512].rearrange("p (a b) -> p a b", a=4)
            cp('act', Qbs[0:64, :, 0, :], pq[0:64], ['PST'], ['Qbs'])
            cp('act', Qbs[64:128, :, 1, :], pq[64:128], ['PST'], ['Qbs'])
            for h in range(4):
                mm(PSSB[0][0:64, 0:128], KTs[:, h, :], Qbs[:, h, :, :].rearrange("p a b -> p (a b)"), True, True,
                   ['KTs', 'Qbs'], [('PSS', 0)])
                act(Pself, PSSB[0][0:64, 0:128], AF.Exp, [('PSS', 0)], ['Pself'], scale=0.125)
                tt('pool', Pself, Pself, bm2, ALU.mult, ['Pself', 'bm2'], ['Pself'])
                mm(PO[0][:, 0:128], vsb[:, h, :], Pself, True, True, ['vsb', 'Pself'], ['PO'])
                mm(PO[1][:, 0:128], onesb2[0:64, :], Pself, True, True, ['onesb2', 'Pself'], ['PO'])
                cp('act', accS[:, h, :], PO[0][:, 0:128], ['PO'], ['accS'])
                cp('act', denS[:, h, :], PO[1][:, 0:128], ['PO'], ['denS'])
            for b_ in range(16):
                for j in range(16):
                    pi = b_ * 16 + j
                    s2 = pi % 2
                    kpb, KTp, Pp = kpb2[s2], KTp2[s2], Pp2[s2]
                    kpbn, KTpn, Ppn = ('kpb', s2), ('KTp', s2), ('Pp', s2)

                    def gk(e, pi=pi, s2=s2):
                        return e.indirect_dma_start(out=kpg[s2], out_offset=None, in_=ckp,
                                                    in_offset=bass.IndirectOffsetOnAxis(ap=idxI[:, pi:pi + 1], axis=0))

                    def gv(e, pi=pi, s2=s2):
                        return e.indirect_dma_start(out=vpg[s2], out_offset=None, in_=cvp,
                                                    in_offset=bass.IndirectOffsetOnAxis(ap=idxI[:, pi:pi + 1], axis=0))
                    P.dmaf('pool', gk, reads=['idxI'], writes=[('kpg', s2)])
                    P.dmaf('pool', gv, reads=['idxI'], writes=[('vpg', s2)])
                    cp('dve', kpb, kpg[s2], [('kpg', s2)], [kpbn])
                    cp('act', vpb[s2], vpg[s2], [('vpg', s2)], [('vpb', s2)])
                    for h in range(4):
                        tr(pT[:, h * 128:(h + 1) * 128], kpb[:, h * 128:(h + 1) * 128], identb[:], [kpbn, 'identb'], ['PST'])
                    cp('act', KTp.rearrange("p a b -> p (a b)"), pT[:, 0:512], ['PST'], [KTpn])
                    for h in range(4):
                        mm(PSSB[1][:, h * 8:(h + 1) * 8], KTp[:, h, :], Qbs[:, h, :, b_ * 4:(b_ + 1) * 4], True, True,
                           [KTpn, 'Qbs'], [('PSS', 1)])
                    act(Pp, PSSB[1][:, 0:32], AF.Exp, [('PSS', 1)], [Ppn], scale=0.125)
                    for h in range(4):
                        mm(PO[0][:, h * 8:(h + 1) * 8], vpb[s2][:, h * 128:(h + 1) * 128], Pp[:, h * 8:(h + 1) * 8], True, True,
                           [('vpb', s2), Ppn], ['PO'])
                    mm(PO[1][:, 0:32], onesb2, Pp, True, True, ['onesb2', Ppn], ['PO'])
                    tt('dve', accP[:, b_, :], accP[:, b_, :], PO[0][:, 0:32], ALU.add, ['accP', 'PO'], ['accP'])
                    tt('dve', denP[:, b_, :], denP[:, b_, :], PO[1][:, 0:32], ALU.add, ['denP', 'PO'], ['denP'])
            accP5 = accP.rearrange("p b (h c q) -> p h c b q", h=4, c=2)
            denP5 = denP.rearrange("p b (h c q) -> p h c b q", h=4, c=2)
            for h in range(4):
                for c in range(2):
                    tt('dve', numT[:, h, c, :].rearrange("p (b q) -> p b q", q=4), accP5[:, h, c],
                       accS[:, h, c * 64:(c + 1) * 64].rearrange("p (b q) -> p b q", q=4), ALU.add, ['accP', 'accS'], ['numT'])
                    tt('dve', denT[:, h, c, :].rearrange("p (b q) -> p b q", q=4), denP5[:, h, c],
                       denS[:, h, c * 64:(c + 1) * 64].rearrange("p (b q) -> p b q", q=4), ALU.add, ['denP', 'denS'], ['denT'])
            nT2 = numT.rearrange("p a b c -> p (a b c)")
            dT2 = denT.rearrange("p a b c -> p (a b c)")
            P.op('dve', lambda e: e.reciprocal(out=dT2, in_=dT2), ['denT'], ['denT'])
            tt('dve', nT2, nT2, dT2, ALU.mult, ['numT', 'denT'], ['numT'])
            for h in range(4):
                stt('dve', aT[:, h, :], numT[:, h, 1, :], lam_t[:, 0:1], numT[:, h, 0, :], ALU.mult, ALU.subtract,
                    ['numT', 'lam'], ['aT'])
                mm(PSSB[0][0:64, 0:128], aT[:, h, :], identf[:], True, True, ['aT', 'identf'], [('PSS', 0)])
                act(aTM, PSSB[0][0:64, 0:128], AF.Copy, [('PSS', 0)], ['aTM'], scale=-1.0)
                act(hsq[0:64, :], aTM, AF.Square, ['aTM'], ['hsq', 'hss'], accum_out=hss[0:64, 0:1])
                rstd_of(hss[0:64, :], 64, 128)
                stt('dve', mixTMs2[:, h * 128:(h + 1) * 128], aTM, hss[0:64, 1:2], ghead_t[0:64, 1, h * 128:(h + 1) * 128],
                    ALU.mult, ALU.mult, ['aTM', 'hss', 'ghead'], ['mixTMs2'])
            for kc in range(4):
                tr(pT[:, kc * 128:kc * 128 + 64], mixTMs2[:, kc * 128:(kc + 1) * 128], identb[0:64, 0:64],
                   ['mixTMs2', 'identb'], ['PST'])
            cp('act', mixTs[:, 4:8, :], pT[:, 0:512].rearrange("p (a b) -> p a b", a=4)[:, :, 0:64], ['PST'], ['mixTs1'])
            assert apos[0] <= keep, apos[0]
        P.barrier()

        if stage == 3:
            P.emit()
            return nc
        apos[0] = 0
        w_outb = ar("w_outb", [128, 8, D], BF16)
        w_mqb = ar("w_mqb", [128, 8, D], BF16)
        w_mob = ar("w_mob", [128, 8, D], BF16)
        mkT = ar("mkT", [128, 8, 256], BF16)
        mvb = ar("mvb", [128, 2, D], BF16)
        gpost_t = ar("gpost_t", [128, 3, D])
        gffn_t = ar("gffn_t", [128, D])
        cwa_t = ar("cwa_t", [128, NCF, 4])
        cwg_t = ar("cwg_t", [128, NCF, 4])
        onesb = ar("onesb", [128, 128], BF16)
        keep2 = apos[0]
        for i in range(3):
            P.dma('sp', gpost_t[:, i, :], gpost[i:i + 1, :].partition_broadcast(128), writes=['gpost'])
        P.dma('sp', gffn_t, gffn_d.partition_broadcast(128), writes=['gffn'])
        P.dma('sp', cwa_t, cwa, writes=['cwa'])
        P.dma('sp', cwg_t, cwg, writes=['cwg'])
        ms('pool', onesb, 1.0, ['onesb'])

        def load_w2(dst, src, ncols, gidx, nm):
            hw = ncols // 2
            st = [ar("%s_st%d" % (nm, i), [128, hw]) for i in range(2)]
            for kc in range(8):
                for s in range(2):
                    P.dma('sp', st[s], src[kc * 128:(kc + 1) * 128, s * hw:(s + 1) * hw], writes=[('w2st', s)])
                    if gidx is None:
                        cp('act' if s == 0 else 'pool', dst[:, kc, s * hw:(s + 1) * hw], st[s], [('w2st', s)],
                           [(nm, kc, s)])
                    elif s == 0:
                        act(dst[:, kc, 0:hw], st[s], AF.Copy, [('w2st', s), 'gpre'], [(nm, kc, s)],
                            scale=gpre_t[:, gidx, kc:kc + 1])
                    else:
                        ts('pool', dst[:, kc, hw:], st[s], gpre_t[:, gidx, kc:kc + 1], None, ALU.mult, None,
                           [('w2st', s), 'gpre'], [(nm, kc, s)])
            apos[0] -= 2 * hw
        load_w2(w_outb, w_out, D, None, 'w_out')
        load_w2(w_mqb, w_mq, D, 1, 'w_mq')
        load_w2(w_mob, w_mo, D, None, 'w_mo')
        w_mkb = ar("w_mkb", [128, 8, D], BF16)
        w_mvb = ar("w_mvb", [128, 8, D], BF16)
        load_w2(w_mkb, w_mk, D, 2, 'w_mk')
        load_w2(w_mvb, w_mv, D, 2, 'w_mv')
        P.barrier()
        nb = norm_bufs()
        hTm = ar("hTm", [128, 8, 256], BF16)
        mst = [ar("mst%d" % i, [128, 512]) for i in range(2)]
        for mb in range(2):
            norm_T(nb, memp[mb * 128:(mb + 1) * 128, :], 128, hTm[:, :, mb * 128:(mb + 1) * 128], 'hTm')
        PA = [PS[0], PS[1]]
        PT2 = PS[2]
        PQ = PS[3]
        for c in range(8):
            for kc in range(8):
                mm(PQ[:, 0:256], w_mkb[:, kc, c * 128:(c + 1) * 128], hTm[:, kc, :], kc == 0, kc == 7,
                   ['hTm', 'w_mk'], ['PQ'])
            cp('act', mkT[:, c, :], PQ[:, 0:256], ['PQ'], ['mkT'])
        for (wv, dstb, dout_, nm) in ((w_mvb, mvb, p_mv, 'w_mv'), (w_mkb, None, p_mk, 'w_mk')):
            for mb in range(2):
                for half in range(2):
                    for kc in range(8):
                        mm(PA[half][:], hTm[:, kc, mb * 128:(mb + 1) * 128], wv[:, kc, half * 512:(half + 1) * 512],
                           kc == 0, kc == 7, ['hTm', nm], [('PA', half)])
                    if dstb is not None:
                        cp('act', dstb[:, mb, half * 512:(half + 1) * 512], PA[half][:], [('PA', half)], ['mvb'])
                    cp('act', mst[half], PA[half][:], [('PA', half)], [('mst', half)])
                    P.dma('sp', dout_[mb * 128:(mb + 1) * 128, half * 512:(half + 1) * 512], mst[half],
                          reads=[('mst', half)])
        P.barrier()
        apos[0] = keep2
        x2buf = ar("x2buf", [128, 3, D])
        xnT = ar("xnT", [128, 8, 384], BF16)
        xr = ar("xr", [128, D])
        tmpf = ar("tmpf", [128, D])
        xsb2 = ar("xsb2", [128, D], BF16)
        xsq2 = ar("xsq2", [128, D], BF16)
        x1nT = ar("x1nT", [128, 8, 128], BF16)
        qT = ar("qT", [128, 8, 128], BF16)
        PmT = ar("PmT", [128, 4, 2, 128], BF16)
        rden = ar("rden", [128, 4, 128])
        oT = ar("oT", [128, 8, 128], BF16)
        ss2 = ar("ss2", [128, 4])
        wupst = [ar("wupst%d" % i, [128, 8, 128]) for i in range(2)]
        wupb = [ar("wupb%d" % i, [128, 8, 2, 128], BF16) for i in range(2)]
        wdst = [ar("wdst0", [128, D])] * 2
        wdb = [ar("wdb%d" % i, [128, D], BF16) for i in range(2)]
        uxa = ar("uxa", [128, 386])
        uxg = ar("uxg", [128, 386])
        o_c1a = apos[0]
        c1a = ar("c1a", [128, 384])
        c1g = ar("c1g", [128, 384])
        sgt = ar("sgt", [128, 384])
        actT = [ar("actT%d" % i, [128, 384], BF16) for i in range(2)]
        cara = ar("cara", [128, NCF, 2])
        carg = ar("carg", [128, NCF, 2])
        yst = [ar("yst0", [128, D])] * 2
        o_pcst = apos[0]
        pcst = ar("pcst", [2, 512])
        ms('pool', cara.rearrange("p a b -> p (a b)"), 0.0, ['cara'])
        ms('pool', carg.rearrange("p a b -> p (a b)"), 0.0, ['carg'])
        PSC2 = [PS[4], PS[5]]
        PDN = PS[6]

        def sumsq_rstd(srcs, srcnames, n, width, col):
            for i, (sap, snm) in enumerate(zip(srcs, srcnames)):
                act(xsq2[0:n, 0:sap.shape[1]], sap, AF.Square, [snm], ['xsq2', ('ss2', i)], accum_out=ss2[0:n, i:i + 1])
            if len(srcs) == 2:
                tt('dve', ss2[0:n, 0:1], ss2[0:n, 0:1], ss2[0:n, 1:2], ALU.add, [('ss2', 0), ('ss2', 1)], [('ss2', 0)])
            act(ss2[0:n, 2:3], ss2[0:n, 0:1], AF.Ln, [('ss2', 0), 'epsc'], ['ss2r'], scale=1.0 / width, bias=epsc[0:n, :])
            act(ss2[0:n, 2:3], ss2[0:n, 2:3], AF.Exp, ['ss2r'], ['ss2r'], scale=-0.5)
            return ss2[0:n, 2:3]

        def post_norm_res(gi_, resid, resname, dst, dstname, n=128):
            r = sumsq_rstd([PA[0][0:n, :], PA[1][0:n, :]], [('PA', 0), ('PA', 1)], n, D, 0)
            for half in range(2):
                sl = slice(half * 512, (half + 1) * 512)
                stt('dve', tmpf[0:n, sl], PA[half][0:n, :], r, gpost_t[0:n, gi_, sl], ALU.mult, ALU.mult,
                    [('PA', half), 'ss2r', 'gpost'], [('tmpf', half)])
                tt('pool', dst[:, sl], tmpf[0:n, sl], resid[:, sl], ALU.add, [('tmpf', half), resname], [dstname])

        def pre_norm_T(src, srcname, gbc, gname, dst, dstname, n=128):
            r = sumsq_rstd([src], [srcname], n, D, 0)
            if gbc is None:
                ts('dve', xsb2[0:n, :], src, r, None, ALU.mult, None, [srcname, 'ss2r'], ['xsb2'])
            else:
                stt('dve', xsb2[0:n, :], src, r, gbc[0:n, :], ALU.mult, ALU.mult, [srcname, 'ss2r', gname], ['xsb2'])
            pT = PT2[:].bitcast(BF16)
            for kc in range(8):
                tr(pT[:, kc * 128:kc * 128 + n], xsb2[0:n, kc * 128:(kc + 1) * 128], identb[0:n, 0:n],
                   ['xsb2', 'identb'], ['PT2'])
            cp('act', dst, pT[:, :].rearrange("p (a b) -> p a b", a=8)[:, :, 0:n], ['PT2'], [dstname])

        def mem_attend(n, mk_of, mv_of, tok_groups):
            for (c0, c1, mkT_, mvb_, mnames) in tok_groups:
                w = c1 - c0
                for h in range(4):
                    for mb in range(2):
                        for j in range(2):
                            mm(PSC2[h // 2][:, ((h % 2) * 2 + mb) * 128 + c0:((h % 2) * 2 + mb) * 128 + c1],
                               mkT_[:, 2 * h + j, mb * 128:(mb + 1) * 128], qT[:, 2 * h + j, c0:c1], j == 0, j == 1,
                               ['qT'] + mnames, [('PSC2', h // 2)])
                for hp in range(2):
                    act(PmT[:, 2 * hp:2 * hp + 2, :, c0:c1],
                        PSC2[hp][:].rearrange("p (a b c) -> p a b c", a=2, b=2)[:, :, :, c0:c1],
                        AF.Exp, [('PSC2', hp)], ['PmT'], scale=1.0 / 16.0)
                for h in range(4):
                    for mb in range(2):
                        mm(PDN[:, h * 128 + c0:h * 128 + c1], onesb, PmT[:, h, mb, c0:c1], mb == 0, mb == 1,
                           ['PmT', 'onesb'], ['PDN'])
                P.op('dve', lambda e, c0=c0, c1=c1: e.reciprocal(
                    out=rden[:, :, c0:c1], in_=PDN[:].rearrange("p (a b) -> p a b", a=4)[:, :, c0:c1]), ['PDN'], ['rden'])
                for rnd in range(2):
                    for cc in range(4):
                        c = rnd * 4 + cc
                        h = c // 2
                        for mb in range(2):
                            mm(PQ[:, cc * 128 + c0:cc * 128 + c1], mvb_[:, mb, c * 128:(c + 1) * 128],
                               PmT[:, h, mb, c0:c1], mb == 0, mb == 1, ['PmT'] + mnames, ['PQ'])
                    tt('dve', oT[:, rnd * 4:rnd * 4 + 4, c0:c1].rearrange("p (h j) t -> p h j t", j=2),
                       PQ[:].rearrange("p (h j t) -> p h j t", h=2, j=2)[:, :, :, c0:c1],
                       rden[:, rnd * 2:rnd * 2 + 2, c0:c1].unsqueeze(2).to_broadcast([128, 2, 2, w]), ALU.mult,
                       ['PQ', 'rden'], ['oT'])

        def part_a(mixsrc, mixname, xsrc_dram, n, x2dst, x2name, xnTdst, xnTname, tok_groups):
            for half in range(2):
                for kc in range(8):
                    mm(PA[half][0:n, :], mixsrc[:, kc, :], w_outb[:, kc, half * 512:(half + 1) * 512], kc == 0, kc == 7,
                       [mixname], [('PA', half)])
            P.dma('sp', xr[0:n, :], xsrc_dram, writes=['xr'])
            post_norm_res(0, xr[0:n, :], 'xr', x2dst, x2name, n)
            pre_norm_T(x2dst, x2name, None, None, x1nT[:, :, 0:n], 'x1nT', n)
            for rnd in range(2):
                for cc in range(4):
                    c = rnd * 4 + cc
                    for kc in range(8):
                        mm(PQ[:, cc * 128:cc * 128 + n], w_mqb[:, kc, c * 128:(c + 1) * 128], x1nT[:, kc, 0:n],
                           kc == 0, kc == 7, ['x1nT'], ['PQ'])
                cp('act', qT[:, rnd * 4:rnd * 4 + 4, 0:n], PQ[:].rearrange("p (a b) -> p a b", a=4)[:, :, 0:n], ['PQ'], ['qT'])
            if callable(tok_groups):
                tok_groups()
            else:
                mem_attend(n, None, None, tok_groups)
            for half in range(2):
                for c in range(8):
                    mm(PA[half][0:n, :], oT[:, c, 0:n], w_mob[:, c, half * 512:(half + 1) * 512], c == 0, c == 7,
                       ['oT'], [('PA', half)])
            post_norm_res(1, x2dst, x2name, x2dst, x2name, n)
            pre_norm_T(x2dst, x2name, gffn_t, 'gffn', xnTdst, xnTname, n)

        def part_b(blocks, n_per, out_fn, first_tile, last_tile, conv_init=None):
            nblk = len(blocks)
            ntok = nblk * n_per
            PU = [PS[0], PS[1]]
            PD = [[PS[2 + 2 * b_ + hf] for hf in range(2)] for b_ in range(nblk)]
            for cf in range(NCF):
                sz = 128 if cf < 21 else 64
                s = cf % 2
                for j, off in enumerate((0, DFF)):
                    P.dma('sp', wupst[j][:, :, 0:sz],
                          w_up[:, off + cf * 128:off + cf * 128 + sz].rearrange("(kc p) n -> p kc n", p=128),
                          writes=[('wupst', j)])
                    cp('pool' if j == 0 else 'dve', wupb[s][:, :, j, 0:sz], wupst[j][:, :, 0:sz], [('wupst', j)],
                       [('wupb', s, j)])
                P.dma('sp', wdst[s][0:sz, :], w_down[cf * 128:cf * 128 + sz, :], writes=['wdst'])
                cp('pool', wdb[s][0:sz, :], wdst[s][0:sz, :], ['wdst'], [('wdb', s)])
                for j, (ux, car, cw_, c1, nm) in enumerate(((uxa, cara, cwa_t, c1a, 'a'), (uxg, carg, cwg_t, c1g, 'g'))):
                    for kc in range(8):
                        mm(PU[j][0:sz, 0:ntok], wupb[s][:, kc, j, 0:sz], xnT[:, kc, 0:ntok], kc == 0, kc == 7,
                           [('wupb', s, j), 'xnT'], [('PU', j)])
                    if conv_init is None:
                        cp('pool', ux[0:sz, 0:2], car[0:sz, cf, :], ['car' + nm], ['ux' + nm])
                        cp('act', ux[0:sz, 2:2 + ntok], PU[j][0:sz, 0:ntok], [('PU', j)], ['ux' + nm])
                        if first_tile:
                            ts('pool', ux[0:sz, 2 + 126:2 + 128], ux[0:sz, 2 + 126:2 + 128], hflag[0:sz, :], None, ALU.mult,
                               None, ['ux' + nm, 'hflag'], ['ux' + nm])
                        cp('pool', car[0:sz, cf, :], ux[0:sz, ntok:ntok + 2], ['ux' + nm], ['car' + nm])
                        eng = 'dve' if j == 0 else 'pool'
                        ts(eng, c1[0:sz, 0:ntok], ux[0:sz, 2:2 + ntok], cw_[0:sz, cf, 2:3], cw_[0:sz, cf, 3:4], ALU.mult,
                           ALU.add, ['ux' + nm, 'cw' + nm], ['c1' + nm])
                        stt('dve', c1[0:sz, 0:ntok], ux[0:sz, 1:1 + ntok], cw_[0:sz, cf, 1:2], c1[0:sz, 0:ntok], ALU.mult,
                            ALU.add, ['ux' + nm, 'cw' + nm, 'c1' + nm], ['c1' + nm])
                        stt('dve', c1[0:sz, 0:ntok], ux[0:sz, 0:ntok], cw_[0:sz, cf, 0:1], c1[0:sz, 0:ntok], ALU.mult,
                            ALU.add, ['ux' + nm, 'cw' + nm, 'c1' + nm], ['c1' + nm])
                    else:
                        conv_init(j, cf, sz, PU[j], ux, c1, cw_, nm)
                act(sgt[0:sz, 0:ntok], c1g[0:sz, 0:ntok], AF.Silu, ['c1g'], ['sgt'])
                tt('dve', actT[s][0:sz, 0:ntok], c1a[0:sz, 0:ntok], sgt[0:sz, 0:ntok], ALU.mult, ['c1a', 'sgt'],
                   [('actT', s)])
                for b_ in range(nblk):
                    for hf in range(2):
                        mm(PD[b_][hf][0:n_per, :], actT[s][0:sz, b_ * n_per:(b_ + 1) * n_per],
                           wdb[s][0:sz, hf * 512:(hf + 1) * 512], cf == 0, cf == NCF - 1,
                           [('actT', s), ('wdb', s)], [('PD', b_, hf)])
            for b_, slot in enumerate(blocks):
                if slot is None:
                    continue
                r = sumsq_rstd([PD[b_][0][0:n_per, :], PD[b_][1][0:n_per, :]], [('PD', b_, 0), ('PD', b_, 1)], n_per, D, 0)
                ys = yst[b_ % 2]
                for hf in range(2):
                    sl = slice(hf * 512, (hf + 1) * 512)
                    stt('dve', tmpf[0:n_per, sl], PD[b_][hf][0:n_per, :], r, gpost_t[0:n_per, 2, sl], ALU.mult, ALU.mult,
                        [('PD', b_, hf), 'ss2r', 'gpost'], [('tmpf', hf)])
                    tt('pool', ys[0:n_per, sl], tmpf[0:n_per, sl], x2buf[0:n_per, slot, sl], ALU.add,
                       [('tmpf', hf), ('x2', slot)], ['yst'])
                out_fn(b_, slot, ys)

        own_tiles = [[0, 1, 2], [3, 4, 5], [6, 7, 8], [9, 10, 11], [12, 13, 14], [15, 16]]
        PGRP = [(0, 128, mkT, mvb, ['mkT', 'mvb'])]
        for ti, tile_ in enumerate(own_tiles):
            for j, wb in enumerate(tile_):
                part_a(mixT[:, :, wb * 128:(wb + 1) * 128], ('mixT', wb), xall[(W0 + wb) * 128:(W0 + wb + 1) * 128, :], 128,
                       x2buf[:, j, :], ('x2', j), xnT[:, :, j * 128:(j + 1) * 128], 'xnT', PGRP)
            P.barrier()

            def out_p(b_, slot, ys, tile_=tile_):
                ob = tile_[b_] - 1
                P.dma('sp', y_p[ob * 128:(ob + 1) * 128, :], ys, reads=['yst'])
            blocks = [None if wb == 0 else j for j, wb in enumerate(tile_)]
            part_b(blocks, 128, out_p, ti == 0, ti == len(own_tiles) - 1)
            P.barrier()
        for j, (car, off) in enumerate(((cara, 0), (carg, DFF))):
            for g0 in range(0, NCF, 4):
                ncf = min(4, NCF - g0)
                wid = 0
                for cf in range(g0, g0 + ncf):
                    sz = 128 if cf < 21 else 64
                    mm(PS[0][0:2, wid:wid + sz], car[0:sz, cf, :], identf[0:sz, 0:sz], True, True, ['cara', 'carg', 'identf'],
                       ['PCV'])
                    wid += sz
                cp('dve', pcst[:, 0:wid], PS[0][0:2, 0:wid], ['PCV'], ['pcst'])
                P.dma('sp', p_cv[:, off + g0 * 128:off + g0 * 128 + wid], pcst[:, 0:wid], reads=['pcst'])
        P.barrier()

        if 'sp2' not in SKIP:
            _mk = x2buf[:, 1, :].bitcast(BF16).rearrange("p (a b) -> p a b", a=8)
            mkTs = [_mk, _mk]
            _mv = arena[:, o_c1a:o_c1a + 1024].bitcast(BF16).rearrange("p (a b) -> p a b", a=2)
            mvbs = [_mv, _mv]
            cms = [yst[0], wdst[0]]
            cmb = x2buf[:, 2, :].bitcast(BF16)[:, 0:D]
            scst = ar("scst", [32, 128])
            scT = ar("scT", [128, 32])
            scout = arena[0:32, o_pcst:o_pcst + 512]
            lctr = [0]

            def sample_mem():
                for b_ in range(16):
                    s2 = 0
                    for mb in range(2):
                        i = lctr[0] % 2
                        lctr[0] += 1
                        P.dma('sp', cms[i], cmk[b_, mb * 128:(mb + 1) * 128, :], writes=[('cms', i)])
                        cp('pool', cmb, cms[i], [('cms', i)], ['cmb'])
                        pT = PT2[:].bitcast(BF16)
                        for c in range(8):
                            tr(pT[:, c * 128:(c + 1) * 128], cmb[:, c * 128:(c + 1) * 128], identb[:], ['cmb', 'identb'], ['PT2'])
                        cp('act', mkTs[s2][:, :, mb * 128:(mb + 1) * 128], pT[:, :].rearrange("p (a b) -> p a b", a=8),
                           ['PT2'], [('mkTs', s2)])
                        i = lctr[0] % 2
                        lctr[0] += 1
                        P.dma('sp', cms[i], cmv[b_, mb * 128:(mb + 1) * 128, :], writes=[('cms', i)])
                        cp('pool', mvbs[s2][:, mb, :], cms[i], [('cms', i)], [('mvbs', s2)])
                    mem_attend(64, None, None, [(b_ * 4, b_ * 4 + 4, mkTs[s2], mvbs[s2], [('mkTs', s2), ('mvbs', s2)])])
            part_a(mixTs[:, :, :], 'mixTs', xsm, 64, x2buf[0:64, 0, :], ('x2', 0), xnT[:, :, 0:64], 'xnT', sample_mem)
            P.barrier()
            stcv2 = stcv.rearrange("b r f -> (b r) f")
            scv2 = s_cv.rearrange("b r f -> (b r) f")
            wacc = [0]

            def sample_conv(j, cf, sz, PUj, ux, c1, cw_, nm):
                off = j * DFF + cf * 128
                P.dma('sp', scst[:, 0:sz], stcv2[:, off:off + sz], writes=['scst'])
                mm(PS[7][0:sz, 0:32], scst[:, 0:sz], identf[0:32, 0:32], True, True, ['scst', 'identf'], ['PS7'])
                ux6 = ux[0:sz, 0:96].rearrange("p (b t) -> p b t", t=6)
                cp('act', ux6[:, :, 0:2], PS[7][0:sz, 0:32].rearrange("p (b t) -> p b t", t=2), ['PS7'], ['ux' + nm])
                cp('act', ux6[:, :, 2:6], PUj[0:sz, 0:64].rearrange("p (b t) -> p b t", t=4), [('PU', j)], ['ux' + nm])
                c1v = c1[0:sz, 0:64].rearrange("p (b t) -> p b t", t=4)
                ts('dve' if j == 0 else 'pool', c1v, ux6[:, :, 2:6], cw_[0:sz, cf, 2:3], cw_[0:sz, cf, 3:4], ALU.mult, ALU.add,
                   ['ux' + nm, 'cw' + nm], ['c1' + nm])
                stt('dve', c1v, ux6[:, :, 1:5], cw_[0:sz, cf, 1:2], c1v, ALU.mult, ALU.add, ['ux' + nm, 'cw' + nm, 'c1' + nm],
                    ['c1' + nm])
                stt('dve', c1v, ux6[:, :, 0:4], cw_[0:sz, cf, 0:1], c1v, ALU.mult, ALU.add, ['ux' + nm, 'cw' + nm, 'c1' + nm],
                    ['c1' + nm])
                cp('pool', scT[0:sz, :].rearrange("p (b t) -> p b t", t=2), ux6[:, :, 4:6], ['ux' + nm], ['scT'])
                g0 = (cf // 4) * 4
                col = (cf - g0) * 128
                psj = PS[6] if j == 0 else PS[5]
                mm(psj[0:32, col:col + sz], scT[0:sz, :], identf[0:sz, 0:sz], True, True, ['scT', 'identf'], [('PS6', j)])
                if cf % 4 == 3 or cf == NCF - 1:
                    wid = col + sz
                    cp('act', scout[:, 0:wid], psj[0:32, 0:wid], [('PS6', j)], ['scout'])
                    P.dma('sp', scv2[:, j * DFF + g0 * 128:j * DFF + g0 * 128 + wid], scout[:, 0:wid], reads=['scout'])

            def out_s(b_, slot, ys):
                P.dma('sp', y_s, ys[0:64, :], reads=['yst'])
            part_b([0], 64, out_s, False, False, conv_init=sample_conv)
            P.barrier()
        P.emit()
    return nc


def _consts(half):
    c = {}
    c["ident"] = np.eye(128, dtype=np.float32)
    k = np.arange(128)
    c["tril"] = (k[:, None] <= k[None, :]).astype(np.float32)
    halfd = 8
    inv = np.exp(-math.log(500000.0) * (2.0 * np.arange(halfd, dtype=np.float32) / 16)).astype(np.float32)
    pos = np.arange(4096, dtype=np.float32) - (2048.0 if half == 0 else 0.0)
    ang = (pos[:, None] * inv[None, :]).astype(np.float32)
    c["ropeC"] = np.ascontiguousarray(np.tile(np.cos(ang).astype(np.float32), (1, 8)))
    c["ropeS"] = np.ascontiguousarray(np.tile(np.sin(ang).astype(np.float32), (1, 8)))
    poss = (2048.0 + np.arange(4, dtype=np.float32))
    angs = (poss[:, None] * inv[None, :]).astype(np.float32)
    c["ropeCs"] = np.ascontiguousarray(np.tile(np.cos(angs).astype(np.float32), (16, 8)))
    c["ropeSs"] = np.ascontiguousarray(np.tile(np.sin(angs).astype(np.float32), (16, 8)))
    kb = np.zeros((128, NB), np.float32)
    gv = np.ones((4, NB, 128), np.float32)
    if half == 0:
        kb[:, :16] = -BIG
        gv[:, :16, :] = 0.0
    gv = gv.reshape(128, 128)
    hk = np.arange(128)
    c["lt"] = ((hk[:, None] // NB == hk[None, :] // NB) & (hk[:, None] < hk[None, :])).astype(np.float32)
    c["kbias"] = kb
    c["gvalid"] = gv
    c["hflag"] = np.full((128, 1), float(half), np.float32)
    c["iota"] = np.arange(128, dtype=np.float32).reshape(128, 1)
    t = np.arange(64)
    c["bm"] = ((t[:, None] // 4 == t[None, :] // 4) & (t[:, None] <= t[None, :])).astype(np.float32)
    c["bmask"] = (t[:, None] // 4 == np.arange(16)[None, :]).astype(np.float32)
    c["bmT"] = np.ascontiguousarray(c["bmask"].T)
    c["bmn"] = np.where(t[:, None] // 4 == t[None, :] // 4, 0.0, -BIG).astype(np.float32)
    sel = np.zeros((4, 4, 128), np.float32)
    for h in range(4):
        sel[h, h, :] = 1.0
    c["sel"] = sel
    return c


def make_in_maps(inp, pool_k=None, pool_v=None, ptab=None):
    f = lambda a: np.ascontiguousarray(np.asarray(a, dtype=np.float32))
    xp = f(inp["x_prompt"])
    xs = f(inp["x_sample"])
    ck = f(inp["cache_dk"])[0].reshape(-1, 512) if pool_k is None else pool_k
    cv = f(inp["cache_dv"])[0].reshape(-1, 512) if pool_v is None else pool_v
    pt = np.asarray(inp["page_table"], dtype=np.int32) if ptab is None else ptab
    gp = np.stack([f(inp[n])[0].reshape(8, 128).T for n in ("g_mix_pre", "g_mem_pre", "g_mem_src", "g_ffn_pre")], 1)
    gpost = np.stack([f(inp[n])[0] for n in ("g_mix_post", "g_mem_post", "g_ffn_post")], 0)
    ghead = np.stack([f(inp["g_ml_head"])[0].reshape(512), f(inp["g_da_head"])[0].reshape(512)], 0)
    b = f(inp["b_if"])[0]
    bif = np.stack([b[:4], b[4:]], 1)
    wdw = f(inp["w_dw"])[0]
    bdw = f(inp["b_dw"])[0]

    def cw(off):
        o = np.zeros((128, NCF, 4), np.float32)
        for cf in range(NCF):
            sz = 128 if cf < 21 else 64
            o[:sz, cf, 0:3] = wdw[:, off + cf * 128: off + cf * 128 + sz].T
            o[:sz, cf, 3] = bdw[off + cf * 128: off + cf * 128 + sz]
        return o
    shared = {
        "w_in": f(inp["w_in"])[0], "w_out": f(inp["w_out"])[0], "w_mq": f(inp["w_mq"])[0],
        "w_mk": f(inp["w_mk"])[0], "w_mv": f(inp["w_mv"])[0], "w_mo": f(inp["w_mo"])[0],
        "w_up": f(inp["w_up"])[0], "w_down": f(inp["w_down"])[0],
        "gpre": np.ascontiguousarray(gp), "gpost": np.ascontiguousarray(gpost), "ghead": np.ascontiguousarray(ghead),
        "gffn": np.ascontiguousarray(f(inp["g_ffn_pre"])[0].reshape(1, D)),
        "bif": np.ascontiguousarray(bif), "bifr": np.ascontiguousarray(b.reshape(1, 8)), "dlam": f(inp["da_lambda"])[0].reshape(1, 256),
        "cwa": cw(0), "cwg": cw(DFF), "ckp": ck, "cvp": cv,
    }
    cst = [_consts(0), _consts(1)]
    maps = []
    for c in range(8):
        bb, half = c // 2, c % 2
        m = dict(shared)
        m.update(cst[half])
        if half == 1:
            m["xall"] = np.ascontiguousarray(xp[bb])
        else:
            xa = np.zeros((4096, D), np.float32)
            xa[2048:] = xp[bb, :2048]
            m["xall"] = xa
        sl = slice(c * 16, (c + 1) * 16)
        m["xsm"] = np.ascontiguousarray(xs[sl].reshape(64, D))
        m["memp"] = np.ascontiguousarray(f(inp["mem_prompt"])[bb])
        m["ptab"] = np.ascontiguousarray(pt[sl].reshape(1, 256))
        m["cmk"] = np.ascontiguousarray(f(inp["cache_mem_k"])[0, sl].reshape(16, 256, D))
        m["cmv"] = np.ascontiguousarray(f(inp["cache_mem_v"])[0, sl].reshape(16, 256, D))
        m["stC"] = np.ascontiguousarray(f(inp["state_ml_C"])[0, sl])
        m["stn"] = np.ascontiguousarray(f(inp["state_ml_n"])[0, sl])
        m["stm"] = np.ascontiguousarray(f(inp["state_ml_m"])[0, sl])
        m["stcv"] = np.ascontiguousarray(f(inp["state_conv"])[0, sl])
        maps.append(m)
    return maps


def assemble(res):
    r = res
    yp = np.zeros((4, 4096, D), np.float32)
    pdk = np.zeros((1, 4, 4096, 4, 128), np.float32)
    pdv = np.zeros((1, 4, 4096, 4, 128), np.float32)
    pmk = np.zeros((1, 4, 256, 4, 256), np.float32)
    pmv = np.zeros((1, 4, 256, 4, 256), np.float32)
    pC = np.zeros((1, 4, 4, 128, 128), np.float32)
    pn = np.zeros((1, 4, 4, 128), np.float32)
    pm = np.zeros((1, 4, 4), np.float32)
    pcv = np.zeros((1, 4, 2, 2 * DFF), np.float32)
    ys = np.zeros((128, 4, D), np.float32)
    sdk = np.zeros((1, 128, 4, 4, 128), np.float32)
    sdv = np.zeros((1, 128, 4, 4, 128), np.float32)
    sC = np.zeros((1, 128, 4, 128, 128), np.float32)
    sn = np.zeros((1, 128, 4, 128), np.float32)
    sm = np.zeros((1, 128, 4), np.float32)
    scv = np.zeros((1, 128, 2, 2 * DFF), np.float32)
    for c in range(8):
        bb, half = c // 2, c % 2
        o = r[c]
        ts_ = slice(half * 2048, (half + 1) * 2048)
        yp[bb, ts_] = o["y_p"]
        pdk[0, bb, ts_] = o["p_dk"].reshape(2048, 4, 128)
        pdv[0, bb, ts_] = o["p_dv"].reshape(2048, 4, 128)
        if half == 1:
            pmk[0, bb] = o["p_mk"].reshape(256, 4, 256)
            pmv[0, bb] = o["p_mv"].reshape(256, 4, 256)
            pC[0, bb] = o["p_C"]
            pn[0, bb] = o["p_n"]
            pm[0, bb] = o["p_m"].reshape(4)
            pcv[0, bb] = o["p_cv"]
        sl = slice(c * 16, (c + 1) * 16)
        ys[sl] = o["y_s"].reshape(16, 4, D)
        sdk[0, sl] = o["s_dk"].reshape(16, 4, 4, 128)
        sdv[0, sl] = o["s_dv"].reshape(16, 4, 4, 128)
        sC[0, sl] = o["s_C"]
        sn[0, sl] = o["s_n"]
        sm[0, sl] = o["s_m"]
        scv[0, sl] = o["s_cv"]
    return (yp, ys, pdk, pdv, pmk, pmv, pC, pn, pm, pcv, sdk, sdv, sC, sn, sm, scv)


STAGE = 99


def kernel(**inputs):
    ck = np.asarray(inputs["cache_dk"], dtype=np.float32)
    cv = np.asarray(inputs["cache_dv"], dtype=np.float32)
    n_pool = int(ck.shape[1])
    pk = np.ascontiguousarray(ck[0].reshape(-1, 512))
    pv = np.ascontiguousarray(cv[0].reshape(-1, 512))
    nc = build(n_pool, stage=STAGE)
    maps = make_in_maps(inputs, pool_k=pk, pool_v=pv)
    res = run_bass_kernel_spmd(nc, maps, core_ids=list(range(8)))
    return assemble(res.results)
```
